# Optimizing a Trainium2 kernel written in Bass

```python
import jax
import jax.numpy as jnp
from jax import lax
import numpy as np

D_MODEL = 1024
BATCH = 8
SEQ = 2048
DEPTH = 4
DEC_BATCH = 32
DEC_SEQ = 4
PAST_LEN = 8192
PAGE_SIZE = 128

N_MIXERS = 3
N_LAYERS_A = (DEPTH + 2) // 3
N_LAYERS_B = (DEPTH + 1) // 3
N_LAYERS_C = DEPTH // 3

D_CONV = D_MODEL
CONV_A_WIDTH = 3

N_HEADS = 16
HEAD_DIM = D_MODEL // N_HEADS
N_KV = 2
HPG = N_HEADS // N_KV
D_ATT = N_HEADS * HEAD_DIM
CMP_BLOCK = 32
SLC_BLOCK = 64
CMP_PER_SLC = SLC_BLOCK // CMP_BLOCK
TOP_N = 16
N_LOCAL = 2
WINDOW = 512
Q_BLOCK = 32
NEG_INF = -1e30
FORCE = 1e9

D_RNN = D_MODEL
N_RG_BLOCKS = 4
RG_BLOCK = D_RNN // N_RG_BLOCKS
CONV_C_WIDTH = 4
RG_C = 8.0

ALPHA = (2 * DEPTH) ** 0.25
BETA = (8 * DEPTH) ** -0.25
LN_EPS = 1e-5

kernel_name = 'hybrid_conv_nsa_rglru_deepnorm_step'


def layer_norm(x, g, b):
    xf = x.astype(jnp.float32)
    mu = jnp.mean(xf, axis=-1, keepdims=True)
    var = jnp.mean(jnp.square(xf - mu), axis=-1, keepdims=True)
    return ((xf - mu) * lax.rsqrt(var + LN_EPS) * g.astype(jnp.float32) + b.astype(jnp.float32)).astype(x.dtype)


def causal_dwconv(u, buf, w):
    width = w.shape[0]
    t_len = u.shape[1]
    ext = jnp.concatenate([buf.astype(u.dtype), u], axis=1)
    out = ext[:, 0:t_len] * w[0]
    for k in range(1, width):
        out = out + ext[:, k:k + t_len] * w[k]
    return out, ext[:, ext.shape[1] - (width - 1):]


def masked_softmax(s, valid):
    s = jnp.where(valid, s, NEG_INF)
    return jax.nn.softmax(s, axis=-1) * valid


def alibi_slopes():
    h = jnp.arange(1, N_HEADS + 1, dtype=jnp.float32)
    return (2.0 ** (-8.0 * h / N_HEADS)).reshape(N_KV, HPG)


def short_conv_mixer(x, buf, w_in, conv_w, w_out):
    h, b_gate, c_gate, z = jnp.split(x @ w_in, 4, axis=-1)
    conv, new_buf = causal_dwconv(c_gate * h, buf, conv_w)
    return (jax.nn.silu(z) * b_gate * conv) @ w_out, new_buf


def rglru_mixer(x, h0, buf, w_in, conv_w, conv_b, w_a, b_a, w_x, b_x, lam, w_out):
    bsz, t_len, _ = x.shape
    u, z = jnp.split(x @ w_in, 2, axis=-1)
    uc, new_buf = causal_dwconv(u, buf, conv_w)
    uc = uc + conv_b
    ub = uc.reshape(bsz, t_len, N_RG_BLOCKS, RG_BLOCK)
    r = jax.nn.sigmoid(jnp.einsum('btni,nij->btnj', ub, w_a).reshape(bsz, t_len, D_RNN) + b_a)
    i = jax.nn.sigmoid(jnp.einsum('btni,nij->btnj', ub, w_x).reshape(bsz, t_len, D_RNN) + b_x)
    log_a = (-RG_C * jax.nn.softplus(-lam.astype(jnp.float32))) * r.astype(jnp.float32)
    a = jnp.exp(log_a)
    b = jnp.sqrt(-jnp.expm1(2.0 * log_a)) * (i * uc).astype(jnp.float32)
    b = b.at[:, 0].add(a[:, 0] * h0.astype(jnp.float32))

    def combine(left, right):
        return left[0] * right[0], right[0] * left[1] + right[1]

    _, h = lax.associative_scan(combine, (a, b), axis=1)
    y = (jax.nn.silu(z) * h.astype(x.dtype)) @ w_out
    return y, h[:, -1].astype(x.dtype), new_buf


def nsa_mixer(x, past_rows, win_buf, w_in, w_cmp, w_out):
    bsz, t_len, _ = x.shape
    p_len = past_rows.shape[1]
    l_len = p_len + t_len
    n_kvcols = 6 * N_KV * HEAD_DIM
    proj = x @ w_in
    q = proj[..., :D_ATT].reshape(bsz, t_len, N_KV, HPG, HEAD_DIM) * (HEAD_DIM ** -0.5)
    kv = proj[..., D_ATT:D_ATT + n_kvcols].reshape(bsz, t_len, 6, N_KV, HEAD_DIM)
    gates = jax.nn.sigmoid(proj[..., D_ATT + n_kvcols:D_ATT + n_kvcols + 3 * N_HEADS]).reshape(bsz, t_len, 3, N_KV, HPG)
    z = proj[..., D_ATT + n_kvcols + 3 * N_HEADS:]
    rows = kv[:, :, :4]
    win_new = kv[:, :, 4:]
    dt = rows.dtype
    full = jnp.concatenate([past_rows.astype(dt), rows], axis=1)

    n_cmp = l_len // CMP_BLOCK
    blocks = full[:, :n_cmp * CMP_BLOCK, :2].reshape(bsz, n_cmp, CMP_BLOCK, 2, N_KV, HEAD_DIM)
    kv_c = jnp.einsum('bnjcgd,cjgd->bncgd', blocks, w_cmp)
    k_c, v_c = kv_c[:, :, 0], kv_c[:, :, 1]
    cmp_end = jnp.arange(n_cmp) * CMP_BLOCK + (CMP_BLOCK - 1)

    n_slc = -(-l_len // SLC_BLOCK)
    slc = jnp.pad(full[:, :, 2:], ((0, 0), (0, n_slc * SLC_BLOCK - l_len), (0, 0), (0, 0), (0, 0)))
    slc = slc.reshape(bsz, n_slc, SLC_BLOCK, 2, N_KV, HEAD_DIM)
    k_sl = slc[:, :, :, 0].transpose(0, 3, 1, 2, 4)
    v_sl = slc[:, :, :, 1].transpose(0, 3, 1, 2, 4)
    n_sel = min(TOP_N, n_slc)

    n_pad = WINDOW - win_buf.shape[1]
    win = jnp.concatenate([jnp.zeros((bsz, n_pad, 2, N_KV, HEAD_DIM), dt), win_buf.astype(dt), win_new], axis=1)

    qb_len = Q_BLOCK if t_len % Q_BLOCK == 0 else t_len
    n_qb = t_len // qb_len
    slopes = alibi_slopes()
    blk = jnp.arange(n_slc)
    b_ix = jnp.arange(bsz)[:, None, None, None]
    g_ix = jnp.arange(N_KV)[None, :, None, None]

    def attend_block(args):
        qb, gb, t0 = args
        t = p_len + t0 + jnp.arange(qb_len)
        tf = t.astype(jnp.float32)
        s_c = jnp.einsum('bqghd,bngd->bghqn', qb, k_c, preferred_element_type=jnp.float32)
        s_c = s_c - slopes[:, :, None, None] * (tf[:, None] - cmp_end[None, :].astype(jnp.float32))
        p_c = masked_softmax(s_c, cmp_end[None, :] <= t[:, None])
        o_c = jnp.einsum('bghqn,bngd->bqghd', p_c.astype(dt), v_c)
        imp = jnp.pad(p_c.sum(axis=2), ((0, 0), (0, 0), (0, 0), (0, n_slc * CMP_PER_SLC - n_cmp)))
        imp = imp.reshape(bsz, N_KV, qb_len, n_slc, CMP_PER_SLC).sum(-1)
        cur = (t // SLC_BLOCK)[:, None]
        forced = (blk == 0) | ((blk <= cur) & (blk > cur - N_LOCAL))
        score = jnp.where(forced, FORCE, jnp.where(blk <= cur, imp, NEG_INF))
        _, idx = lax.top_k(score, n_sel)
        k_s = k_sl[b_ix, g_ix, idx].reshape(bsz, N_KV, qb_len, n_sel * SLC_BLOCK, HEAD_DIM)
        v_s = v_sl[b_ix, g_ix, idx].reshape(bsz, N_KV, qb_len, n_sel * SLC_BLOCK, HEAD_DIM)
        pos_s = (idx[..., None] * SLC_BLOCK + jnp.arange(SLC_BLOCK)).reshape(bsz, N_KV, qb_len, n_sel * SLC_BLOCK)
        s_s = jnp.einsum('bqghd,bgqsd->bghqs', qb, k_s, preferred_element_type=jnp.float32)
        s_s = s_s - slopes[None, :, :, None, None] * (tf[:, None] - pos_s.astype(jnp.float32))[:, :, None]
        p_s = masked_softmax(s_s, (pos_s <= t[:, None])[:, :, None])
        o_s = jnp.einsum('bghqs,bgqsd->bqghd', p_s.astype(dt), v_s)
        kw = lax.dynamic_slice_in_dim(win, t0, WINDOW + qb_len, axis=1)
        pos_w = p_len - WINDOW + t0 + jnp.arange(WINDOW + qb_len)
        dist = t[:, None] - pos_w[None, :]
        valid_w = (pos_w[None, :] >= 0) & (dist >= 0) & (dist < WINDOW)
        s_w = jnp.einsum('bqghd,bsgd->bghqs', qb, kw[:, :, 0], preferred_element_type=jnp.float32)
        s_w = s_w - slopes[:, :, None, None] * dist.astype(jnp.float32)
        p_w = masked_softmax(s_w, valid_w)
        o_w = jnp.einsum('bghqs,bsgd->bqghd', p_w.astype(dt), kw[:, :, 1])
        g = gb[..., None].astype(dt)
        return g[:, :, 0] * o_c + g[:, :, 1] * o_s + g[:, :, 2] * o_w

    q_blocks = q.reshape(bsz, n_qb, qb_len, N_KV, HPG, HEAD_DIM).transpose(1, 0, 2, 3, 4, 5)
    g_blocks = gates.reshape(bsz, n_qb, qb_len, 3, N_KV, HPG).transpose(1, 0, 2, 3, 4, 5)
    starts = jnp.arange(n_qb, dtype=jnp.int32) * qb_len
    o = lax.map(attend_block, (q_blocks, g_blocks, starts))
    o = o.transpose(1, 0, 2, 3, 4, 5).reshape(bsz, t_len, D_ATT)
    y = (o * jax.nn.silu(z)) @ w_out

    all_win = jnp.concatenate([win_buf.astype(dt), win_new], axis=1)
    keep = win_buf.shape[1] if p_len > 0 else min(WINDOW, t_len)
    return y, rows, all_win[:, all_win.shape[1] - keep:]


def setup_inputs(seed: int = 0) -> dict:
    key = jax.random.key(seed)
    ks = jax.random.split(key, 32)
    f32 = jnp.float32

    def nrm(k, shape, scale):
        return jax.random.normal(k, shape, f32) * scale

    n_pages = PAST_LEN // PAGE_SIZE
    n_used = DEC_BATCH * n_pages
    n_pool = n_used + max(1, n_used // 4)
    win_rows = min(WINDOW, PAST_LEN)
    page_table = jax.random.permutation(ks[7], n_pool)[:n_used].reshape(DEC_BATCH, n_pages).astype(jnp.int32)
    a0 = jax.random.uniform(ks[23], (N_LAYERS_C, D_RNN), f32, 0.9, 0.999)
    s0 = a0 ** (1.0 / RG_C)
    d_bin = D_ATT + 6 * N_KV * HEAD_DIM + 3 * N_HEADS + D_ATT
    return {
        'x_prompt': nrm(ks[0], (BATCH, SEQ, D_MODEL), 1.0),
        'x_sample': nrm(ks[1], (DEC_BATCH, DEC_SEQ, D_MODEL), 1.0),
        'cache_nsa_kv': nrm(ks[2], (N_LAYERS_B, n_pool, PAGE_SIZE, 4, N_KV, HEAD_DIM), 1.0),
        'cache_nsa_win': nrm(ks[3], (N_LAYERS_B, DEC_BATCH, win_rows, 2, N_KV, HEAD_DIM), 1.0),
        'state_conv_a': nrm(ks[4], (N_LAYERS_A, DEC_BATCH, CONV_A_WIDTH - 1, D_CONV), 1.0),
        'state_lru_h': nrm(ks[5], (N_LAYERS_C, DEC_BATCH, D_RNN), 0.5),
        'state_lru_conv': nrm(ks[6], (N_LAYERS_C, DEC_BATCH, CONV_C_WIDTH - 1, D_RNN), 1.0),
        'page_table': page_table,
        'ln_g': 1.0 + nrm(ks[8], (DEPTH, D_MODEL), 0.02),
        'ln_b': nrm(ks[9], (DEPTH, D_MODEL), 0.02),
        'a_w_in': nrm(ks[10], (N_LAYERS_A, D_MODEL, 4 * D_CONV), D_MODEL ** -0.5),
        'a_conv_w': nrm(ks[11], (N_LAYERS_A, CONV_A_WIDTH, D_CONV), CONV_A_WIDTH ** -0.5),
        'a_w_out': nrm(ks[12], (N_LAYERS_A, D_CONV, D_MODEL), BETA * D_CONV ** -0.5),
        'b_w_in': nrm(ks[13], (N_LAYERS_B, D_MODEL, d_bin), D_MODEL ** -0.5),
        'b_w_cmp': (1.0 + nrm(ks[14], (N_LAYERS_B, 2, CMP_BLOCK, N_KV, HEAD_DIM), 0.1)) * CMP_BLOCK ** -0.5,
        'b_w_out': nrm(ks[15], (N_LAYERS_B, D_ATT, D_MODEL), BETA * D_ATT ** -0.5),
        'c_w_in': nrm(ks[16], (N_LAYERS_C, D_MODEL, 2 * D_RNN), D_MODEL ** -0.5),
        'c_conv_w': nrm(ks[17], (N_LAYERS_C, CONV_C_WIDTH, D_RNN), CONV_C_WIDTH ** -0.5),
        'c_conv_b': nrm(ks[18], (N_LAYERS_C, D_RNN), 0.02),
        'c_w_a': nrm(ks[19], (N_LAYERS_C, N_RG_BLOCKS, RG_BLOCK, RG_BLOCK), RG_BLOCK ** -0.5),
        'c_b_a': nrm(ks[20], (N_LAYERS_C, D_RNN), 0.02),
        'c_w_x': nrm(ks[21], (N_LAYERS_C, N_RG_BLOCKS, RG_BLOCK, RG_BLOCK), RG_BLOCK ** -0.5),
        'c_b_x': nrm(ks[22], (N_LAYERS_C, D_RNN), 0.02),
        'c_lam': jnp.log(s0) - jnp.log1p(-s0),
        'c_w_out': nrm(ks[24], (N_LAYERS_C, D_RNN, D_MODEL), BETA * D_RNN ** -0.5),
    }


def reference(x_prompt, x_sample, cache_nsa_kv, cache_nsa_win, state_conv_a, state_lru_h, state_lru_conv, page_table,
              ln_g, ln_b, a_w_in, a_conv_w, a_w_out, b_w_in, b_w_cmp, b_w_out,
              c_w_in, c_conv_w, c_conv_b, c_w_a, c_b_a, c_w_x, c_b_x, c_lam, c_w_out):
    yp, ys = x_prompt, x_sample
    bp, bs = x_prompt.shape[0], x_sample.shape[0]
    dt = x_prompt.dtype
    n_pages = page_table.shape[1]
    conv_a_p, conv_a_s = [], []
    rows_p, rows_s, win_p, win_s = [], [], [], []
    h_p, h_s, cc_p, cc_s = [], [], [], []
    for i in range(DEPTH):
        kind, j = i % N_MIXERS, i // N_MIXERS
        if kind == 0:
            mp, sp = short_conv_mixer(yp, jnp.zeros((bp, CONV_A_WIDTH - 1, D_CONV), dt), a_w_in[j], a_conv_w[j], a_w_out[j])
            ms, ss = short_conv_mixer(ys, state_conv_a[j], a_w_in[j], a_conv_w[j], a_w_out[j])
            conv_a_p.append(sp)
            conv_a_s.append(ss)
        elif kind == 1:
            past = cache_nsa_kv[j][page_table].reshape(bs, n_pages * PAGE_SIZE, 4, N_KV, HEAD_DIM)
            mp, rp, wp = nsa_mixer(yp, jnp.zeros((bp, 0, 4, N_KV, HEAD_DIM), dt), jnp.zeros((bp, 0, 2, N_KV, HEAD_DIM), dt),
                                   b_w_in[j], b_w_cmp[j], b_w_out[j])
            ms, rs, ws = nsa_mixer(ys, past, cache_nsa_win[j], b_w_in[j], b_w_cmp[j], b_w_out[j])
            rows_p.append(rp)
            rows_s.append(rs)
            win_p.append(wp)
            win_s.append(ws)
        else:
            mp, hp, cp = rglru_mixer(yp, jnp.zeros((bp, D_RNN), dt), jnp.zeros((bp, CONV_C_WIDTH - 1, D_RNN), dt),
                                     c_w_in[j], c_conv_w[j], c_conv_b[j], c_w_a[j], c_b_a[j], c_w_x[j], c_b_x[j], c_lam[j], c_w_out[j])
            ms, hs, cs = rglru_mixer(ys, state_lru_h[j], state_lru_conv[j],
                                     c_w_in[j], c_conv_w[j], c_conv_b[j], c_w_a[j], c_b_a[j], c_w_x[j], c_b_x[j], c_lam[j], c_w_out[j])
            h_p.append(hp)
            h_s.append(hs)
            cc_p.append(cp)
            cc_s.append(cs)
        yp = layer_norm(ALPHA * yp + mp, ln_g[i], ln_b[i])
        ys = layer_norm(ALPHA * ys + ms, ln_g[i], ln_b[i])
    return (yp, ys, jnp.stack(conv_a_p), jnp.stack(conv_a_s), jnp.stack(rows_p), jnp.stack(rows_s),
            jnp.stack(win_p), jnp.stack(win_s), jnp.stack(h_p), jnp.stack(h_s), jnp.stack(cc_p), jnp.stack(cc_s))
```

```python
import contextlib
import os
import numpy as np
KDBG = int(os.environ.get('KDBG', '99'))
import concourse.bass as bass
import concourse.mybir as mybir
from concourse.bass_utils import run_bass_kernel_spmd

F32 = mybir.dt.float32
BF16 = mybir.dt.bfloat16
I32 = mybir.dt.int32
I16 = mybir.dt.int16
AF = mybir.ActivationFunctionType
ALU = mybir.AluOpType
AX = mybir.AxisListType

NCORES = 8
D = 1024
T = 2048
NS = 16
NTOK = T + NS
ALPHA = (2 * 4) ** 0.25
LN_EPS = 1e-5
TT = [(0, 512), (512, 512), (1024, 512), (1536, 512), (2048, 16)]
BIG = 30000.0


class Buf:
    __slots__ = ("name", "w", "r")

    def __init__(self, name=""):
        self.name = name
        self.w = None
        self.r = []


class FW:
    SAME = {"pe": False, "dve": True, "act": True, "pool": True, "sp": False}
    NDMA = 12

    def __init__(self, nc, es):
        self.nc = nc
        self.eng = {"pe": nc.tensor, "dve": nc.vector, "act": nc.scalar, "pool": nc.gpsimd, "sp": nc.sync}
        self.sem = {k: es.enter_context(nc.semaphore("s_" + k)) for k in self.eng}
        self.cnt = {k: 0 for k in self.eng}
        self.waited = {}
        self.dsem = {q: [es.enter_context(nc.semaphore(f"d_{q}{i}")) for i in range(self.NDMA)]
                     for q in ("sp", "act", "pool")}
        self.dcnt = {q: 0 for q in self.dsem}
        self.dtok = {q: [] for q in self.dsem}
        self.out_tokens = []
        self.ninst = 0

    def need(self, e, tok):
        if tok is None:
            return
        sem, val, src = tok
        if src == e and not self.SAME[e]:
            return
        key = (e, id(sem))
        if self.waited.get(key, 0) >= val:
            return
        self.eng[e].wait_ge(sem, val)
        self.waited[key] = val

    def deps(self, e, reads, writes):
        for b in reads:
            self.need(e, b.w)
        for b in writes:
            self.need(e, b.w)
            for t in b.r:
                self.need(e, t)

    def commit(self, tok, reads, writes):
        for b in reads:
            b.r.append(tok)
            if len(b.r) > 48:
                best = {}
                for t in b.r:
                    k = id(t[0])
                    if k not in best or best[k][1] < t[1]:
                        best[k] = t
                b.r = list(best.values())
        for b in writes:
            b.w = tok
            b.r = []

    def op(self, e, fn, reads=(), writes=()):
        self.deps(e, reads, writes)
        ins = fn(self.eng[e])
        self.cnt[e] += 1
        ins.then_inc(self.sem[e], 1)
        tok = (self.sem[e], self.cnt[e], e)
        self.commit(tok, reads, writes)
        self.ninst += 1
        return tok

    def dma(self, q, out, in_, reads=(), writes=(), is_output=False, fn=None, **kw):
        self.deps(q, reads, writes)
        k = self.dcnt[q]
        n = self.NDMA
        sem = self.dsem[q][k % n]
        val = 16 * (k // n + 1)
        if k >= n:
            self.need(q, self.dtok[q][k - n])
        if fn is not None:
            ins = fn(self.eng[q])
        else:
            ins = self.eng[q].dma_start(out=out, in_=in_, **kw)
        ins.then_inc(sem, 16)
        tok = (sem, val, "dma_" + q)
        self.dtok[q].append(tok)
        self.dcnt[q] += 1
        self.commit(tok, reads, writes)
        if is_output:
            self.out_tokens.append(tok)
        self.ninst += 1
        return tok

    def all_tokens(self):
        toks = []
        for q in self.dtok:
            toks += self.dtok[q][-self.NDMA:]
        for e in ("pe", "dve", "act", "pool"):
            if self.cnt[e]:
                toks.append((self.sem[e], self.cnt[e], e))
        return toks

    def barrier(self):
        toks = self.all_tokens()
        for e in ("pe", "dve", "act", "pool", "sp"):
            for t in toks:
                if t[2] != e:
                    self.need(e, t)
            if self.cnt.get(e, 0) and e != "sp":
                self.need_self(e)

    def need_self(self, e):
        sem, val = self.sem[e], self.cnt[e]
        key = (e, id(sem))
        if self.waited.get(key, 0) >= val:
            return
        self.eng[e].wait_ge(sem, val)
        self.waited[key] = val

    def finish(self):
        for tok in self.out_tokens:
            self.need("sp", tok)
        for t in self.all_tokens():
            self.need("sp", t)


def _wtile(w, cols):
    cols = np.asarray(cols)
    out = np.zeros((128, 8, len(cols)), np.float32)
    valid = cols >= 0
    sub = w[:, cols[valid]].reshape(8, 128, -1).transpose(1, 0, 2)
    out[:, :, valid] = sub
    return out


def _weights_layout(inp):
    tiles = []
    idx = {}

    def add(name, arr):
        idx[name] = len(tiles)
        tiles.append(np.ascontiguousarray(arr.reshape(128, 4096)))

    def layer_a(j, tag):
        w = inp["a_w_in"][j]
        for c in range(8):
            cols = np.concatenate([np.arange(c * 128, c * 128 + 128) + o for o in (0, 1024, 2048, 3072)])
            add(f"{tag}_in{c}", _wtile(w, cols))
        wo = inp["a_w_out"][j]
        for m in range(2):
            add(f"{tag}_out{m}", _wtile(wo, np.arange(m * 512, m * 512 + 512)))

    layer_a(0, "a0")
    wb = inp["b_w_in"][0]
    for m in range(2):
        add(f"b_q{m}", _wtile(wb, np.arange(m * 512, m * 512 + 512)))
    kv0 = 1024

    def comp(ci, g):
        return kv0 + ci * 128 + g * 64 + np.arange(64)

    def dup(ci, g):
        return np.concatenate([comp(ci, g), comp(ci, g)])

    plain = lambda ci: kv0 + ci * 128 + np.arange(128)
    gates = np.concatenate([np.arange(1792, 1840), -np.ones(80, np.int64)])
    add("b_kvA", _wtile(wb, np.concatenate([dup(0, 0), dup(0, 1), plain(1), dup(2, 0)])))
    add("b_kvB", _wtile(wb, np.concatenate([dup(2, 1), dup(4, 0), dup(4, 1), gates])))
    add("b_kvR1", _wtile(wb, np.arange(1024, 1536)))
    add("b_kvR2", _wtile(wb, np.concatenate([np.arange(1536, 1840), -np.ones(512 - 304, np.int64)])))
    for m in range(2):
        add(f"b_z{m}", _wtile(wb, 1840 + np.arange(m * 512, m * 512 + 512)))
    wo = inp["b_w_out"][0]
    for m in range(2):
        add(f"b_out{m}", _wtile(wo, np.arange(m * 512, m * 512 + 512)))
    wc = inp["c_w_in"][0]

    def blk(w):
        return w.reshape(4, 2, 128, 256).transpose(2, 0, 1, 3).reshape(128, 2048)
    add("c_u0", _wtile(wc, np.arange(0, 512)))
    add("c_wax", np.concatenate([blk(inp["c_w_a"][0]), blk(inp["c_w_x"][0])], axis=1))
    add("c_u1", _wtile(wc, np.arange(512, 1024)))
    for m in range(2):
        add(f"c_z{m}", _wtile(wc, 1024 + np.arange(m * 512, m * 512 + 512)))
    wo = inp["c_w_out"][0]
    for m in range(2):
        add(f"c_out{m}", _wtile(wo, np.arange(m * 512, m * 512 + 512)))
    layer_a(1, "a1")
    return np.stack(tiles), idx


def _fm(v):
    v = np.asarray(v, np.float32)
    lead = v.shape[:-1]
    return np.ascontiguousarray(np.moveaxis(v.reshape(lead + (8, 128)), -1, 0))


VEC = {}


def _vec_layout(inp):
    cols = []

    def add(name, arr):
        arr = np.asarray(arr, np.float32).reshape(128, -1)
        VEC[name] = (sum(a.shape[1] for a in cols), arr.shape[1])
        cols.append(arr)

    add("ln_g", _fm(inp["ln_g"]))
    add("ln_b", _fm(inp["ln_b"]))
    add("a_conv", _fm(inp["a_conv_w"]))
    add("c_conv", _fm(inp["c_conv_w"]))
    add("c_conv_b", _fm(inp["c_conv_b"]))
    add("c_b_a", _fm(inp["c_b_a"]))
    add("c_b_x", _fm(inp["c_b_x"]))
    add("c_lam", _fm(inp["c_lam"]))
    wc = inp["b_w_cmp"][0]
    for g in range(2):
        a = wc[0, :, g, :].T
        add(f"b_wk{g}", np.concatenate([a, a], axis=0))
    add("b_wv", wc[1].transpose(1, 2, 0).reshape(128, 32))
    return np.ascontiguousarray(np.concatenate(cols, axis=1))


RING = 3


class Kern:
    def __init__(self, nc, es, widx, ntiles, nvec, stages):
        self.nc = nc
        self.es = es
        self.fw = FW(nc, es)
        self.widx = widx
        self.stages = stages
        dt = nc.dram_tensor
        self.d_xp = dt("xp", [T, D], F32, kind="ExternalInput").ap()
        self.d_xs = dt("xs", [NS, D], F32, kind="ExternalInput").ap()
        self.d_wt = dt("wt", [ntiles, 128, 4096], F32, kind="ExternalInput").ap()
        self.d_vecs = dt("vecs", [128, nvec], F32, kind="ExternalInput").ap()
        self.d_ident = dt("ident", [128, 128], F32, kind="ExternalInput").ap()
        self.d_sconv = dt("sconv", [2, 4, 2, D], F32, kind="ExternalInput").ap()
        self.d_lruh = dt("lruh", [4, D], F32, kind="ExternalInput").ap()
        self.d_lruconv = dt("lruconv", [4, 3, D], F32, kind="ExternalInput").ap()
        self.d_lhp = dt("lhp", [D], F32, kind="ExternalOutput").ap()
        self.d_lhs = dt("lhs", [4, D], F32, kind="ExternalOutput").ap()
        self.d_lcp = dt("lcp", [3, D], F32, kind="ExternalOutput").ap()
        self.d_lcs = dt("lcs", [4, 3, D], F32, kind="ExternalOutput").ap()
        self.d_tdw = dt("tdw", [128, 2432], F32, kind="ExternalInput").ap()
        self.d_tww = dt("tww", [128, 1408], F32, kind="ExternalInput").ap()
        self.d_tdc = dt("tdc", [64, 2048], F32, kind="ExternalInput").ap()
        self.d_te2 = dt("te2", [128, 2048], F32, kind="ExternalInput").ap()
        self.d_tselg = dt("tselg", [48, 3072], F32, kind="ExternalInput").ap()
        self.d_tm12 = dt("tm12", [128, 2048], F32, kind="ExternalInput").ap()
        self.d_biasc = dt("biasc", [16, 128, 1024], F32, kind="ExternalInput").ap()
        self.d_winbuf = dt("winbuf", [4, 512, 256], F32, kind="ExternalInput").ap()
        self.d_rowsp = dt("rowsp", [T, 512], F32, kind="ExternalOutput").ap()
        self.d_winp = dt("winp", [512, 256], F32, kind="ExternalOutput").ap()
        self.d_rowss = dt("rowss", [4, 4, 512], F32, kind="ExternalOutput").ap()
        self.d_wins = dt("wins", [4, 512, 256], F32, kind="ExternalOutput").ap()
        self.d_kvpool = dt("kvpool", [2560 * 128 * 2, 256], F32, kind="ExternalInput").ap()
        self.d_ptab = dt("ptab", [4, 64], I32, kind="ExternalInput").ap()
        self.d_scr = dt("scr_slc", [256, 128, 256], F32, kind="Internal").ap()
        self.scrb = [Buf() for _ in range(256)]
        self.d_tee = dt("tee", [128, 8192], F32, kind="ExternalInput").ap()
        self.d_tb9 = dt("tb9", [128, 71 * 64], F32, kind="ExternalInput").ap()
        self.d_tshift = dt("tshift", [128, 512], F32, kind="ExternalInput").ap()
        self.d_ta9 = dt("ta9", [128, 128], F32, kind="ExternalInput").ap()
        self.d_tselw = dt("tselw", [128, 256], F32, kind="ExternalInput").ap()
        self.d_tb2 = dt("tb2", [128, 64], F32, kind="ExternalInput").ap()
        self.d_twt = dt("twt", [128, 256], F32, kind="ExternalInput").ap()
        self.d_yp = dt("yp", [T, D], F32, kind="ExternalOutput").ap()
        self.d_ys = dt("ys", [NS, D], F32, kind="ExternalOutput").ap()
        self.d_cap = dt("cap", [2, 2, D], F32, kind="ExternalOutput").ap()
        self.d_cas = dt("cas", [2, 4, 2, D], F32, kind="ExternalOutput").ap()
        sb = self.sb
        self.XH = sb("XH", [128, 8, NTOK], BF16)
        self.XL = sb("XL", [128, 8, NTOK], BF16)
        self.G = sb("G", [128, 8, NTOK], BF16)
        self.ring = [sb(f"ring{i}", [128, 8, 512], BF16) for i in range(RING)]
        self.ringb = [Buf(f"ring{i}") for i in range(RING)]
        self.vecs = sb("vecs", [128, nvec], F32)
        self.ident = sb("ident", [128, 128], F32)
        self.ones = sb("ones", [128, 128], BF16)
        self.b_vecs, self.b_ident, self.b_ones = Buf("vecs"), Buf("ident"), Buf("ones")
        self.xb = {(k, t): Buf(f"x{k}_{t}") for k in range(8) for t in range(5)}
        self.gb = {(k, t): Buf(f"g{k}_{t}") for k in range(8) for t in range(5)}
        self.psum = [es.enter_context(nc.psum_tensor(f"ps{i}", [128, 512], F32)) for i in range(8)]
        self.psb = [Buf(f"ps{i}") for i in range(8)]
        self.psi = 0
        self.ps_rot = list(range(8))
        self.wnext = 0
        self.ntiles = ntiles
        self.tmp_i = {}

    def sb(self, name, shape, dtype=F32, es=None):
        self.nsb = getattr(self, "nsb", 0) + 1
        return (es or self.es).enter_context(self.nc.sbuf_tensor(f"s{self.nsb}_{name}", shape, dtype))

    def ps(self):
        i = self.ps_rot[self.psi % len(self.ps_rot)]
        self.psi += 1
        return self.psum[i], self.psb[i]

    def vec(self, name):
        o, n = VEC[name]
        return self.vecs[:, o:o + n]

    def _issue_w(self, i):
        slot = i % RING
        self.fw.dma("pool", self.ring[slot][:], self.d_wt[i].rearrange("p (k n) -> p k n", k=8),
                    writes=[self.ringb[slot]])

    def wtile(self, name, group=None):
        i = self.widx[name]
        g0 = self.widx[group] if group else i
        assert i < g0 + RING
        while self.wnext < min(self.ntiles, g0 + RING):
            self._issue_w(self.wnext)
            self.wnext += 1
        return self.ring[i % RING], self.ringb[i % RING]

    def setup(self):
        fw = self.fw
        fw.dma("sp", self.vecs[:], self.d_vecs[:, :], writes=[self.b_vecs])
        fw.dma("sp", self.ident[:], self.d_ident[:, :], writes=[self.b_ident])
        fw.op("dve", lambda e: e.memset(self.ones[:], 1.0), writes=[self.b_ones])

    def load_x(self, es):
        fw = self.fw
        xin = [self.sb(f"xin{i}", [128, D], F32, es) for i in range(2)]
        xinb = [Buf(), Buf()]
        for i in range(17):
            n = 128 if i < 16 else NS
            t0 = i * 128
            tt = min(i // 4, 4)
            xi, xib = xin[i % 2], xinb[i % 2]
            src = self.d_xp[t0:t0 + 128, :] if i < 16 else self.d_xs[:, :]
            fw.dma("sp", xi[0:n, :], src, writes=[xib])
            for half in range(2):
                pst, psb = self.ps()
                for q in range(4):
                    dc = half * 4 + q
                    fw.op("pe", lambda e: e.transpose(pst[:, q * 128:q * 128 + n], xi[0:n, dc * 128:(dc + 1) * 128], self.ident[0:n, 0:n]),
                          reads=[xib, self.b_ident], writes=[psb])
                bufs = [self.xb[(half * 4 + q, tt)] for q in range(4)]
                src_ps = pst[:].rearrange("p (q t) -> p q t", q=4)[:, :, 0:n]
                hi = self.XH[:, half * 4:half * 4 + 4, t0:t0 + n]
                lo = self.XL[:, half * 4:half * 4 + 4, t0:t0 + n]
                fw.op("act", lambda e: e.activation(out=hi, in_=src_ps, func=AF.Identity), reads=[psb], writes=bufs)
                fw.op("dve", lambda e: e.tensor_tensor(out=lo, in0=src_ps, in1=hi, op=ALU.subtract), reads=[psb] + bufs, writes=bufs)

    def resid_ln_begin(self, es):
        self.R = self.sb("R", [128, 8, 512], F32, es)
        self.Rb = [Buf(f"R{m}") for m in range(8)]
        self.lt = {n: (self.sb("ln_" + n, [128, 512], F32, es), Buf()) for n in ("t", "M", "A", "B")}
        self.lbf = [(self.sb(f"ln_bf{i}", [128, 512], BF16, es), Buf()) for i in range(4)]
        self.lbi = 0
        self.lhl = [(self.sb(f"ln_hl{i}", [128, 512], F32, es), Buf()) for i in range(2)]

    def resid_add(self, layer, tt, m, yps, ypsb, acc):
        fw = self.fw
        t0, n = TT[tt]
        tT, tB = self.lt["t"]
        xb = self.xb[(m, tt)]
        hl_, hlb_ = self.lhl[m % 2]
        fw.op("pool", lambda e: e.tensor_tensor(out=hl_[:, 0:n], in0=self.XH[:, m, t0:t0 + n], in1=self.XL[:, m, t0:t0 + n], op=ALU.add),
              reads=[xb], writes=[hlb_])
        fw.op("dve", lambda e: e.scalar_tensor_tensor(out=self.R[:, m, 0:n], in0=hl_[:, 0:n], scalar=ALPHA, in1=yps[:, 0:n], op0=ALU.mult, op1=ALU.add),
              reads=[hlb_, ypsb], writes=[self.Rb[m]])
        (b1, b1b), (b2, b2b) = self.lbf[self.lbi], self.lbf[self.lbi + 1]
        self.lbi = (self.lbi + 2) % 4
        fw.op("act", lambda e: e.activation(out=b1[:, 0:n], in_=self.R[:, m, 0:n], func=AF.Identity), reads=[self.Rb[m]], writes=[b1b])
        fw.op("act", lambda e: e.activation(out=b2[:, 0:n], in_=self.R[:, m, 0:n], func=AF.Square), reads=[self.Rb[m]], writes=[b2b])
        (ps1, ps1b), (ps2, ps2b) = acc
        fw.op("pe", lambda e: e.matmul(ps1[:, 0:n], lhsT=self.ones[:, :], rhs=b1[:, 0:n], start=(m == 0), stop=(m == 7)),
              reads=[b1b, self.b_ones], writes=[ps1b])
        fw.op("pe", lambda e: e.matmul(ps2[:, 0:n], lhsT=self.ones[:, :], rhs=b2[:, 0:n], start=(m == 0), stop=(m == 7)),
              reads=[b2b, self.b_ones], writes=[ps2b])

    def ln_apply(self, layer, tt, acc, final, es_out=None):
        fw = self.fw
        t0, n = TT[tt]
        (ps1, ps1b), (ps2, ps2b) = acc
        (tT, tB), (M, Mb), (A, Ab), (B, Bb) = self.lt["t"], self.lt["M"], self.lt["A"], self.lt["B"]
        fw.op("act", lambda e: e.activation(out=M[:, 0:n], in_=ps1[:, 0:n], func=AF.Identity, scale=1.0 / D), reads=[ps1b], writes=[Mb])
        fw.op("dve", lambda e: e.tensor_tensor(out=tT[:, 0:n], in0=M[:, 0:n], in1=M[:, 0:n], op=ALU.mult), reads=[Mb], writes=[tB])
        fw.op("dve", lambda e: e.scalar_tensor_tensor(out=tT[:, 0:n], in0=ps2[:, 0:n], scalar=1.0 / D, in1=tT[:, 0:n], op0=ALU.mult, op1=ALU.subtract),
              reads=[ps2b, tB], writes=[tB])
        fw.op("dve", lambda e: e.tensor_scalar(out=tT[:, 0:n], in0=tT[:, 0:n], scalar1=LN_EPS, scalar2=None, op0=ALU.add), reads=[tB], writes=[tB])
        fw.op("act", lambda e: e.activation(out=tT[:, 0:n], in_=tT[:, 0:n], func=AF.Sqrt), reads=[tB], writes=[tB])
        fw.op("dve", lambda e: e.reciprocal(out=A[:, 0:n], in_=tT[:, 0:n]), reads=[tB], writes=[Ab])
        fw.op("dve", lambda e: e.scalar_tensor_tensor(out=B[:, 0:n], in0=M[:, 0:n], scalar=-1.0, in1=A[:, 0:n], op0=ALU.mult, op1=ALU.mult),
              reads=[Mb, Ab], writes=[Bb])
        og, ob = VEC["ln_g"][0] + layer * 8, VEC["ln_b"][0] + layer * 8
        Rm = [self.R[:, m, 0:n] for m in range(8)]
        for m in range(8):
            fw.op("dve", lambda e: e.tensor_tensor(out=Rm[m], in0=Rm[m], in1=A[:, 0:n], op=ALU.mult), reads=[self.Rb[m], Ab], writes=[self.Rb[m]])
        for m in range(8):
            fw.op("dve", lambda e: e.tensor_tensor(out=Rm[m], in0=Rm[m], in1=B[:, 0:n], op=ALU.add), reads=[self.Rb[m], Bb], writes=[self.Rb[m]])
        for m in range(8):
            fw.op("act", lambda e: e.activation(out=Rm[m], in_=Rm[m], func=AF.Identity, scale=self.vecs[:, og + m:og + m + 1], bias=self.vecs[:, ob + m:ob + m + 1]),
                  reads=[self.Rb[m], self.b_vecs], writes=[self.Rb[m]])
        if not final:
            for m in range(8):
                fw.op("act", lambda e: e.activation(out=self.XH[:, m, t0:t0 + n], in_=Rm[m], func=AF.Identity), reads=[self.Rb[m]], writes=[self.xb[(m, tt)]])
            for m in range(8):
                fw.op("pool", lambda e: e.tensor_tensor(out=self.XL[:, m, t0:t0 + n], in0=Rm[m], in1=self.XH[:, m, t0:t0 + n], op=ALU.subtract),
                      reads=[self.Rb[m], self.xb[(m, tt)]], writes=[self.xb[(m, tt)]])
        if final:
            for j in range((n + 127) // 128):
                nn = min(128, n - j * 128)
                yt, ytb = self.yt[self.yti]
                self.yti = (self.yti + 1) % 2
                for half in range(2):
                    pst, psb = self.ps()
                    for q in range(4):
                        m = half * 4 + q
                        fw.op("pe", lambda e: e.transpose(pst[0:nn, q * 128:(q + 1) * 128], self.R[:, m, j * 128:j * 128 + nn], self.ident[:, :]),
                              reads=[self.Rb[m], self.b_ident], writes=[psb])
                    if half == 0:
                        fw.op("act", lambda e: e.activation(out=yt[0:nn, 0:512], in_=pst[0:nn, :], func=AF.Identity), reads=[psb], writes=[ytb])
                    else:
                        fw.op("dve", lambda e: e.tensor_copy(out=yt[0:nn, 512:1024], in_=pst[0:nn, :]), reads=[psb], writes=[ytb])
                dst = self.d_yp[t0 + j * 128:t0 + j * 128 + nn, :] if tt < 4 else self.d_ys[:, :]
                fw.dma("sp", dst, yt[0:nn, :], reads=[ytb], is_output=True)

    def out_proj_ln(self, layer, wnames, final):
        fw = self.fw
        self.ps_rot = list(range(6))
        for tt in range(5):
            t0, n = TT[tt]
            acc = ((self.psum[6], self.psb[6]), (self.psum[7], self.psb[7]))
            for m in range(8):
                w, wb = self.wtile(wnames[m // 4], group=wnames[0])
                yps, ypsb = self.ps()
                for k in range(8):
                    fw.op("pe", lambda e: e.matmul(yps[:, 0:n], lhsT=w[:, k, (m % 4) * 128:(m % 4) * 128 + 128], rhs=self.G[:, k, t0:t0 + n],
                                                   start=(k == 0), stop=(k == 7)),
                          reads=[wb, self.gb[(k, tt)]], writes=[ypsb])
                self.resid_add(layer, tt, m, yps, ypsb, acc)
            self.ln_apply(layer, tt, acc, final)
        self.ps_rot = list(range(8))

    def layer_A(self, layer, j, final=False):
        fw = self.fw
        tag = f"a{j}"
        with contextlib.ExitStack() as es:
            self.resid_ln_begin(es)
            if final:
                self.yt = [(self.sb(f"yt{i}", [128, D], F32, es), Buf()) for i in range(2)]
                self.yti = 0
            ext = self.sb("a_ext", [128, 8, 514], F32, es)
            extb = [Buf(f"ext{c}") for c in range(8)]
            exs = self.sb("a_exs", [128, 8, 4, 6], F32, es)
            exsb = [Buf(f"exs{c}") for c in range(8)]
            tmp = [(self.sb(f"a_t{i}", [128, 512], F32, es), Buf()) for i in range(8)]
            ti = [0]

            def T_():
                r = tmp[ti[0]]
                ti[0] = (ti[0] + 1) % len(tmp)
                return r
            fw.op("pool", lambda e: e.memset(ext[:, :, 0:2], 0.0), writes=extb)
            st_in = self.sb("a_stin", [128, 8, 8], F32, es)
            st_out = self.sb("a_stout", [128, 8, 8], F32, es)
            stib, stob = Buf(), [Buf() for _ in range(8)]
            for c in range(8):
                fw.dma("sp", st_in[:, c, :], self.d_sconv[j, :, :, c * 128:(c + 1) * 128].rearrange("s r p -> p (s r)"),
                       writes=[stib], allow_slow_non_contiguous=True)
            for c in range(8):
                fw.op("act", lambda e: e.activation(out=exs[:, c, :, 0:2], in_=st_in[:, c, :].rearrange("p (s r) -> p s r", s=4), func=AF.Identity),
                      reads=[stib], writes=[exsb[c]])
            cw = VEC["a_conv"][0] + j * 24
            for c in range(8):
                w, wb = self.wtile(f"{tag}_in{c}")
                wcol = [self.vecs[:, cw + r * 8 + c:cw + r * 8 + c + 1] for r in range(3)]
                for tt in range(5):
                    t0, n = TT[tt]
                    pss = []
                    for q in range(4):
                        p_, pb_ = self.ps()
                        for k in range(8):
                            fw.op("pe", lambda e: e.matmul(p_[:, 0:n], lhsT=w[:, k, q * 128:(q + 1) * 128], rhs=self.XH[:, k, t0:t0 + n],
                                                           start=(k == 0), stop=(k == 7)),
                                  reads=[wb, self.xb[(k, tt)]], writes=[pb_])
                        pss.append((p_, pb_))
                    (ph, phb), (pb, pbb), (pc, pcb), (pz, pzb) = pss
                    (sz, szb), (hh, hhb), (cv, cvb), (gbt, gbb) = T_(), T_(), T_(), T_()
                    fw.op("act", lambda e: e.activation(out=hh[:, 0:n], in_=ph[:, 0:n], func=AF.Identity), reads=[phb], writes=[hhb])
                    fw.op("act", lambda e: e.activation(out=sz[:, 0:n], in_=pz[:, 0:n], func=AF.Silu), reads=[pzb], writes=[szb])
                    if tt < 4:
                        eb = extb[c]
                        u_new = ext[:, c, 2:2 + n]
                        sh = [ext[:, c, r:r + n] for r in range(3)]
                        o_cv, o_gb, o_hh, o_pc = cv[:, 0:n], gbt[:, 0:n], hh[:, 0:n], pc[:, 0:n]
                        o_sz, o_pb = sz[:, 0:n], pb[:, 0:n]
                        o_g = self.G[:, c, t0:t0 + n]
                    else:
                        eb = exsb[c]
                        v4 = lambda ap: ap[:, 0:n].rearrange("p (s t) -> p s t", s=4)
                        u_new = exs[:, c, :, 2:6]
                        sh = [exs[:, c, :, r:r + 4] for r in range(3)]
                        o_cv, o_gb, o_hh, o_pc = v4(cv), v4(gbt), v4(hh), v4(pc)
                        o_sz, o_pb = v4(sz), v4(pb)
                        o_g = self.G[:, c, t0:t0 + n].rearrange("p (s t) -> p s t", s=4)
                    fw.op("dve", lambda e: e.tensor_tensor(out=u_new, in0=o_pc, in1=o_hh, op=ALU.mult), reads=[pcb, hhb], writes=[eb])
                    fw.op("dve", lambda e: e.tensor_scalar(out=o_cv, in0=sh[0], scalar1=wcol[0], scalar2=None, op0=ALU.mult),
                          reads=[eb, self.b_vecs], writes=[cvb])
                    fw.op("dve", lambda e: e.scalar_tensor_tensor(out=o_cv, in0=sh[1], scalar=wcol[1], in1=o_cv, op0=ALU.mult, op1=ALU.add),
                          reads=[eb, cvb], writes=[cvb])
                    fw.op("dve", lambda e: e.scalar_tensor_tensor(out=o_cv, in0=sh[2], scalar=wcol[2], in1=o_cv, op0=ALU.mult, op1=ALU.add),
                          reads=[eb, cvb], writes=[cvb])
                    fw.op("dve", lambda e: e.tensor_tensor(out=o_gb, in0=o_pb, in1=o_sz, op=ALU.mult), reads=[pbb, szb], writes=[gbb])
                    fw.op("dve", lambda e: e.tensor_tensor(out=o_g, in0=o_gb, in1=o_cv, op=ALU.mult), reads=[gbb, cvb], writes=[self.gb[(c, tt)]])
                    if tt < 3:
                        fw.op("act", lambda e: e.activation(out=ext[:, c, 0:2], in_=ext[:, c, 512:514], func=AF.Identity), reads=[eb], writes=[eb])
                fw.dma("sp", self.d_cap[j, :, c * 128:(c + 1) * 128].rearrange("r p -> p r"), ext[:, c, 512:514],
                       reads=[extb[c]], is_output=True, allow_slow_non_contiguous=True)
                fw.op("act", lambda e: e.activation(out=st_out[:, c, :].rearrange("p (s r) -> p s r", s=4), in_=exs[:, c, :, 4:6], func=AF.Identity),
                      reads=[exsb[c]], writes=[stob[c]])
                fw.dma("sp", self.d_cas[j, :, :, c * 128:(c + 1) * 128].rearrange("s r p -> p (s r)"), st_out[:, c, :],
                       reads=[stob[c]], is_output=True, allow_slow_non_contiguous=True)
            self.out_proj_ln(layer, [f"{tag}_out0", f"{tag}_out1"], final)
            fw.barrier()


    def layer_C(self, layer):
        fw = self.fw
        with contextlib.ExitStack() as es:
            self.resid_ln_begin(es)
            ext = self.sb("c_ext", [128, 2, 515], F32, es)
            extb = [Buf(), Buf()]
            exs = self.sb("c_exs", [128, 8, 4, 7], F32, es)
            exsb = [Buf() for _ in range(8)]
            st_in = self.sb("c_stin", [128, 8, 12], F32, es)
            st_out = self.sb("c_stout", [128, 8, 12], F32, es)
            stib, stob = Buf(), [Buf() for _ in range(8)]
            h0 = self.sb("c_h0", [128, 8, 4], F32, es)
            h0b = Buf()
            hl = self.sb("c_hl", [128, 8], F32, es)
            hlb = [Buf() for _ in range(8)]
            hs = self.sb("c_hs", [128, 8, 4], F32, es)
            hsb = [Buf() for _ in range(8)]
            cl = self.sb("c_cl", [128, 16], F32, es)
            clb = Buf()
            tmp = [(self.sb(f"c_t{i}", [128, 512], F32, es), Buf()) for i in range(10)]
            ucb16 = [(self.sb(f"c_ub{i}", [128, 512], BF16, es), Buf()) for i in range(4)]
            ti = [0]

            def T_():
                r = tmp[ti[0]]
                ti[0] = (ti[0] + 1) % len(tmp)
                return r
            for c in range(8):
                fw.dma("sp", st_in[:, c, :], self.d_lruconv[:, :, c * 128:(c + 1) * 128].rearrange("s r p -> p (s r)"),
                       writes=[stib], allow_slow_non_contiguous=True)
                fw.dma("sp", h0[:, c, :], self.d_lruh[:, c * 128:(c + 1) * 128].rearrange("s p -> p s"),
                       writes=[h0b], allow_slow_non_contiguous=True)
            for c in range(8):
                fw.op("act", lambda e: e.activation(out=exs[:, c, :, 0:3], in_=st_in[:, c, :].rearrange("p (s r) -> p s r", s=4), func=AF.Identity),
                      reads=[stib], writes=[exsb[c]])
            lam = self.vec("c_lam")
            fw.op("act", lambda e: e.activation(out=cl[:, 0:8], in_=lam, func=AF.Exp, scale=-1.0), reads=[self.b_vecs], writes=[clb])
            fw.op("dve", lambda e: e.tensor_scalar(out=cl[:, 0:8], in0=cl[:, 0:8], scalar1=1.0, scalar2=None, op0=ALU.add), reads=[clb], writes=[clb])
            fw.op("act", lambda e: e.activation(out=cl[:, 0:8], in_=cl[:, 0:8], func=AF.Ln), reads=[clb], writes=[clb])
            fw.op("dve", lambda e: e.tensor_scalar(out=cl[:, 8:16], in0=cl[:, 0:8], scalar1=-16.0, scalar2=None, op0=ALU.mult), reads=[clb], writes=[clb])
            fw.op("dve", lambda e: e.tensor_scalar(out=cl[:, 0:8], in0=cl[:, 0:8], scalar1=-8.0, scalar2=None, op0=ALU.mult), reads=[clb], writes=[clb])
            cw = VEC["c_conv"][0]
            V1 = lambda name, m: self.vecs[:, VEC[name][0] + m:VEC[name][0] + m + 1]
            ucf = [(self.sb(f"c_uc{i}", [128, 512], F32, es), Buf()) for i in range(4)]
            st = {}

            def front(nb, tt):
                wu, wub = self.wtile("c_u0" if nb < 2 else "c_u1", group="c_u0")
                t0, n = TT[tt]
                ucs = []
                for kk in range(2):
                    m = 2 * nb + kk
                    q = m % 4
                    pu, pub = self.ps()
                    for k in range(8):
                        fw.op("pe", lambda e: e.matmul(pu[:, 0:n], lhsT=wu[:, k, q * 128:(q + 1) * 128], rhs=self.XH[:, k, t0:t0 + n],
                                                       start=(k == 0), stop=(k == 7)),
                              reads=[wub, self.xb[(k, tt)]], writes=[pub])
                    wcol = [self.vecs[:, cw + r * 8 + m:cw + r * 8 + m + 1] for r in range(4)]
                    uc, ucbuf = ucf[(2 * tt + kk) % 4]
                    ub, ubb = ucb16[(2 * tt + kk) % 4]
                    if tt < 4:
                        eb = extb[kk]
                        u_new = ext[:, kk, 3:3 + n]
                        sh = [ext[:, kk, r:r + n] for r in range(4)]
                        o_uc, o_pu = uc[:, 0:n], pu[:, 0:n]
                    else:
                        eb = exsb[m]
                        v4 = lambda ap: ap[:, 0:n].rearrange("p (s t) -> p s t", s=4)
                        u_new = exs[:, m, :, 3:7]
                        sh = [exs[:, m, :, r:r + 4] for r in range(4)]
                        o_uc, o_pu = v4(uc), v4(pu)
                    fw.op("act", lambda e: e.activation(out=u_new, in_=o_pu, func=AF.Identity), reads=[pub], writes=[eb])
                    fw.op("dve", lambda e: e.tensor_scalar(out=o_uc, in0=sh[0], scalar1=wcol[0], scalar2=V1("c_conv_b", m), op0=ALU.mult, op1=ALU.add),
                          reads=[eb, self.b_vecs], writes=[ucbuf])
                    for r in range(1, 4):
                        fw.op("dve", lambda e: e.scalar_tensor_tensor(out=o_uc, in0=sh[r], scalar=wcol[r], in1=o_uc, op0=ALU.mult, op1=ALU.add),
                              reads=[eb, ucbuf], writes=[ucbuf])
                    fw.op("act", lambda e: e.activation(out=ub[:, 0:n], in_=uc[:, 0:n], func=AF.Identity), reads=[ucbuf], writes=[ubb])
                    if tt < 3:
                        fw.op("act", lambda e: e.activation(out=ext[:, kk, 0:3], in_=ext[:, kk, 512:515], func=AF.Identity), reads=[eb], writes=[eb])
                    if tt == 3:
                        fw.dma("sp", self.d_lcp[:, m * 128:(m + 1) * 128].rearrange("r p -> p r"), ext[:, kk, 512:515],
                               reads=[eb], is_output=True, allow_slow_non_contiguous=True)
                    if tt == 4:
                        fw.op("act", lambda e: e.activation(out=st_out[:, m, :].rearrange("p (s r) -> p s r", s=4), in_=exs[:, m, :, 4:7], func=AF.Identity),
                              reads=[eb], writes=[stob[m]])
                        fw.dma("sp", self.d_lcs[:, :, m * 128:(m + 1) * 128].rearrange("s r p -> p (s r)"), st_out[:, m, :],
                               reads=[stob[m]], is_output=True, allow_slow_non_contiguous=True)
                    ucs.append((uc, ucbuf, ub, ubb))
                st[(nb, tt)] = ucs

            def back(nb, tt):
                wax, waxb = self.wtile("c_wax", group="c_u0")
                t0, n = TT[tt]
                ucs = st.pop((nb, tt))
                P = []
                for kk in range(2):
                    pr, prb = self.ps()
                    pi, pib = self.ps()
                    for k in range(2):
                        fw.op("pe", lambda e: e.matmul(pr[:, 0:n], lhsT=wax[:, nb, k * 256 + kk * 128:k * 256 + kk * 128 + 128], rhs=ucs[k][2][:, 0:n],
                                                       start=(k == 0), stop=(k == 1)),
                              reads=[waxb, ucs[k][3]], writes=[prb])
                    for k in range(2):
                        fw.op("pe", lambda e: e.matmul(pi[:, 0:n], lhsT=wax[:, 4 + nb, k * 256 + kk * 128:k * 256 + kk * 128 + 128], rhs=ucs[k][2][:, 0:n],
                                                       start=(k == 0), stop=(k == 1)),
                              reads=[waxb, ucs[k][3]], writes=[pib])
                    P.append((pr, prb, pi, pib, T_(), T_(), T_(), T_(), T_()))
                for kk in range(2):
                    m = 2 * nb + kk
                    pr, prb, pi, pib, (rr, rrb), (ii, iib), (aa, aab), (bb, bbb), (hh, hhb) = P[kk]
                    fw.op("act", lambda e: e.activation(out=rr[:, 0:n], in_=pr[:, 0:n], func=AF.Sigmoid, bias=V1("c_b_a", m)), reads=[prb, self.b_vecs], writes=[rrb])
                    fw.op("act", lambda e: e.activation(out=ii[:, 0:n], in_=pi[:, 0:n], func=AF.Sigmoid, bias=V1("c_b_x", m)), reads=[pib, self.b_vecs], writes=[iib])
                for kk in range(2):
                    m = 2 * nb + kk
                    pr, prb, pi, pib, (rr, rrb), (ii, iib), (aa, aab), (bb, bbb), (hh, hhb) = P[kk]
                    fw.op("act", lambda e: e.activation(out=aa[:, 0:n], in_=rr[:, 0:n], func=AF.Exp, scale=cl[:, m:m + 1]), reads=[rrb, clb], writes=[aab])
                    fw.op("act", lambda e: e.activation(out=bb[:, 0:n], in_=rr[:, 0:n], func=AF.Exp, scale=cl[:, 8 + m:9 + m]), reads=[rrb, clb], writes=[bbb])
                for kk in range(2):
                    pr, prb, pi, pib, (rr, rrb), (ii, iib), (aa, aab), (bb, bbb), (hh, hhb) = P[kk]
                    uc, ucbuf = ucs[kk][0], ucs[kk][1]
                    fw.op("dve", lambda e: e.tensor_scalar(out=bb[:, 0:n], in0=bb[:, 0:n], scalar1=-1.0, scalar2=1.0, op0=ALU.mult, op1=ALU.add), reads=[bbb], writes=[bbb])
                    fw.op("dve", lambda e: e.tensor_scalar(out=bb[:, 0:n], in0=bb[:, 0:n], scalar1=0.0, scalar2=None, op0=ALU.max), reads=[bbb], writes=[bbb])
                    fw.op("pool", lambda e: e.tensor_tensor(out=ii[:, 0:n], in0=ii[:, 0:n], in1=uc[:, 0:n], op=ALU.mult), reads=[iib, ucbuf], writes=[iib])
                for kk in range(2):
                    pr, prb, pi, pib, (rr, rrb), (ii, iib), (aa, aab), (bb, bbb), (hh, hhb) = P[kk]
                    fw.op("act", lambda e: e.activation(out=bb[:, 0:n], in_=bb[:, 0:n], func=AF.Sqrt), reads=[bbb], writes=[bbb])
                for kk in range(2):
                    m = 2 * nb + kk
                    pr, prb, pi, pib, (rr, rrb), (ii, iib), (aa, aab), (bb, bbb), (hh, hhb) = P[kk]
                    fw.op("dve", lambda e: e.tensor_tensor(out=bb[:, 0:n], in0=bb[:, 0:n], in1=ii[:, 0:n], op=ALU.mult), reads=[bbb, iib], writes=[bbb])
                    if tt < 4:
                        init = 0.0 if tt == 0 else hl[:, m:m + 1]
                        fw.op("dve", lambda e: e.tensor_tensor_scan(out=hh[:, 0:n], data0=aa[:, 0:n], data1=bb[:, 0:n], initial=init, op0=ALU.mult, op1=ALU.add),
                              reads=[aab, bbb, hlb[m]], writes=[hhb])
                    else:
                        for s_ in range(4):
                            fw.op("dve", lambda e: e.tensor_tensor_scan(out=hh[:, 4 * s_:4 * s_ + 4], data0=aa[:, 4 * s_:4 * s_ + 4], data1=bb[:, 4 * s_:4 * s_ + 4],
                                                                        initial=h0[:, m, s_:s_ + 1], op0=ALU.mult, op1=ALU.add),
                                  reads=[aab, bbb, h0b], writes=[hhb])
                for kk in range(2):
                    m = 2 * nb + kk
                    pr, prb, pi, pib, (rr, rrb), (ii, iib), (aa, aab), (bb, bbb), (hh, hhb) = P[kk]
                    if tt < 4:
                        fw.op("act", lambda e: e.activation(out=hl[:, m:m + 1], in_=hh[:, n - 1:n], func=AF.Identity), reads=[hhb], writes=[hlb[m]])
                    else:
                        fw.op("act", lambda e: e.activation(out=hs[:, m, :], in_=hh[:, 0:n].rearrange("p (s t) -> p s t", s=4)[:, :, 3], func=AF.Identity),
                              reads=[hhb], writes=[hsb[m]])
                    fw.op("act", lambda e: e.activation(out=self.G[:, m, t0:t0 + n], in_=hh[:, 0:n], func=AF.Identity), reads=[hhb], writes=[self.gb[(m, tt)]])

            for nb in range(4):
                fw.op("pool", lambda e: e.memset(ext[:, :, 0:3], 0.0), writes=extb)
                front(nb, 0)
                for tt in range(5):
                    if tt + 1 < 5:
                        front(nb, tt + 1)
                    back(nb, tt)
            for m in range(8):
                fw.dma("sp", self.d_lhs[:, m * 128:(m + 1) * 128].rearrange("s p -> p s"), hs[:, m, :], reads=[hsb[m]], is_output=True, allow_slow_non_contiguous=True)
                fw.dma("sp", self.d_lhp[m * 128:(m + 1) * 128].rearrange("(p o) -> p o", o=1), hl[:, m:m + 1], reads=[hlb[m]], is_output=True, allow_slow_non_contiguous=True)
            for m in range(8):
                wz, wzb = self.wtile(f"c_z{m // 4}")
                for tt in range(5):
                    t0, n = TT[tt]
                    pz, pzb = self.ps()
                    for k in range(8):
                        fw.op("pe", lambda e: e.matmul(pz[:, 0:n], lhsT=wz[:, k, (m % 4) * 128:(m % 4) * 128 + 128], rhs=self.XH[:, k, t0:t0 + n],
                                                       start=(k == 0), stop=(k == 7)),
                              reads=[wzb, self.xb[(k, tt)]], writes=[pzb])
                    sz, szb = T_()
                    fw.op("act", lambda e: e.activation(out=sz[:, 0:n], in_=pz[:, 0:n], func=AF.Silu), reads=[pzb], writes=[szb])
                    gsl = self.G[:, m, t0:t0 + n]
                    fw.op("dve", lambda e: e.tensor_tensor(out=gsl, in0=gsl, in1=sz[:, 0:n], op=ALU.mult), reads=[szb, self.gb[(m, tt)]], writes=[self.gb[(m, tt)]])
            self.out_proj_ln(layer, ["c_out0", "c_out1"], False)
            fw.barrier()


    def b_tables(self, es, which):
        fw = self.fw
        t = {}
        def ld(name, dram, shape, dtype, q="pool"):
            tl = self.sb("tb_" + name, shape, dtype, es)
            b = Buf(name)
            fw.dma(q, tl[:], dram, writes=[b])
            t[name] = (tl, b)
        if which == "attn":
            ld("dw", self.d_tdw[:, :], [128, 2432], I16)
            ld("ww", self.d_tww[:, :], [128, 1408], I16)
            ld("dc", self.d_tdc[:, :], [64, 2048], I16)
            ld("e2", self.d_te2[:, :], [128, 2048], BF16)
            ld("selg", self.d_tselg[:, :], [48, 3072], BF16)
        else:
            ld("m12", self.d_tm12[:, :], [128, 2048], F32, q="sp")
        return t

    def layer_B(self, layer):
        fw = self.fw
        with contextlib.ExitStack() as es:
            es_p = contextlib.ExitStack()
            GTh = self.sb("GTh", [48, NTOK], BF16, es)
            KCf = self.sb("KCf", [128, 2, 64], F32, es)
            KC = self.sb("KC", [128, 2, 64], BF16, es)
            VCt = self.sb("VCt", [128, 64], F32, es)
            VC = self.sb("VC", [64, 128], BF16, es)
            kvs = [self.sb(f"kvs{i}", [64, 560], F32, es) for i in range(2)]
            KS = self.sb("KS", [128, 2, NTOK], BF16, es_p)
            KW = self.sb("KW", [128, 2, NTOK], BF16, es_p)
            ksb = {(g, tt): Buf() for g in range(2) for tt in range(5)}
            kwb = {(g, tt): Buf() for g in range(2) for tt in range(5)}
            gtb = [Buf() for _ in range(5)]
            VS = self.sb("VS", [128, 16, 128], BF16, es_p)
            VW = self.sb("VW", [128, 16, 128], BF16, es_p)
            vsb = [Buf() for _ in range(16)]
            NM = self.sb("NM", [128, 2, T], BF16, es_p)
            nmb = [Buf() for _ in range(16)]
            kcfb, kcb, vctb, vcb = Buf(), Buf(), Buf(), Buf()
            kvsb = [Buf() for _ in range(4)]
            self.B = dict(KS=KS, KW=KW, ksb=ksb, kwb=kwb, GTh=GTh, gtb=gtb, VS=VS, VW=VW, vsb=vsb, NM=NM, nmb=nmb,
                          KC=KC, kcb=kcb, VC=VC, vcb=vcb, kvs=kvs, kvsb=kvsb)
            with contextlib.ExitStack() as es1:
                tmp = [(self.sb(f"b_t{i}", [128, 512], F32, es1), Buf()) for i in range(4)]
                kvr = [(self.sb(f"b_kvr{i}", [128, 768], F32, es1), Buf()) for i in range(2)]
                ti = [0]

                def T_():
                    r = tmp[ti[0]]
                    ti[0] = (ti[0] + 1) % len(tmp)
                    return r
                for c in range(8):
                    w, wb = self.wtile(f"b_q{c // 4}")
                    for tt in range(5):
                        t0, n = TT[tt]
                        p_, pb_ = self.ps()
                        for k in range(8):
                            fw.op("pe", lambda e: e.matmul(p_[:, 0:n], lhsT=w[:, k, (c % 4) * 128:(c % 4) * 128 + 128], rhs=self.XH[:, k, t0:t0 + n],
                                                           start=(k == 0), stop=(k == 7)), reads=[wb, self.xb[(k, tt)]], writes=[pb_])
                        fw.op("act", lambda e: e.activation(out=self.G[:, c, t0:t0 + n], in_=p_[:, 0:n], func=AF.Identity, scale=0.125),
                              reads=[pb_], writes=[self.gb[(c, tt)]])
                wk = [self.vec("b_wk0"), self.vec("b_wk1"), self.vec("b_wv")]
                for wi, wname in enumerate(("b_kvA", "b_kvB")):
                    w, wb = self.wtile(wname)
                    for q in range(4):
                        for tt in range(5):
                            t0, n = TT[tt]
                            kind = wi * 4 + q
                            if kind <= 2 and tt == 4:
                                continue
                            p_, pb_ = self.ps()
                            for k in range(8):
                                fw.op("pe", lambda e: e.matmul(p_[:, 0:n], lhsT=w[:, k, q * 128:(q + 1) * 128], rhs=self.XH[:, k, t0:t0 + n],
                                                               start=(k == 0), stop=(k == 7)), reads=[wb, self.xb[(k, tt)]], writes=[pb_])
                            if kind <= 2:
                                tm, tmb = T_()
                                wv_ = wk[kind].rearrange("p (o j) -> p o j", o=1).to_broadcast([128, 16, 32])
                                fw.op("dve", lambda e: e.tensor_tensor(out=tm[:, :].rearrange("p (b j) -> p b j", j=32), in0=p_[:, :].rearrange("p (b j) -> p b j", j=32),
                                                                       in1=wv_, op=ALU.mult), reads=[pb_, self.b_vecs], writes=[tmb])
                                dst = KCf[:, kind, 16 * tt:16 * tt + 16] if kind < 2 else VCt[:, 16 * tt:16 * tt + 16]
                                fw.op("dve", lambda e: e.reduce_sum(out=dst, in_=tm[:, :].rearrange("p (b j) -> p b j", j=32), axis=AX.X),
                                      reads=[tmb], writes=[kcfb if kind < 2 else vctb])
                            elif kind <= 6:
                                g = (kind - 3) % 2
                                dstT, dstb = (KS, ksb) if kind <= 4 else (KW, kwb)
                                fw.op("act", lambda e: e.activation(out=dstT[:, g, t0:t0 + n], in_=p_[:, 0:n], func=AF.Identity), reads=[pb_], writes=[dstb[(g, tt)]])
                            else:
                                tm, tmb = T_()
                                fw.op("act", lambda e: e.activation(out=tm[0:48, 0:n], in_=p_[0:48, 0:n], func=AF.Sigmoid), reads=[pb_], writes=[tmb])
                                fw.op("act", lambda e: e.activation(out=GTh[:, t0:t0 + n], in_=tm[0:48, 0:n], func=AF.Identity), reads=[tmb], writes=[gtb[tt]])
                w1, w1b = self.wtile("b_kvR1", group="b_kvR1")
                w2, w2b = self.wtile("b_kvR2", group="b_kvR1")
                for i in range(16):
                    tt = i // 4
                    kv, kvb = kvr[i % 2]
                    p1, p1b = self.ps()
                    p2, p2b = self.ps()
                    for k in range(8):
                        fw.op("pe", lambda e: e.matmul(p1[:, :], lhsT=self.XH[:, k, i * 128:(i + 1) * 128], rhs=w1[:, k, :], start=(k == 0), stop=(k == 7)),
                              reads=[w1b, self.xb[(k, tt)]], writes=[p1b])
                    for k in range(8):
                        fw.op("pe", lambda e: e.matmul(p2[:, 0:256], lhsT=self.XH[:, k, i * 128:(i + 1) * 128], rhs=w2[:, k, 0:256], start=(k == 0), stop=(k == 7)),
                              reads=[w2b, self.xb[(k, tt)]], writes=[p2b])
                    fw.op("act", lambda e: e.activation(out=kv[:, 0:512], in_=p1[:, :], func=AF.Identity), reads=[p1b], writes=[kvb])
                    fw.op("dve", lambda e: e.tensor_copy(out=kv[:, 512:768], in_=p2[:, 0:256]), reads=[p2b], writes=[kvb])
                    fw.dma("sp", self.d_rowsp[i * 128:(i + 1) * 128, :], kv[:, 0:512], reads=[kvb], is_output=True)
                    if i >= 12:
                        fw.dma("sp", self.d_winp[(i - 12) * 128:(i - 11) * 128, :], kv[:, 512:768], reads=[kvb], is_output=True)
                    fw.op("pool", lambda e: e.tensor_copy(out=VS[:, i, :], in_=kv[:, 384:512]), reads=[kvb], writes=[vsb[i]])
                    fw.op("pool", lambda e: e.tensor_copy(out=VW[:, i, :], in_=kv[:, 640:768]), reads=[kvb], writes=[vsb[i]])
                for s_ in range(4):
                    p1, p1b = self.ps()
                    p2, p2b = self.ps()
                    c0 = T + 4 * s_
                    po = 32 * (s_ % 2)
                    kvt = kvs[s_ // 2]
                    kv, kvb = kvr[s_ % 2]
                    for k in range(8):
                        fw.op("pe", lambda e: e.matmul(p1[po:po + 4, :], lhsT=self.XH[:, k, c0:c0 + 4], rhs=w1[:, k, :], start=(k == 0), stop=(k == 7)),
                              reads=[w1b, self.xb[(k, 4)]], writes=[p1b])
                    for k in range(8):
                        fw.op("pe", lambda e: e.matmul(p2[po:po + 4, 0:304], lhsT=self.XH[:, k, c0:c0 + 4], rhs=w2[:, k, 0:304], start=(k == 0), stop=(k == 7)),
                              reads=[w2b, self.xb[(k, 4)]], writes=[p2b])
                    fw.op("act", lambda e: e.activation(out=kv[po:po + 4, 0:512], in_=p1[po:po + 4, :], func=AF.Identity), reads=[p1b], writes=[kvb])
                    fw.op("act", lambda e: e.activation(out=kvt[po:po + 4, 0:256], in_=p1[po:po + 4, 256:512], func=AF.Identity), reads=[p1b], writes=[kvsb[s_]])
                    fw.op("dve", lambda e: e.tensor_copy(out=kvt[po:po + 4, 256:560], in_=p2[po:po + 4, 0:304]), reads=[p2b], writes=[kvsb[s_]])
                    fw.dma("sp", self.d_rowss[s_, :, :], kv[po:po + 4, 0:512], reads=[kvb], is_output=True)
                    fw.dma("sp", self.d_wins[s_, 508:512, :], kvt[po:po + 4, 256:512], reads=[kvsb[s_]], is_output=True)
                    fw.dma("sp", self.d_wins[s_, 0:508, :], self.d_winbuf[s_, 4:512, :], is_output=True)
                fw.op("act", lambda e: e.activation(out=KC[:, :, :], in_=KCf[:, :, :], func=AF.Identity), reads=[kcfb], writes=[kcb])
                pv, pvb = self.ps()
                fw.op("pe", lambda e: e.transpose(pv[0:64, 0:128], VCt[:, 0:64], self.ident[:, :]), reads=[vctb, self.b_ident], writes=[pvb])
                fw.op("act", lambda e: e.activation(out=VC[:, :], in_=pv[0:64, 0:128], func=AF.Identity), reads=[pvb], writes=[vcb])
                fw.barrier()
            if "Bimp" in self.stages:
                self.b_importance(es)
            if "Battn" in self.stages:
                self.b_attention(es)
            fw.barrier()
            es_p.close()
            if "Bsamp" in self.stages:
                self.b_sample(es)
            with contextlib.ExitStack() as es3:
                self.resid_ln_begin(es3)
                tmp = [(self.sb(f"b_z{i}", [128, 512], F32, es3), Buf()) for i in range(3)]
                for m in range(8):
                    wz, wzb = self.wtile(f"b_z{m // 4}")
                    for tt in range(5):
                        t0, n = TT[tt]
                        pz, pzb = self.ps()
                        for k in range(8):
                            fw.op("pe", lambda e: e.matmul(pz[:, 0:n], lhsT=wz[:, k, (m % 4) * 128:(m % 4) * 128 + 128], rhs=self.XH[:, k, t0:t0 + n],
                                                           start=(k == 0), stop=(k == 7)), reads=[wzb, self.xb[(k, tt)]], writes=[pzb])
                        sz, szb = tmp[(m * 5 + tt) % 3]
                        fw.op("act", lambda e: e.activation(out=sz[:, 0:n], in_=pz[:, 0:n], func=AF.Silu), reads=[pzb], writes=[szb])
                        gsl = self.G[:, m, t0:t0 + n]
                        fw.op("dve", lambda e: e.tensor_tensor(out=gsl, in0=gsl, in1=sz[:, 0:n], op=ALU.mult), reads=[szb, self.gb[(m, tt)]], writes=[self.gb[(m, tt)]])
                fin = "final_after_B" in self.stages
                if fin:
                    self.yt = [(self.sb(f"yt{i}", [128, D], F32, es3), Buf()) for i in range(2)]
                    self.yti = 0
                self.out_proj_ln(layer, ["b_out0", "b_out1"], fin)
                fw.barrier()

    def b_importance(self, es):
        fw = self.fw
        B = self.B
        KC, kcb, NM, nmb = B["KC"], B["kcb"], B["NM"], B["nmb"]
        with contextlib.ExitStack() as es2:
            m12, m12b = self.b_tables(es2, "imp")["m12"]
            KCbd = self.sb("bi_kcbd", [128, 2, 128], BF16, es2)
            kcbdb = Buf()
            fw.op("pool", lambda e: e.memset(KCbd[:, :, :], 0.0), writes=[kcbdb])
            fw.op("act", lambda e: e.activation(out=KCbd[0:64, :, 0:64], in_=KC[0:64, :, :], func=AF.Identity), reads=[kcb], writes=[kcbdb])
            fw.op("act", lambda e: e.activation(out=KCbd[64:128, :, 64:128], in_=KC[64:128, :, :], func=AF.Identity), reads=[kcb], writes=[kcbdb])
            bc = [(self.sb(f"bi_bc{i}", [128, 1024], F32, es2), Buf()) for i in range(2)]
            E = [(self.sb(f"bi_E{i}", [128, 1024], F32, es2), Buf()) for i in range(2)]
            t1 = self.sb("bi_t1", [128, 512], F32, es2); t1b = Buf()
            sm = self.sb("bi_sm", [128, 16], F32, es2); smb = Buf()
            t2 = self.sb("bi_t2", [128, 128], F32, es2); t2b = Buf()
            imp = self.sb("bi_imp", [128, 64], F32, es2); impb = Buf()
            wk = self.sb("bi_wk", [128, 64], F32, es2); wkb = Buf()
            m8 = self.sb("bi_m8", [128, 16], F32, es2); m8b = Buf()
            nq = [(self.sb(f"bi_nq{i}", [128, 2, 128], F32, es2), Buf()) for i in range(2)]
            for nqt, nqb in nq:
                fw.op("pool", lambda e: e.memset(nqt[:, :, :], 0.0), writes=[nqb])
            for qt in range(16):
                bct, bcb = bc[qt % 2]
                Et, Eb = E[qt % 2]
                if KDBG <= -1:
                    continue
                fw.dma("sp", bct[:], self.d_biasc[qt], writes=[bcb])
                if KDBG <= 0:
                    continue
                pA, pAb = self.ps()
                pB, pBb = self.ps()
                for c in range(8):
                    pp, ppb = (pA, pAb) if c < 4 else (pB, pBb)
                    fw.op("pe", lambda e: e.matmul(pp[:, (c % 4) * 128:(c % 4) * 128 + 128], lhsT=self.G[:, c, qt * 128:(qt + 1) * 128],
                                                   rhs=KCbd[:, c // 4, :], start=True, stop=True),
                          reads=[self.gb[(c, qt // 4)], kcbdb], writes=[ppb])
                if not (KDBG == 1 and os.environ.get("KSUB") == "add2"):
                    fw.op("dve", lambda e: e.tensor_tensor(out=Et[:, 0:512], in0=pA[:, :], in1=bct[:, 0:512], op=ALU.add), reads=[pAb, bcb], writes=[Eb])
                if KDBG == 1 and os.environ.get("KSUB") == "add1":
                    continue
                fw.op("dve", lambda e: e.tensor_tensor(out=Et[:, 512:1024], in0=pB[:, :], in1=bct[:, 512:1024], op=ALU.add), reads=[pBb, bcb, Eb], writes=[Eb])
                if KDBG <= 1:
                    continue
                fw.op("act", lambda e: e.activation(out=Et[:, :], in_=Et[:, :], func=AF.Exp), reads=[Eb], writes=[Eb])
                E3 = Et[:, :].rearrange("p (h n) -> p h n", h=16)
                fw.op("dve", lambda e: e.reduce_sum(out=sm[:, :], in_=E3, axis=AX.X), reads=[Eb], writes=[smb])
                fw.op("dve", lambda e: e.tensor_scalar(out=sm[:, :], in0=sm[:, :], scalar1=1e-30, scalar2=None, op0=ALU.max), reads=[smb], writes=[smb])
                fw.op("dve", lambda e: e.reciprocal(out=sm[:, :], in_=sm[:, :]), reads=[smb], writes=[smb])
                fw.op("dve", lambda e: e.tensor_tensor(out=E3, in0=E3, in1=sm[:, :].rearrange("p (h o) -> p h o", o=1).to_broadcast([128, 16, 64]), op=ALU.mult),
                      reads=[Eb, smb], writes=[Eb])
                if KDBG <= 2:
                    continue
                fw.op("dve", lambda e: e.reduce_sum(out=t1[:, :], in_=Et[:, :].rearrange("p (x t) -> p x t", t=2), axis=AX.X), reads=[Eb], writes=[t1b])
                fw.op("dve", lambda e: e.reduce_sum(out=imp[:, :].rearrange("p (g b) -> p g b", g=2), in_=t1[:, :].rearrange("p (g h b) -> p g b h", g=2, h=8), axis=AX.X),
                      reads=[t1b], writes=[impb])
                if KDBG <= 3:
                    continue
                M1 = m12[:, qt * 128:qt * 128 + 64]
                M2 = m12[:, qt * 128 + 64:qt * 128 + 128]
                fw.op("dve", lambda e: e.tensor_tensor(out=imp[:, :], in0=imp[:, :], in1=M1, op=ALU.mult), reads=[impb, m12b], writes=[impb])
                fw.op("dve", lambda e: e.tensor_tensor(out=imp[:, :], in0=imp[:, :], in1=M2, op=ALU.add), reads=[impb, m12b], writes=[impb])
                if KDBG <= 4:
                    continue
                for g in range(2):
                    sl = slice(g * 32, g * 32 + 32)
                    ml = slice(g * 8, g * 8 + 8)
                    fw.op("dve", lambda e: e.max(out=m8[:, ml], in_=imp[:, sl]), reads=[impb], writes=[m8b])
                    fw.op("dve", lambda e: e.match_replace(out=wk[:, sl], in_to_replace=m8[:, ml], in_values=imp[:, sl], imm_value=-2.0), reads=[impb, m8b], writes=[wkb])
                    fw.op("dve", lambda e: e.max(out=m8[:, ml], in_=wk[:, sl]), reads=[wkb], writes=[m8b])
                    fw.op("dve", lambda e: e.match_replace(out=wk[:, sl], in_to_replace=m8[:, ml], in_values=wk[:, sl], imm_value=-2.0), reads=[wkb, m8b], writes=[wkb])
                if KDBG <= 5:
                    continue
                nqt, nqb = nq[qt % 2]
                for g in range(2):
                    fw.op("dve", lambda e: e.tensor_scalar(out=nqt[:, g, :].rearrange("p (a x) -> p a x", a=2)[:, :, 0:32],
                                                           in0=wk[:, g * 32:g * 32 + 32].rearrange("p (o x) -> p o x", o=1).to_broadcast([128, 2, 32]),
                                                           scalar1=-2.0, scalar2=-BIG, op0=ALU.not_equal, op1=ALU.mult), reads=[wkb], writes=[nqb])
                for g in range(2):
                    pt_, ptb_ = self.ps()
                    fw.op("pe", lambda e: e.transpose(pt_[:, 0:128], nqt[:, g, :], self.ident[:, :]), reads=[nqb, self.b_ident], writes=[ptb_])
                    fw.op("act", lambda e: e.activation(out=NM[:, g, qt * 128:(qt + 1) * 128], in_=pt_[:, 0:128], func=AF.Identity), reads=[ptb_], writes=[nmb[qt]])
            fw.barrier()

    def b_attention(self, es):
        fw = self.fw
        B = self.B
        KS, KW, ksb, kwb, VS, VW, vsb = B["KS"], B["KW"], B["ksb"], B["kwb"], B["VS"], B["VW"], B["vsb"]
        KC, kcb, VC, vcb, NM, nmb = B["KC"], B["kcb"], B["VC"], B["vcb"], B["NM"], B["nmb"]
        GTh, gtb = B["GTh"], B["gtb"]
        slopes = [2.0 ** (-8.0 * (h + 1) / 16.0) for h in range(16)]
        LOOK = 2
        with contextlib.ExitStack() as es2:
            tb = self.b_tables(es2, "attn")
            (dw, dwb), (ww, wwb), (dc, dcb), (e2, e2b), (selg, selgb) = (tb[k] for k in ("dw", "ww", "dc", "e2", "selg"))
            SB = [(self.sb(f"ba_sb{i}", [128, 512], F32, es2), Buf()) for i in range(2)]
            PT = [(self.sb(f"ba_pt{i}", [128, 512], BF16, es2), Buf()) for i in range(3)]
            R1 = [(self.sb(f"ba_r{i}", [128, 512], F32, es2), Buf()) for i in range(1)]
            TM = [(self.sb(f"ba_tm{i}", [128, 512], F32, es2), Buf()) for i in range(1)]
            ACC = [(self.sb(f"ba_acc{i}", [128, 512], F32, es2), Buf()) for i in range(1)]
            QZ = [[(self.sb(f"ba_qz{i}{hh}", [128, 512], BF16, es2), Buf()) for hh in range(2)] for i in range(2)]
            for i in range(2):
                for hh in range(2):
                    fw.op("pool", lambda e: e.memset(QZ[i][hh][0][:, :], 0.0), writes=[QZ[i][hh][1]])
            self.ps_rot = [0, 1, 2]
            items = []
            pair_i = 0
            br_i = 0
            for QT in range(4):
                for c in range(8):
                    for br in range(3):
                        for hh in range(2):
                            if br == 0:
                                chunks = [None]
                            elif br == 1:
                                chunks = list(range(0, 4 * QT + 4))
                            else:
                                chunks = list(range(max(0, 4 * QT - 4), 4 * QT + 4))
                            for ci, kc in enumerate(chunks):
                                items.append(dict(QT=QT, c=c, br=br, hh=hh, kc=kc, first=(ci == 0), last=(ci == len(chunks) - 1),
                                                  pair=pair_i, bri=br_i, new_pair=(br == 0 and hh == 0 and ci == 0),
                                                  end_br=(hh == 1 and ci == len(chunks) - 1)))
                        br_i += 1
                    pair_i += 1
            for n_, it in enumerate(items):
                it["pt"] = PT[n_ % 3]
                it["sb"] = SB[n_ % 2]

            def front(it):
                QT, c, br, hh, kc = it["QT"], it["c"], it["br"], it["hh"], it["kc"]
                q0 = QT * 512
                h = 2 * c + hh
                g = h // 8
                qz = QZ[it["pair"] % 2]
                if it["new_pair"]:
                    for h2 in range(2):
                        hs_ = slice(h2 * 64, h2 * 64 + 64)
                        fw.op("act", lambda e: e.activation(out=qz[h2][0][hs_, :], in_=self.G[hs_, c, q0:q0 + 512], func=AF.Identity),
                              reads=[self.gb[(c, QT)]], writes=[qz[h2][1]])
                qrhs, qzb = qz[hh][0][:, :], qz[hh][1]
                ps_, psb_ = self.ps()
                sbt, sbb = it["sb"]
                ptt, ptb = it["pt"]
                if br == 0:
                    nk = 64
                    fw.op("pe", lambda e: e.matmul(ps_[0:64, :], lhsT=KC[:, g, :], rhs=qrhs, start=True, stop=True), reads=[kcb, qzb], writes=[psb_])
                    dsl, dbuf = dc[0:64, q0:q0 + 512], dcb
                else:
                    nk = 128
                    Kt, Kb = (KS, ksb) if br == 1 else (KW, kwb)
                    fw.op("pe", lambda e: e.matmul(ps_[:, :], lhsT=Kt[:, g, kc * 128:(kc + 1) * 128], rhs=qrhs, start=True, stop=(br == 2)),
                          reads=[Kb[(g, kc // 4)], qzb], writes=[psb_])
                    if br == 1:
                        fw.op("pe", lambda e: e.matmul(ps_[:, :], lhsT=e2[:, kc * 128:(kc + 1) * 128], rhs=NM[:, g, q0:q0 + 512], start=False, stop=True),
                              reads=[e2b] + [nmb[QT * 4 + j] for j in range(4)], writes=[psb_])
                    off = 512 * QT - 128 * kc + 384
                    tbl, dbuf = (dw, dwb) if br == 1 else (ww, wwb)
                    dsl = tbl[:, off:off + 512]
                it["nk"] = nk
                fw.op("dve", lambda e: e.scalar_tensor_tensor(out=sbt[0:nk, :], in0=dsl, scalar=slopes[h], in1=ps_[0:nk, :], op0=ALU.mult, op1=ALU.add),
                      reads=[dbuf, psb_], writes=[sbb])
                fw.op("act", lambda e: e.activation(out=ptt[0:nk, :], in_=sbt[0:nk, :], func=AF.Exp), reads=[sbb], writes=[ptb])

            def back(it):
                QT, c, br, hh, kc = it["QT"], it["c"], it["br"], it["hh"], it["kc"]
                q0 = QT * 512
                g = (2 * c + hh) // 8
                hs = slice(hh * 64, hh * 64 + 64)
                oi = it["bri"] % 2
                psO, psOb = self.psum[3 + 3 * oi], self.psb[3 + 3 * oi]
                psD, psDb = self.psum[4 + 3 * oi], self.psb[4 + 3 * oi]
                ptt, ptb = it["pt"]
                nk = it["nk"]
                if br == 0:
                    vl, vbuf = VC[0:64, g * 64:g * 64 + 64], vcb
                else:
                    vl, vbuf = (VS if br == 1 else VW)[:, kc, g * 64:g * 64 + 64], vsb[kc]
                fw.op("pe", lambda e: e.matmul(psO[hs, :], lhsT=vl, rhs=ptt[0:nk, :], start=it["first"], stop=it["last"]), reads=[vbuf, ptb], writes=[psOb])
                fw.op("pe", lambda e: e.matmul(psD[hs, :], lhsT=self.ones[0:nk, 0:64], rhs=ptt[0:nk, :], start=it["first"], stop=it["last"]),
                      reads=[self.b_ones, ptb], writes=[psDb])
                if not it["end_br"]:
                    return
                acc, accb = ACC[0]
                psG, psGb = self.psum[5], self.psb[5]
                so = (br * 8 + c) * 128
                fw.op("pe", lambda e: e.matmul(psG[:, :], lhsT=selg[0:48, so:so + 128], rhs=GTh[0:48, q0:q0 + 512], start=True, stop=True),
                      reads=[selgb, gtb[QT]], writes=[psGb])
                r1, r1b = R1[0]
                fw.op("dve", lambda e: e.tensor_scalar(out=r1[:, :], in0=psD[:, :], scalar1=1e-30, scalar2=None, op0=ALU.max), reads=[psDb], writes=[r1b])
                fw.op("dve", lambda e: e.reciprocal(out=r1[:, :], in_=r1[:, :]), reads=[r1b], writes=[r1b])
                fw.op("dve", lambda e: e.tensor_tensor(out=r1[:, :], in0=psG[:, :], in1=r1[:, :], op=ALU.mult), reads=[psGb, r1b], writes=[r1b])
                if br == 0:
                    fw.op("dve", lambda e: e.tensor_tensor(out=acc[:, :], in0=psO[:, :], in1=r1[:, :], op=ALU.mult), reads=[psOb, r1b], writes=[accb])
                else:
                    tm, tmb = TM[0]
                    fw.op("dve", lambda e: e.tensor_tensor(out=tm[:, :], in0=psO[:, :], in1=r1[:, :], op=ALU.mult), reads=[psOb, r1b], writes=[tmb])
                    fw.op("dve", lambda e: e.tensor_tensor(out=acc[:, :], in0=acc[:, :], in1=tm[:, :], op=ALU.add), reads=[accb, tmb], writes=[accb])
                if br == 2:
                    fw.op("act", lambda e: e.activation(out=self.G[:, c, q0:q0 + 512], in_=acc[:, :], func=AF.Identity), reads=[accb], writes=[self.gb[(c, QT)]])

            do_pre = "Bsamp" in self.stages
            if do_pre:
                gidx = self.sb("ba_gidx", [128, 256], I32, es2); gidxb = Buf()
                stg = [(self.sb(f"ba_stg{i}", [128, 256], F32, es2), Buf()) for i in range(3)]
                gio = self.sb("ba_gio", [128, 1], F32, es2); giob = Buf()
                pti_ = stg[0][0][:, :].bitcast(I32)
                fw.dma("sp", pti_, self.d_ptab.rearrange("s p -> (s p)").rearrange("(o n) -> o n", o=1).to_broadcast([128, 256]), writes=[stg[0][1]])
                fw.op("pool", lambda e: e.iota(gio[:], pattern=[[0, 1]], base=0, channel_multiplier=1, allow_small_or_imprecise_dtypes=True), writes=[giob])
                fw.op("dve", lambda e: e.tensor_copy(out=stg[1][0][:, :], in_=pti_), reads=[stg[0][1]], writes=[stg[1][1]])
                fw.op("dve", lambda e: e.tensor_scalar(out=stg[1][0][:, :], in0=stg[1][0][:, :], scalar1=128.0, scalar2=gio[:, 0:1], op0=ALU.mult, op1=ALU.add),
                      reads=[stg[1][1], giob], writes=[stg[1][1]])
                fw.op("dve", lambda e: e.tensor_scalar(out=gidx[:, :], in0=stg[1][0][:, :], scalar1=2.0, scalar2=1.0, op0=ALU.mult, op1=ALU.add),
                      reads=[stg[1][1]], writes=[gidxb])
            gn = [0]

            def pregather():
                n = gn[0]
                if not do_pre or n >= 256:
                    return
                gn[0] += 1
                st_, stb_ = stg[n % 3]
                fw.dma("pool", None, None, reads=[gidxb], writes=[stb_],
                       fn=lambda e: e.indirect_dma_start(out=st_[:, :], out_offset=None, in_=self.d_kvpool[:, :],
                                                         in_offset=bass.IndirectOffsetOnAxis(ap=gidx[:, n:n + 1], axis=0)))
                fw.dma("sp", self.d_scr[n], st_[:, :], reads=[stb_], writes=[self.scrb[n]])

            N = len(items)
            for i in range(N + LOOK):
                if i < N:
                    front(items[i])
                if i >= LOOK:
                    back(items[i - LOOK])
                if i % 4 == 1:
                    pregather()
            while do_pre and gn[0] < 256:
                pregather()
            self.ps_rot = list(range(8))
            fw.barrier()

    def b_sample(self, es):
        fw = self.fw
        B = self.B
        kvs, kvsb, GTh, gtb = B["kvs"], B["kvsb"], B["GTh"], B["gtb"]
        with contextlib.ExitStack() as es2:
            def ld(name, dram, shape, dtype, q="pool"):
                tl = self.sb("ts_" + name, shape, dtype, es2)
                b = Buf(name)
                fw.dma(q, tl[:], dram, writes=[b])
                return tl, b
            EE, EEb = ld("ee", self.d_tee[:, :], [128, 8192], BF16)
            T9, T9b = ld("b9", self.d_tb9[:, :], [128, 71 * 64], BF16)
            SH, SHb = ld("shift", self.d_tshift[:, :], [128, 512], BF16)
            A9, A9b = ld("a9", self.d_ta9[:, :], [128, 128], BF16)
            SW, SWb = ld("selw", self.d_tselw[:, :], [128, 256], BF16)
            B2, B2b = ld("b2", self.d_tb2[:, :], [128, 64], F32, q="sp")
            WT, WTb = ld("wt", self.d_twt[:, :], [128, 256], F32, q="sp")
            selg, selgb = ld("selg", self.d_tselg[:, :], [48, 3072], BF16)
            T9v = T9[:, :].rearrange("p (x c) -> p x c", c=64)
            QS = self.sb("QS", [128, 4, 2, 32], BF16, es2); QSb = Buf()
            idx = self.sb("idx", [128, 2, 256], I32, es2); idxb = Buf()
            pti = self.sb("pti", [128, 256], I32, es2)
            ptf = self.sb("ptf", [128, 256], F32, es2)
            io = self.sb("io", [128, 1], F32, es2)
            ptb_, iob = Buf(), Buf()
            PG = [(self.sb(f"PG{i}", [128, 256], F32, es2), Buf()) for i in range(6)]
            PRD = [(self.sb(f"PRD{i}", [128, 256], BF16, es2), Buf()) for i in range(3)]
            KT = [(self.sb(f"KTs{i}", [128, 128], BF16, es2), Buf()) for i in range(4)]
            VB = [(self.sb(f"VBs{i}", [128, 128], BF16, es2), Buf()) for i in range(4)]
            PTs = [(self.sb(f"PTs{i}", [128, 32], BF16, es2), Buf()) for i in range(10)]
            PTN = [(self.sb(f"PTn{i}", [128, 32], BF16, es2), Buf()) for i in range(4)]
            VNs = [(self.sb(f"VN{i}", [128, 128], BF16, es2), Buf()) for i in range(2)]
            NEWf = self.sb("NEWf", [4, 512], F32, es2); NEWfb = Buf()
            KN = self.sb("KN", [128, 8], BF16, es2); KNb = Buf()
            kcT = self.sb("kcT", [128, 256], BF16, es2); kcTb = Buf()
            vc = self.sb("vcs", [128, 2, 128], BF16, es2); vcb_ = Buf()
            ef = self.sb("ef", [128, 2, 32], F32, es2); efb = Buf()
            pns = self.sb("pns", [128, 2, 4], F32, es2); pnsb = Buf()
            rd = self.sb("rd", [128, 32], F32, es2); rdb = Buf()
            imps = self.sb("imps", [4, 128], F32, es2); impsb = Buf()
            IMP = self.sb("IMP", [32, 136], F32, es2); IMPb = Buf()
            WK = self.sb("WKs", [32, 136], F32, es2); WKb = Buf()
            M8 = self.sb("M8s", [32, 8], F32, es2); M8b = Buf()
            NQ = self.sb("NQs", [32, 128], F32, es2); NQb = Buf()
            NMs = self.sb("NMs", [128, 32], BF16, es2); NMsb = Buf()
            NMr = self.sb("NMr", [128, 8, 8, 4], BF16, es2); NMrb = Buf()
            cr = [(self.sb(f"cr{i}", [128, 16], F32, es2), Buf()) for i in range(2)]
            cacc = self.sb("cacc", [128, 16], F32, es2); caccb = Buf()
            fw.dma("sp", pti[:], self.d_ptab.rearrange("s p -> (s p)").rearrange("(o n) -> o n", o=1).to_broadcast([128, 256]), writes=[ptb_])
            fw.op("pool", lambda e: e.iota(io[:], pattern=[[0, 1]], base=0, channel_multiplier=1, allow_small_or_imprecise_dtypes=True), writes=[iob])
            fw.op("dve", lambda e: e.tensor_copy(out=ptf[:], in_=pti[:]), reads=[ptb_], writes=[ptb_])
            fw.op("dve", lambda e: e.tensor_scalar(out=ptf[:], in0=ptf[:], scalar1=128.0, scalar2=io[:, 0:1], op0=ALU.mult, op1=ALU.add), reads=[ptb_, iob], writes=[ptb_])
            fw.op("dve", lambda e: e.tensor_scalar(out=idx[:, 0, :], in0=ptf[:], scalar1=2.0, scalar2=None, op0=ALU.mult), reads=[ptb_], writes=[idxb])
            fw.op("dve", lambda e: e.tensor_scalar(out=idx[:, 1, :], in0=ptf[:], scalar1=2.0, scalar2=1.0, op0=ALU.mult, op1=ALU.add), reads=[ptb_], writes=[idxb])
            for t_, b_ in PTN + VNs:
                fw.op("pool", lambda e: e.memset(t_[:, :], 0.0), writes=[b_])
            fw.op("pool", lambda e: e.memset(IMP[:, :], 0.0), writes=[IMPb])
            pq, pqb = self.ps()
            for h in range(16):
                c, hh, g = h // 2, h % 2, h // 8
                fw.op("pe", lambda e: e.matmul(pq[:, h * 16:(h + 1) * 16], lhsT=SH[:, (hh * 2 + g) * 128:(hh * 2 + g) * 128 + 128], rhs=self.G[:, c, T:T + 16],
                                               start=True, stop=True), reads=[SHb, self.gb[(c, 4)]], writes=[pqb])
            for g in range(2):
                fw.op("act", lambda e: e.activation(out=QS[:, :, g, :].rearrange("p s (h i) -> p s h i", h=8),
                                                    in_=pq[:, g * 128:(g + 1) * 128].rearrange("p (h s i) -> p s h i", h=8, s=4), func=AF.Identity),
                      reads=[pqb], writes=[QSb])
            cnt = [0, 0, 0]

            def gather(s_, p, half):
                pg, pgb = PG[cnt[0] % len(PG)]
                cnt[0] += 1
                col = s_ * 64 + p
                fw.dma("pool", None, None, reads=[idxb], writes=[pgb],
                       fn=lambda e: e.indirect_dma_start(out=pg[:, :], out_offset=None, in_=self.d_kvpool[:, :],
                                                         in_offset=bass.IndirectOffsetOnAxis(ap=idx[:, half, col:col + 1], axis=0)))
                return pg, pgb

            def scores(lhsT, lb, nk, s_, g, x, pst, pstb):
                fw.op("pe", lambda e: e.matmul(pst[0:nk, 0:32], lhsT=lhsT, rhs=QS[:, s_, g, :], start=True, stop=False), reads=[lb, QSb], writes=[pstb])
                fw.op("pe", lambda e: e.matmul(pst[0:nk, 0:32], lhsT=A9[:, 0:nk], rhs=T9v[:, x, g * 32:(g + 1) * 32], start=False, stop=True),
                      reads=[A9b, T9b], writes=[pstb])

            def pv(ptile, ptileb, nk, vl, vlb, g, col0, first, last):
                pO, pOb, pD, pDb = self.psum[4 + g], self.psb[4 + g], self.psum[6 + g], self.psb[6 + g]
                for par in range(2):
                    rhs = ptile[0:nk, 0:32].rearrange("p (j r i) -> p j r i", j=4, r=2)[:, :, par, :]
                    hs = slice(par * 64, par * 64 + 64)
                    fw.op("pe", lambda e: e.matmul(pO[hs, col0:col0 + 16], lhsT=vl, rhs=rhs, start=first, stop=last), reads=[vlb, ptileb], writes=[pOb])
                    fw.op("pe", lambda e: e.matmul(pD[hs, col0:col0 + 16], lhsT=self.ones[0:nk, 0:64], rhs=rhs, start=first, stop=last), reads=[self.b_ones, ptileb], writes=[pDb])
            self.ps_rot = [0, 1, 2, 3]
            for s_ in range(4):
                pkc, pkcb = self.ps()
                pvc, pvcb = self.ps()
                for p in range(64):
                    pg, pgb = gather(s_, p, 0)
                    prd, prdb = PRD[p % 3]
                    fw.op("dve", lambda e: e.tensor_tensor(out=prd[:, :], in0=pg[:, :], in1=WT[:, :], op=ALU.mult), reads=[pgb, WTb], writes=[prdb])
                    fw.op("pe", lambda e: e.matmul(pkc[:, 4 * p:4 * p + 4], lhsT=prd[:, 0:128], rhs=SW[:, 124:128], start=True, stop=True), reads=[prdb, SWb], writes=[pkcb])
                    hf, pp = p // 32, p % 32
                    fw.op("pe", lambda e: e.matmul(pvc[:, hf * 128:(hf + 1) * 128], lhsT=SW[:, 124 - 4 * pp:124 - 4 * pp + 128], rhs=prd[:, 128:256],
                                                   start=(pp == 0), stop=(pp == 31)), reads=[prdb, SWb], writes=[pvcb])
                fw.op("act", lambda e: e.activation(out=kcT[:, :], in_=pkc[:, 0:256], func=AF.Identity), reads=[pkcb], writes=[kcTb])
                fw.op("act", lambda e: e.activation(out=vc[:, :, :], in_=pvc[:, 0:256].rearrange("p (a b) -> p a b", a=2), func=AF.Identity), reads=[pvcb], writes=[vcb_])
                for g in range(2):
                    col0 = ((s_ * 3 + 0) * 16)
                    pden, pdenb = self.ps()
                    for nc_ in range(2):
                        pst, pstb = self.ps()
                        scores(kcT[:, nc_ * 128:(nc_ + 1) * 128], kcTb, 128, s_, g, 68 + nc_, pst, pstb)
                        ptile, ptileb = PTs[cnt[1] % len(PTs)]
                        cnt[1] += 1
                        fw.op("act", lambda e: e.activation(out=ef[:, nc_, :], in_=pst[:, 0:32], func=AF.Exp), reads=[pstb], writes=[efb])
                        fw.op("act", lambda e: e.activation(out=ptile[:, :], in_=ef[:, nc_, :], func=AF.Identity), reads=[efb], writes=[ptileb])
                        pv(ptile, ptileb, 128, vc[:, nc_, g * 64:(g + 1) * 64], vcb_, g, col0, nc_ == 0, nc_ == 1)
                        fw.op("pe", lambda e: e.matmul(pden[:, 0:32], lhsT=self.ones[:, :], rhs=ptile[:, :], start=(nc_ == 0), stop=(nc_ == 1)),
                              reads=[self.b_ones, ptileb], writes=[pdenb])
                    fw.op("dve", lambda e: e.reciprocal(out=rd[:, :], in_=pden[:, 0:32]), reads=[pdenb], writes=[rdb])
                    fw.op("dve", lambda e: e.tensor_tensor(out=ef[:, :, :], in0=ef[:, :, :], in1=rd[:, :].rearrange("p (o c) -> p o c", o=1).to_broadcast([128, 2, 32]), op=ALU.mult),
                          reads=[efb, rdb], writes=[efb])
                    fw.op("dve", lambda e: e.reduce_sum(out=pns[:, :, :], in_=ef[:, :, :].rearrange("p a (h i) -> p a i h", h=8), axis=AX.X), reads=[efb], writes=[pnsb])
                    pim, pimb = self.ps()
                    for nc_ in range(2):
                        fw.op("pe", lambda e: e.matmul(pim[0:4, nc_ * 64:(nc_ + 1) * 64], lhsT=pns[:, nc_, :], rhs=B2[:, :], start=True, stop=True),
                              reads=[pnsb, B2b], writes=[pimb])
                    fw.op("act", lambda e: e.activation(out=imps[:, :], in_=pim[0:4, 0:128], func=AF.Identity), reads=[pimb], writes=[impsb])
                    j4 = 4 * (2 * s_ + g)
                    fw.dma("sp", IMP[j4:j4 + 4, 0:128], imps[:, :], reads=[impsb], writes=[IMPb])
            for col in (0, 127, 128):
                fw.op("dve", lambda e: e.memset(IMP[:, col:col + 1], 1e9), writes=[IMPb])
            fw.op("dve", lambda e: e.memset(IMP[:, 129:136], -1.0), writes=[IMPb])
            fw.op("dve", lambda e: e.max(out=M8[:, :], in_=IMP[:, :]), reads=[IMPb], writes=[M8b])
            fw.op("dve", lambda e: e.match_replace(out=WK[:, :], in_to_replace=M8[:, :], in_values=IMP[:, :], imm_value=-2.0), reads=[IMPb, M8b], writes=[WKb])
            fw.op("dve", lambda e: e.max(out=M8[:, :], in_=WK[:, :]), reads=[WKb], writes=[M8b])
            fw.op("dve", lambda e: e.match_replace(out=WK[:, :], in_to_replace=M8[:, :], in_values=WK[:, :], imm_value=-2.0), reads=[WKb, M8b], writes=[WKb])
            fw.op("dve", lambda e: e.tensor_scalar(out=NQ[:, :], in0=WK[:, 0:128], scalar1=-2.0, scalar2=-BIG, op0=ALU.not_equal, op1=ALU.mult), reads=[WKb], writes=[NQb])
            pt_, ptb2 = self.ps()
            fw.op("pe", lambda e: e.transpose(pt_[:, 0:32], NQ[:, :], self.ident[0:32, 0:32]), reads=[NQb, self.b_ident], writes=[ptb2])
            fw.op("act", lambda e: e.activation(out=NMs[:, :], in_=pt_[:, 0:32], func=AF.Identity), reads=[ptb2], writes=[NMsb])
            fw.op("dve", lambda e: e.tensor_copy(out=NMr[:, :, :, :], in_=NMs[:, :].rearrange("p (a o i) -> p a o i", a=8, o=1).to_broadcast([128, 8, 8, 4])),
                  reads=[NMsb], writes=[NMrb])
            items = []
            for s_ in range(4):
                for br in (1, 2):
                    if br == 1:
                        chunks = [("page", p) for p in range(64)] + [("new", 0)]
                    else:
                        chunks = [("win", c) for c in range(4)] + [("new", 1)]
                    for ci, (kind, j) in enumerate(chunks):
                        items.append(dict(s=s_, br=br, kind=kind, j=j, first=(ci == 0), last=(ci == len(chunks) - 1), seq_start=(br == 1 and ci == 0)))
            nbuf = 4
            for n_, it in enumerate(items):
                it["kt"], it["vb"] = KT[n_ % nbuf], VB[n_ % nbuf]
                it["pts"] = [PTs[(2 * n_ + g) % len(PTs)] for g in range(2)]
                it["ptn"] = [PTN[(2 * n_ + g) % len(PTN)] for g in range(2)]
                it["vn"] = VNs[n_ % 2]

            def front2(it):
                s_, br, kind, j = it["s"], it["br"], it["kind"], it["j"]
                po = 32 * (s_ % 2)
                kvt = kvs[s_ // 2]
                if it["seq_start"]:
                    pk, pkb = self.ps()
                    fw.dma("sp", NEWf[:, :], kvt[po:po + 4, 0:512], reads=[kvsb[s_]], writes=[NEWfb])
                    fw.op("pe", lambda e: e.transpose(pk[:, 0:4], NEWf[:, 0:128], self.ident[0:4, 0:4]), reads=[NEWfb, self.b_ident], writes=[pkb])
                    fw.op("pe", lambda e: e.transpose(pk[:, 4:8], NEWf[:, 256:384], self.ident[0:4, 0:4]), reads=[NEWfb, self.b_ident], writes=[pkb])
                    fw.op("act", lambda e: e.activation(out=KN[:, :], in_=pk[:, 0:8], func=AF.Identity), reads=[pkb], writes=[KNb])
                if kind == "new":
                    voff = 128 if br == 1 else 384
                    vn, vnb = it["vn"]
                    fw.op("act", lambda e: e.activation(out=vn[0:4, :], in_=NEWf[:, voff:voff + 128], func=AF.Identity), reads=[NEWfb], writes=[vnb])
                    for g in range(2):
                        pst, pstb = self.ps()
                        scores(KN[:, 4 * j:4 * j + 4], KNb, 4, s_, g, 70, pst, pstb)
                        ptn, ptnb = it["ptn"][g]
                        fw.op("act", lambda e: e.activation(out=ptn[0:4, :], in_=pst[0:4, 0:32], func=AF.Exp), reads=[pstb], writes=[ptnb])
                    return
                if kind == "page":
                    pg, pgb = PG[cnt[0] % len(PG)]
                    cnt[0] += 1
                    fw.dma("sp", pg[:, :], self.d_scr[s_ * 64 + j], reads=[self.scrb[s_ * 64 + j]], writes=[pgb])
                    x = j
                else:
                    pg, pgb = PG[cnt[0] % len(PG)]
                    cnt[0] += 1
                    fw.dma("sp", pg[:, :], self.d_winbuf[s_, j * 128:(j + 1) * 128, :], writes=[pgb])
                    x = 64 + j
                kt, ktb = it["kt"]
                vb, vbb = it["vb"]
                ptp, ptpb = self.ps()
                fw.op("pe", lambda e: e.transpose(ptp[:, 0:128], pg[:, 0:128], self.ident[:, :]), reads=[pgb, self.b_ident], writes=[ptpb])
                fw.op("act", lambda e: e.activation(out=kt[:, :], in_=ptp[:, 0:128], func=AF.Identity), reads=[ptpb], writes=[ktb])
                fw.op("dve", lambda e: e.tensor_copy(out=vb[:, :], in_=pg[:, 128:256]), reads=[pgb], writes=[vbb])
                for g in range(2):
                    pst, pstb = self.ps()
                    fw.op("pe", lambda e: e.matmul(pst[:, 0:32], lhsT=kt[:, :], rhs=QS[:, s_, g, :], start=True, stop=False), reads=[ktb, QSb], writes=[pstb])
                    if kind == "page":
                        fw.op("pe", lambda e: e.matmul(pst[:, 0:32], lhsT=EE[:, j * 128:(j + 1) * 128], rhs=NMr[:, 2 * s_ + g, :, :], start=False, stop=False),
                              reads=[EEb, NMrb], writes=[pstb])
                    fw.op("pe", lambda e: e.matmul(pst[:, 0:32], lhsT=A9[:, :], rhs=T9v[:, x, g * 32:(g + 1) * 32], start=False, stop=True),
                          reads=[A9b, T9b], writes=[pstb])
                    ptile, ptileb = it["pts"][g]
                    fw.op("act", lambda e: e.activation(out=ptile[:, :], in_=pst[:, 0:32], func=AF.Exp), reads=[pstb], writes=[ptileb])

            def back2(it):
                s_, br, kind = it["s"], it["br"], it["kind"]
                col0 = ((s_ * 3 + br) * 16)
                for g in range(2):
                    if kind == "new":
                        ptn, ptnb = it["ptn"][g]
                        vn, vnb = it["vn"]
                        pv(ptn, ptnb, 128, vn[:, g * 64:(g + 1) * 64], vnb, g, col0, it["first"], it["last"])
                    else:
                        ptile, ptileb = it["pts"][g]
                        vb, vbb = it["vb"]
                        pv(ptile, ptileb, 128, vb[:, g * 64:(g + 1) * 64], vbb, g, col0, it["first"], it["last"])

            LOOK2 = 3
            N2 = len(items)
            for i in range(N2 + LOOK2):
                if i < N2:
                    front2(items[i])
                if i >= LOOK2:
                    back2(items[i - LOOK2])
            for s_ in range(4):
                for g in range(2):
                    pO, pOb, pD, pDb = self.psum[4 + g], self.psb[4 + g], self.psum[6 + g], self.psb[6 + g]
                    for br in range(3):
                        col0 = ((s_ * 3 + br) * 16)
                        pgt, pgtb = self.ps()
                        for j in range(4):
                            so = (br * 8 + 4 * g + j) * 128
                            fw.op("pe", lambda e: e.matmul(pgt[:, j * 4:(j + 1) * 4], lhsT=selg[0:48, so:so + 128], rhs=GTh[0:48, T + 4 * s_:T + 4 * s_ + 4], start=True, stop=True),
                                  reads=[selgb, gtb[4]], writes=[pgtb])
                        r1, r1b = cr[(br) % 2]
                        fw.op("dve", lambda e: e.tensor_scalar(out=r1[:, :], in0=pD[:, col0:col0 + 16], scalar1=1e-30, scalar2=None, op0=ALU.max), reads=[pDb], writes=[r1b])
                        fw.op("dve", lambda e: e.reciprocal(out=r1[:, :], in_=r1[:, :]), reads=[r1b], writes=[r1b])
                        fw.op("dve", lambda e: e.tensor_tensor(out=r1[:, :], in0=pgt[:, 0:16], in1=r1[:, :], op=ALU.mult), reads=[pgtb, r1b], writes=[r1b])
                        fw.op("dve", lambda e: e.tensor_tensor(out=r1[:, :], in0=pO[:, col0:col0 + 16], in1=r1[:, :], op=ALU.mult), reads=[pOb, r1b], writes=[r1b])
                        if br == 0:
                            fw.op("dve", lambda e: e.tensor_copy(out=cacc[:, :], in_=r1[:, :]), reads=[r1b], writes=[caccb])
                        else:
                            fw.op("dve", lambda e: e.tensor_tensor(out=cacc[:, :], in0=cacc[:, :], in1=r1[:, :], op=ALU.add), reads=[r1b, caccb], writes=[caccb])
                    fw.op("act", lambda e: e.activation(out=self.G[:, 4 * g:4 * g + 4, T + 4 * s_:T + 4 * s_ + 4], in_=cacc[:, :].rearrange("p (j i) -> p j i", j=4), func=AF.Identity),
                          reads=[caccb], writes=[self.gb[(4 * g + j, 4)] for j in range(4)])
            self.ps_rot = list(range(8))
            fw.barrier()


def build_program(widx, ntiles, nvec, stages):
    nc = bass.Bass("TRN2", target_bir_lowering=False)
    with contextlib.ExitStack() as es:
        K = Kern(nc, es, widx, ntiles, nvec, stages)
        K.setup()
        with contextlib.ExitStack() as es2:
            K.load_x(es2)
            K.fw.barrier()
        if "A0" in stages:
            K.layer_A(0, 0, final=("final_after_A0" in stages))
        if "B" in stages:
            K.layer_B(1)
        if "C" in stages:
            K.layer_C(2)
        if "A1" in stages:
            K.layer_A(3, 1, final=True)
        K.fw.finish()
        print("instructions:", K.fw.ninst, {k: v for k, v in K.fw.cnt.items()}, K.fw.dcnt)
    return nc


def _b_consts():
    NEG = -32000.0
    kl = np.arange(128)[:, None]
    j = np.arange(2432)[None, :]
    d = (kl - (j - 384)).astype(np.float32)
    tdw = np.where(d <= 0, d, NEG).astype(np.float32)
    j = np.arange(1408)[None, :]
    d = (kl - (j - 384)).astype(np.float32)
    tww = np.where((d <= 0) & (d > -512), d, NEG).astype(np.float32)
    n = np.arange(64)[:, None]
    t = np.arange(2048)[None, :]
    d = (32 * n + 31 - t).astype(np.float32)
    tdc = np.where(d <= 0, d, NEG).astype(np.float32)
    blk = np.arange(32)[:, None]
    pos = np.arange(2048)[None, :]
    e = (blk == pos // 64).astype(np.float32)
    z = np.zeros_like(e)
    te2 = np.concatenate([e, z, e, z], axis=0)
    tselg = np.zeros((48, 3, 8, 128), np.float32)
    for b in range(3):
        for c in range(8):
            for m in range(128):
                tselg[b * 16 + 2 * c + m // 64, b, c, m] = 1.0
    tselg = tselg.reshape(48, 3072)
    tq = np.arange(2048)
    cur = (tq // 64)[:, None]
    b32 = np.arange(32)[None, :]
    forced = (b32 == 0) | ((b32 <= cur) & (b32 > cur - 2))
    M1 = ((b32 <= cur) & ~forced).astype(np.float32)
    M2 = np.where(forced, 1e9, np.where(b32 <= cur, 0.0, -1.0)).astype(np.float32)
    m12 = np.zeros((128, 16, 2, 2, 32), np.float32)
    for qt in range(16):
        for g in range(2):
            m12[:, qt, 0, g, :] = M1[qt * 128:(qt + 1) * 128]
            m12[:, qt, 1, g, :] = M2[qt * 128:(qt + 1) * 128]
    tm12 = m12.reshape(128, 2048)
    slopes = 2.0 ** (-8.0 * np.arange(1, 17) / 16.0)
    cend = 32 * np.arange(64) + 31
    dd = (cend[None, :] - tq[:, None]).astype(np.float64)
    hord = np.array([0, 2, 4, 6, 8, 10, 12, 14, 1, 3, 5, 7, 9, 11, 13, 15])
    bias = np.where(dd[:, None, :] <= 0, slopes[None, :, None] * dd[:, None, :], -30000.0)
    biasc = bias.reshape(16, 128, 1024).astype(np.float32)
    return dict(tdw=tdw, tww=tww, tdc=tdc, te2=te2, tselg=tselg, tm12=tm12, biasc=np.ascontiguousarray(biasc))


def _bf16_pieces(x):
    import ml_dtypes
    x = np.asarray(x, np.float64)
    p1 = x.astype(ml_dtypes.bfloat16).astype(np.float64)
    p2 = (x - p1).astype(ml_dtypes.bfloat16).astype(np.float64)
    p3 = (x - p1 - p2).astype(ml_dtypes.bfloat16).astype(np.float64)
    return np.stack([p1, p2, p3]).astype(np.float32)


def _s_consts():
    slopes = 2.0 ** (-8.0 * np.arange(1, 17) / 16.0)
    blk = np.arange(128)[:, None]
    pos = np.arange(8192)[None, :]
    tee = (blk == pos // 64).astype(np.float32)
    g_, h8_, i_ = np.meshgrid(np.arange(2), np.arange(8), np.arange(4), indexing="ij")
    sl = slopes[(8 * g_ + h8_)].reshape(64)
    ii = i_.reshape(64).astype(np.float64)
    tb9 = np.zeros((128, 71, 64), np.float32)
    sp = _bf16_pieces(sl)
    si = _bf16_pieces(-sl * ii)
    for p in range(64):
        tb9[0:3, p] = _bf16_pieces(sl * (128.0 * p - 8192.0))
        tb9[3:6, p] = sp
        tb9[6:9, p] = si
    for ch in range(4):
        x = 64 + ch
        tb9[0:3, x] = _bf16_pieces(sl * (128.0 * ch - 512.0))
        tb9[3:6, x] = sp
        tb9[6:9, x] = si
        if ch == 0:
            for r in range(4):
                tb9[9 + r, x] = np.where(r <= ii, -BIG, 0.0)
    for nc_ in range(2):
        x = 68 + nc_
        tb9[0:3, x] = _bf16_pieces(sl * (4096.0 * nc_ + 31.0 - 8192.0 - ii))
        tb9[3:6, x] = _bf16_pieces(32.0 * sl)
    x = 70
    tb9[3:6, x] = sp
    tb9[6:9, x] = si
    for r in range(4):
        tb9[9 + r, x] = np.where(r > ii, -BIG, 0.0)
    ta9 = np.zeros((128, 128), np.float32)
    ta9[0:3] = 1.0
    ta9[3:6] = np.arange(128)[None, :]
    ta9[6:9] = 1.0
    for r in range(4):
        ta9[9 + r, r] = 1.0
    tshift = np.zeros((128, 4, 128), np.float32)
    for hh in range(2):
        for g in range(2):
            for d in range(64):
                tshift[hh * 64 + d, hh * 2 + g, g * 64 + d] = 1.0
    row = np.arange(128)[:, None]
    cc = np.arange(256)[None, :]
    tselw = ((cc - 124) == row // 32).astype(np.float32)
    tb2 = (np.arange(128)[:, None] // 2 == np.arange(64)[None, :]).astype(np.float32)
    return dict(tee=tee, tb9=np.ascontiguousarray(tb9.reshape(128, 71 * 64)), tshift=np.ascontiguousarray(tshift.reshape(128, 512)),
                ta9=ta9, tselw=tselw, tb2=tb2)


def prepare_inputs(inp, cores):
    wt, widx = _weights_layout(inp)
    vecs = _vec_layout(inp)
    ident = np.eye(128, dtype=np.float32)
    bc = _b_consts()
    bc.update(_s_consts())
    wc = inp["b_w_cmp"][0]
    r32 = np.arange(128) % 32
    bc["twt"] = np.ascontiguousarray(np.concatenate([wc[0][r32].reshape(128, 128), wc[1][r32].reshape(128, 128)], axis=1))
    kvpool = np.ascontiguousarray(inp["cache_nsa_kv"][0].reshape(2560 * 128 * 2, 256))
    maps = []
    for c in cores:
        s0 = 4 * c
        maps.append({
            "xp": np.ascontiguousarray(inp["x_prompt"][c]),
            "xs": np.ascontiguousarray(inp["x_sample"][s0:s0 + 4].reshape(NS, D)),
            "wt": wt,
            "vecs": vecs,
            "ident": ident,
            "sconv": np.ascontiguousarray(inp["state_conv_a"][:, s0:s0 + 4]),
            "lruh": np.ascontiguousarray(inp["state_lru_h"][0, s0:s0 + 4]),
            "lruconv": np.ascontiguousarray(inp["state_lru_conv"][0, s0:s0 + 4]),
            "winbuf": np.ascontiguousarray(inp["cache_nsa_win"][0, s0:s0 + 4].reshape(4, 512, 256)),
            "kvpool": kvpool,
            "ptab": np.ascontiguousarray(inp["page_table"][s0:s0 + 4].astype(np.int32)),
            **bc,
        })
    return maps, widx, wt.shape[0], vecs.shape[1]


ALL_STAGES = ("A0", "B", "Bimp", "Battn", "Bsamp", "C", "A1")


def run_cores(inp, cores, stages=ALL_STAGES, trace=False):
    maps, widx, ntiles, nvec = prepare_inputs(inp, cores)
    nc = build_program(widx, ntiles, nvec, stages)
    res = run_bass_kernel_spmd(nc, maps, core_ids=list(range(len(cores))), trace=trace)
    return res


def kernel(**inputs):
    inp = {k: np.asarray(v) for k, v in inputs.items()}
    res = run_cores(inp, list(range(NCORES)))
    r = res.results
    n = NCORES
    f = np.float32
    yp = np.stack([r[c]["yp"] for c in range(n)]).astype(f)
    ys = np.concatenate([r[c]["ys"].reshape(4, 4, D) for c in range(n)]).astype(f)
    cap = np.stack([r[c]["cap"] for c in range(n)], axis=1).astype(f)
    cas = np.concatenate([r[c]["cas"] for c in range(n)], axis=1).astype(f)
    rowsp = np.stack([r[c]["rowsp"].reshape(T, 4, 2, 64) for c in range(n)])[None].astype(f)
    rowss = np.concatenate([r[c]["rowss"].reshape(4, 4, 4, 2, 64) for c in range(n)])[None].astype(f)
    winp = np.stack([r[c]["winp"].reshape(512, 2, 2, 64) for c in range(n)])[None].astype(f)
    wins = np.concatenate([r[c]["wins"].reshape(4, 512, 2, 2, 64) for c in range(n)])[None].astype(f)
    lhp = np.stack([r[c]["lhp"] for c in range(n)])[None].astype(f)
    lhs = np.concatenate([r[c]["lhs"] for c in range(n)])[None].astype(f)
    lcp = np.stack([r[c]["lcp"] for c in range(n)])[None].astype(f)
    lcs = np.concatenate([r[c]["lcs"] for c in range(n)])[None].astype(f)
    return (yp, ys, cap, cas, rowsp, rowss, winp, wins, lhp, lhs, lcp, lcs)
```

```python
import contextlib
import os
import numpy as np
KDBG = int(os.environ.get('KDBG', '99'))
import concourse.bass as bass
import concourse.mybir as mybir
from concourse.bass_utils import run_bass_kernel_spmd

F32 = mybir.dt.float32
BF16 = mybir.dt.bfloat16
I32 = mybir.dt.int32
I16 = mybir.dt.int16
AF = mybir.ActivationFunctionType
ALU = mybir.AluOpType
AX = mybir.AxisListType

NCORES = 8
D = 1024
T = 2048
NS = 16
NTOK = T + NS
ALPHA = (2 * 4) ** 0.25
LN_EPS = 1e-5
TT = [(0, 512), (512, 512), (1024, 512), (1536, 512), (2048, 16)]
BIG = 30000.0


class Buf:
    __slots__ = ("name", "w", "r")

    def __init__(self, name=""):
        self.name = name
        self.w = None
        self.r = []


class FW:
    SAME = {"pe": False, "dve": True, "act": True, "pool": True, "sp": False}
    NDMA = 12

    def __init__(self, nc, es):
        self.nc = nc
        self.eng = {"pe": nc.tensor, "dve": nc.vector, "act": nc.scalar, "pool": nc.gpsimd, "sp": nc.sync}
        self.sem = {k: es.enter_context(nc.semaphore("s_" + k)) for k in self.eng}
        self.cnt = {k: 0 for k in self.eng}
        self.waited = {}
        self.dsem = {q: [es.enter_context(nc.semaphore(f"d_{q}{i}")) for i in range(self.NDMA)]
                     for q in ("sp", "act", "pool")}
        self.dcnt = {q: 0 for q in self.dsem}
        self.dtok = {q: [] for q in self.dsem}
        self.out_tokens = []
        self.ninst = 0

    def need(self, e, tok):
        if tok is None:
            return
        sem, val, src = tok
        if src == e and not self.SAME[e]:
            return
        key = (e, id(sem))
        if self.waited.get(key, 0) >= val:
            return
        self.eng[e].wait_ge(sem, val)
        self.waited[key] = val

    def deps(self, e, reads, writes):
        for b in reads:
            self.need(e, b.w)
        for b in writes:
            self.need(e, b.w)
            for t in b.r:
                self.need(e, t)

    def commit(self, tok, reads, writes):
        for b in reads:
            b.r.append(tok)
            if len(b.r) > 48:
                best = {}
                for t in b.r:
                    k = id(t[0])
                    if k not in best or best[k][1] < t[1]:
                        best[k] = t
                b.r = list(best.values())
        for b in writes:
            b.w = tok
            b.r = []

    def op(self, e, fn, reads=(), writes=()):
        self.deps(e, reads, writes)
        ins = fn(self.eng[e])
        self.cnt[e] += 1
        ins.then_inc(self.sem[e], 1)
        tok = (self.sem[e], self.cnt[e], e)
        self.commit(tok, reads, writes)
        self.ninst += 1
        return tok

    def dma(self, q, out, in_, reads=(), writes=(), is_output=False, fn=None, **kw):
        self.deps(q, reads, writes)
        k = self.dcnt[q]
        n = self.NDMA
        sem = self.dsem[q][k % n]
        val = 16 * (k // n + 1)
        if k >= n:
            self.need(q, self.dtok[q][k - n])
        if fn is not None:
            ins = fn(self.eng[q])
        else:
            ins = self.eng[q].dma_start(out=out, in_=in_, **kw)
        ins.then_inc(sem, 16)
        tok = (sem, val, "dma_" + q)
        self.dtok[q].append(tok)
        self.dcnt[q] += 1
        self.commit(tok, reads, writes)
        if is_output:
            self.out_tokens.append(tok)
        self.ninst += 1
        return tok

    def all_tokens(self):
        toks = []
        for q in self.dtok:
            toks += self.dtok[q][-self.NDMA:]
        for e in ("pe", "dve", "act", "pool"):
            if self.cnt[e]:
                toks.append((self.sem[e], self.cnt[e], e))
        return toks

    def barrier(self):
        toks = self.all_tokens()
        for e in ("pe", "dve", "act", "pool", "sp"):
            for t in toks:
                if t[2] != e:
                    self.need(e, t)
            if self.cnt.get(e, 0) and e != "sp":
                self.need_self(e)

    def need_self(self, e):
        sem, val = self.sem[e], self.cnt[e]
        key = (e, id(sem))
        if self.waited.get(key, 0) >= val:
            return
        self.eng[e].wait_ge(sem, val)
        self.waited[key] = val

    def finish(self):
        for tok in self.out_tokens:
            self.need("sp", tok)
        for t in self.all_tokens():
            self.need("sp", t)


def _wtile(w, cols):
    cols = np.asarray(cols)
    out = np.zeros((128, 8, len(cols)), np.float32)
    valid = cols >= 0
    sub = w[:, cols[valid]].reshape(8, 128, -1).transpose(1, 0, 2)
    out[:, :, valid] = sub
    return out


def _weights_layout(inp):
    tiles = []
    idx = {}

    def add(name, arr):
        idx[name] = len(tiles)
        tiles.append(np.ascontiguousarray(arr.reshape(128, 4096)))

    def layer_a(j, tag):
        w = inp["a_w_in"][j]
        for c in range(8):
            cols = np.concatenate([np.arange(c * 128, c * 128 + 128) + o for o in (0, 1024, 2048, 3072)])
            add(f"{tag}_in{c}", _wtile(w, cols))
        wo = inp["a_w_out"][j]
        for m in range(2):
            add(f"{tag}_out{m}", _wtile(wo, np.arange(m * 512, m * 512 + 512)))

    layer_a(0, "a0")
    wb = inp["b_w_in"][0]
    for m in range(2):
        add(f"b_q{m}", _wtile(wb, np.arange(m * 512, m * 512 + 512)))
    kv0 = 1024

    def comp(ci, g):
        return kv0 + ci * 128 + g * 64 + np.arange(64)

    def dup(ci, g):
        return np.concatenate([comp(ci, g), comp(ci, g)])

    plain = lambda ci: kv0 + ci * 128 + np.arange(128)
    gates = np.concatenate([np.arange(1792, 1840), -np.ones(80, np.int64)])
    add("b_kvA", _wtile(wb, np.concatenate([dup(0, 0), dup(0, 1), plain(1), dup(2, 0)])))
    add("b_kvB", _wtile(wb, np.concatenate([dup(2, 1), dup(4, 0), dup(4, 1), gates])))
    add("b_kvR1", _wtile(wb, np.arange(1024, 1536)))
    add("b_kvR2", _wtile(wb, np.concatenate([np.arange(1536, 1840), -np.ones(512 - 304, np.int64)])))
    for m in range(2):
        add(f"b_z{m}", _wtile(wb, 1840 + np.arange(m * 512, m * 512 + 512)))
    wo = inp["b_w_out"][0]
    for m in range(2):
        add(f"b_out{m}", _wtile(wo, np.arange(m * 512, m * 512 + 512)))
    wc = inp["c_w_in"][0]

    def blk(w):
        return w.reshape(4, 2, 128, 256).transpose(2, 0, 1, 3).reshape(128, 2048)
    add("c_u0", _wtile(wc, np.arange(0, 512)))
    add("c_wax", np.concatenate([blk(inp["c_w_a"][0]), blk(inp["c_w_x"][0])], axis=1))
    add("c_u1", _wtile(wc, np.arange(512, 1024)))
    for m in range(2):
        add(f"c_z{m}", _wtile(wc, 1024 + np.arange(m * 512, m * 512 + 512)))
    wo = inp["c_w_out"][0]
    for m in range(2):
        add(f"c_out{m}", _wtile(wo, np.arange(m * 512, m * 512 + 512)))
    layer_a(1, "a1")
    return np.stack(tiles), idx


def _fm(v):
    v = np.asarray(v, np.float32)
    lead = v.shape[:-1]
    return np.ascontiguousarray(np.moveaxis(v.reshape(lead + (8, 128)), -1, 0))


VEC = {}


def _vec_layout(inp):
    cols = []

    def add(name, arr):
        arr = np.asarray(arr, np.float32).reshape(128, -1)
        VEC[name] = (sum(a.shape[1] for a in cols), arr.shape[1])
        cols.append(arr)

    add("ln_g", _fm(inp["ln_g"]))
    add("ln_b", _fm(inp["ln_b"]))
    add("a_conv", _fm(inp["a_conv_w"]))
    add("c_conv", _fm(inp["c_conv_w"]))
    add("c_conv_b", _fm(inp["c_conv_b"]))
    add("c_b_a", _fm(inp["c_b_a"]))
    add("c_b_x", _fm(inp["c_b_x"]))
    add("c_lam", _fm(inp["c_lam"]))
    wc = inp["b_w_cmp"][0]
    for g in range(2):
        a = wc[0, :, g, :].T
        add(f"b_wk{g}", np.concatenate([a, a], axis=0))
    add("b_wv", wc[1].transpose(1, 2, 0).reshape(128, 32))
    return np.ascontiguousarray(np.concatenate(cols, axis=1))


RING = 3


class Kern:
    def __init__(self, nc, es, widx, ntiles, nvec, stages):
        self.nc = nc
        self.es = es
        self.fw = FW(nc, es)
        self.widx = widx
        self.stages = stages
        dt = nc.dram_tensor
        self.d_xp = dt("xp", [T, D], F32, kind="ExternalInput").ap()
        self.d_xs = dt("xs", [NS, D], F32, kind="ExternalInput").ap()
        self.d_wt = dt("wt", [ntiles, 128, 4096], F32, kind="ExternalInput").ap()
        self.d_vecs = dt("vecs", [128, nvec], F32, kind="ExternalInput").ap()
        self.d_ident = dt("ident", [128, 128], F32, kind="ExternalInput").ap()
        self.d_sconv = dt("sconv", [2, 4, 2, D], F32, kind="ExternalInput").ap()
        self.d_lruh = dt("lruh", [4, D], F32, kind="ExternalInput").ap()
        self.d_lruconv = dt("lruconv", [4, 3, D], F32, kind="ExternalInput").ap()
        self.d_lhp = dt("lhp", [D], F32, kind="ExternalOutput").ap()
        self.d_lhs = dt("lhs", [4, D], F32, kind="ExternalOutput").ap()
        self.d_lcp = dt("lcp", [3, D], F32, kind="ExternalOutput").ap()
        self.d_lcs = dt("lcs", [4, 3, D], F32, kind="ExternalOutput").ap()
        self.d_tdw = dt("tdw", [128, 2432], F32, kind="ExternalInput").ap()
        self.d_tww = dt("tww", [128, 1408], F32, kind="ExternalInput").ap()
        self.d_tdc = dt("tdc", [64, 2048], F32, kind="ExternalInput").ap()
        self.d_te2 = dt("te2", [128, 2048], F32, kind="ExternalInput").ap()
        self.d_tselg = dt("tselg", [48, 3072], F32, kind="ExternalInput").ap()
        self.d_tm12 = dt("tm12", [128, 2048], F32, kind="ExternalInput").ap()
        self.d_biasc = dt("biasc", [16, 128, 1024], F32, kind="ExternalInput").ap()
        self.d_winbuf = dt("winbuf", [4, 512, 256], F32, kind="ExternalInput").ap()
        self.d_rowsp = dt("rowsp", [T, 512], F32, kind="ExternalOutput").ap()
        self.d_winp = dt("winp", [512, 256], F32, kind="ExternalOutput").ap()
        self.d_rowss = dt("rowss", [4, 4, 512], F32, kind="ExternalOutput").ap()
        self.d_wins = dt("wins", [4, 512, 256], F32, kind="ExternalOutput").ap()
        self.d_kvpool = dt("kvpool", [2560 * 128 * 2, 256], F32, kind="ExternalInput").ap()
        self.d_ptab = dt("ptab", [4, 64], I32, kind="ExternalInput").ap()
        self.d_scr = dt("scr_slc", [512, 128, 256], F32, kind="Internal").ap()
        self.scrb = [Buf() for _ in range(512)]
        self.d_tee = dt("tee", [128, 8192], F32, kind="ExternalInput").ap()
        self.d_tb9 = dt("tb9", [128, 71 * 64], F32, kind="ExternalInput").ap()
        self.d_tshift = dt("tshift", [128, 512], F32, kind="ExternalInput").ap()
        self.d_ta9 = dt("ta9", [128, 128], F32, kind="ExternalInput").ap()
        self.d_tselw = dt("tselw", [128, 256], F32, kind="ExternalInput").ap()
        self.d_tb2 = dt("tb2", [128, 64], F32, kind="ExternalInput").ap()
        self.d_twt = dt("twt", [128, 256], F32, kind="ExternalInput").ap()
        self.d_yp = dt("yp", [T, D], F32, kind="ExternalOutput").ap()
        self.d_ys = dt("ys", [NS, D], F32, kind="ExternalOutput").ap()
        self.d_cap = dt("cap", [2, 2, D], F32, kind="ExternalOutput").ap()
        self.d_cas = dt("cas", [2, 4, 2, D], F32, kind="ExternalOutput").ap()
        sb = self.sb
        self.XH = sb("XH", [128, 8, NTOK], BF16)
        self.XL = sb("XL", [128, 8, NTOK], BF16)
        self.G = sb("G", [128, 8, NTOK], BF16)
        self.ring = [sb(f"ring{i}", [128, 8, 512], BF16) for i in range(RING)]
        self.ringb = [Buf(f"ring{i}") for i in range(RING)]
        self.vecs = sb("vecs", [128, nvec], F32)
        self.ident = sb("ident", [128, 128], F32)
        self.ones = sb("ones", [128, 128], BF16)
        self.b_vecs, self.b_ident, self.b_ones = Buf("vecs"), Buf("ident"), Buf("ones")
        self.xb = {(k, t): Buf(f"x{k}_{t}") for k in range(8) for t in range(5)}
        self.gb = {(k, t): Buf(f"g{k}_{t}") for k in range(8) for t in range(5)}
        self.psum = [es.enter_context(nc.psum_tensor(f"ps{i}", [128, 512], F32)) for i in range(8)]
        self.psb = [Buf(f"ps{i}") for i in range(8)]
        self.psi = 0
        self.ps_rot = list(range(8))
        self.wnext = 0
        self.ntiles = ntiles
        self.tmp_i = {}

    def sb(self, name, shape, dtype=F32, es=None):
        self.nsb = getattr(self, "nsb", 0) + 1
        return (es or self.es).enter_context(self.nc.sbuf_tensor(f"s{self.nsb}_{name}", shape, dtype))

    def ps(self):
        i = self.ps_rot[self.psi % len(self.ps_rot)]
        self.psi += 1
        return self.psum[i], self.psb[i]

    def vec(self, name):
        o, n = VEC[name]
        return self.vecs[:, o:o + n]

    def _issue_w(self, i):
        slot = i % RING
        self.fw.dma("pool", self.ring[slot][:], self.d_wt[i].rearrange("p (k n) -> p k n", k=8),
                    writes=[self.ringb[slot]])

    def wtile(self, name, group=None):
        i = self.widx[name]
        g0 = self.widx[group] if group else i
        assert i < g0 + RING
        while self.wnext < min(self.ntiles, g0 + RING):
            self._issue_w(self.wnext)
            self.wnext += 1
        return self.ring[i % RING], self.ringb[i % RING]

    def setup(self):
        fw = self.fw
        fw.dma("sp", self.vecs[:], self.d_vecs[:, :], writes=[self.b_vecs])
        fw.dma("sp", self.ident[:], self.d_ident[:, :], writes=[self.b_ident])
        fw.op("dve", lambda e: e.memset(self.ones[:], 1.0), writes=[self.b_ones])

    def load_x(self, es):
        fw = self.fw
        xin = [self.sb(f"xin{i}", [128, D], F32, es) for i in range(2)]
        xinb = [Buf(), Buf()]
        for i in range(17):
            n = 128 if i < 16 else NS
            t0 = i * 128
            tt = min(i // 4, 4)
            xi, xib = xin[i % 2], xinb[i % 2]
            src = self.d_xp[t0:t0 + 128, :] if i < 16 else self.d_xs[:, :]
            fw.dma("sp", xi[0:n, :], src, writes=[xib])
            for half in range(2):
                pst, psb = self.ps()
                for q in range(4):
                    dc = half * 4 + q
                    fw.op("pe", lambda e: e.transpose(pst[:, q * 128:q * 128 + n], xi[0:n, dc * 128:(dc + 1) * 128], self.ident[0:n, 0:n]),
                          reads=[xib, self.b_ident], writes=[psb])
                bufs = [self.xb[(half * 4 + q, tt)] for q in range(4)]
                src_ps = pst[:].rearrange("p (q t) -> p q t", q=4)[:, :, 0:n]
                hi = self.XH[:, half * 4:half * 4 + 4, t0:t0 + n]
                lo = self.XL[:, half * 4:half * 4 + 4, t0:t0 + n]
                fw.op("act", lambda e: e.activation(out=hi, in_=src_ps, func=AF.Identity), reads=[psb], writes=bufs)
                fw.op("dve", lambda e: e.tensor_tensor(out=lo, in0=src_ps, in1=hi, op=ALU.subtract), reads=[psb] + bufs, writes=bufs)

    def resid_ln_begin(self, es):
        self.R = self.sb("R", [128, 8, 512], F32, es)
        self.Rb = [Buf(f"R{m}") for m in range(8)]
        self.lt = {n: (self.sb("ln_" + n, [128, 512], F32, es), Buf()) for n in ("t", "M", "A", "B")}
        self.lbf = [(self.sb(f"ln_bf{i}", [128, 512], BF16, es), Buf()) for i in range(4)]
        self.lbi = 0
        self.lhl = [(self.sb(f"ln_hl{i}", [128, 512], F32, es), Buf()) for i in range(2)]

    def resid_add(self, layer, tt, m, yps, ypsb, acc):
        fw = self.fw
        t0, n = TT[tt]
        tT, tB = self.lt["t"]
        xb = self.xb[(m, tt)]
        hl_, hlb_ = self.lhl[m % 2]
        fw.op("pool", lambda e: e.tensor_tensor(out=hl_[:, 0:n], in0=self.XH[:, m, t0:t0 + n], in1=self.XL[:, m, t0:t0 + n], op=ALU.add),
              reads=[xb], writes=[hlb_])
        fw.op("dve", lambda e: e.scalar_tensor_tensor(out=self.R[:, m, 0:n], in0=hl_[:, 0:n], scalar=ALPHA, in1=yps[:, 0:n], op0=ALU.mult, op1=ALU.add),
              reads=[hlb_, ypsb], writes=[self.Rb[m]])
        (b1, b1b), (b2, b2b) = self.lbf[self.lbi], self.lbf[self.lbi + 1]
        self.lbi = (self.lbi + 2) % 4
        fw.op("act", lambda e: e.activation(out=b1[:, 0:n], in_=self.R[:, m, 0:n], func=AF.Identity), reads=[self.Rb[m]], writes=[b1b])
        fw.op("act", lambda e: e.activation(out=b2[:, 0:n], in_=self.R[:, m, 0:n], func=AF.Square), reads=[self.Rb[m]], writes=[b2b])
        (ps1, ps1b), (ps2, ps2b) = acc
        fw.op("pe", lambda e: e.matmul(ps1[:, 0:n], lhsT=self.ones[:, :], rhs=b1[:, 0:n], start=(m == 0), stop=(m == 7)),
              reads=[b1b, self.b_ones], writes=[ps1b])
        fw.op("pe", lambda e: e.matmul(ps2[:, 0:n], lhsT=self.ones[:, :], rhs=b2[:, 0:n], start=(m == 0), stop=(m == 7)),
              reads=[b2b, self.b_ones], writes=[ps2b])

    def ln_apply(self, layer, tt, acc, final, es_out=None):
        fw = self.fw
        t0, n = TT[tt]
        (ps1, ps1b), (ps2, ps2b) = acc
        (tT, tB), (M, Mb), (A, Ab), (B, Bb) = self.lt["t"], self.lt["M"], self.lt["A"], self.lt["B"]
        fw.op("act", lambda e: e.activation(out=M[:, 0:n], in_=ps1[:, 0:n], func=AF.Identity, scale=1.0 / D), reads=[ps1b], writes=[Mb])
        fw.op("dve", lambda e: e.tensor_tensor(out=tT[:, 0:n], in0=M[:, 0:n], in1=M[:, 0:n], op=ALU.mult), reads=[Mb], writes=[tB])
        fw.op("dve", lambda e: e.scalar_tensor_tensor(out=tT[:, 0:n], in0=ps2[:, 0:n], scalar=1.0 / D, in1=tT[:, 0:n], op0=ALU.mult, op1=ALU.subtract),
              reads=[ps2b, tB], writes=[tB])
        fw.op("dve", lambda e: e.tensor_scalar(out=tT[:, 0:n], in0=tT[:, 0:n], scalar1=LN_EPS, scalar2=None, op0=ALU.add), reads=[tB], writes=[tB])
        fw.op("act", lambda e: e.activation(out=tT[:, 0:n], in_=tT[:, 0:n], func=AF.Sqrt), reads=[tB], writes=[tB])
        fw.op("dve", lambda e: e.reciprocal(out=A[:, 0:n], in_=tT[:, 0:n]), reads=[tB], writes=[Ab])
        fw.op("dve", lambda e: e.scalar_tensor_tensor(out=B[:, 0:n], in0=M[:, 0:n], scalar=-1.0, in1=A[:, 0:n], op0=ALU.mult, op1=ALU.mult),
              reads=[Mb, Ab], writes=[Bb])
        og, ob = VEC["ln_g"][0] + layer * 8, VEC["ln_b"][0] + layer * 8
        Rm = [self.R[:, m, 0:n] for m in range(8)]
        for m in range(8):
            fw.op("dve", lambda e: e.tensor_tensor(out=Rm[m], in0=Rm[m], in1=A[:, 0:n], op=ALU.mult), reads=[self.Rb[m], Ab], writes=[self.Rb[m]])
        for m in range(8):
            fw.op("dve", lambda e: e.tensor_tensor(out=Rm[m], in0=Rm[m], in1=B[:, 0:n], op=ALU.add), reads=[self.Rb[m], Bb], writes=[self.Rb[m]])
        for m in range(8):
            fw.op("act", lambda e: e.activation(out=Rm[m], in_=Rm[m], func=AF.Identity, scale=self.vecs[:, og + m:og + m + 1], bias=self.vecs[:, ob + m:ob + m + 1]),
                  reads=[self.Rb[m], self.b_vecs], writes=[self.Rb[m]])
        if not final:
            for m in range(8):
                fw.op("act", lambda e: e.activation(out=self.XH[:, m, t0:t0 + n], in_=Rm[m], func=AF.Identity), reads=[self.Rb[m]], writes=[self.xb[(m, tt)]])
            for m in range(8):
                fw.op("pool", lambda e: e.tensor_tensor(out=self.XL[:, m, t0:t0 + n], in0=Rm[m], in1=self.XH[:, m, t0:t0 + n], op=ALU.subtract),
                      reads=[self.Rb[m], self.xb[(m, tt)]], writes=[self.xb[(m, tt)]])
        if final:
            for j in range((n + 127) // 128):
                nn = min(128, n - j * 128)
                yt, ytb = self.yt[self.yti]
                self.yti = (self.yti + 1) % 2
                for half in range(2):
                    pst, psb = self.ps()
                    for q in range(4):
                        m = half * 4 + q
                        fw.op("pe", lambda e: e.transpose(pst[0:nn, q * 128:(q + 1) * 128], self.R[:, m, j * 128:j * 128 + nn], self.ident[:, :]),
                              reads=[self.Rb[m], self.b_ident], writes=[psb])
                    if half == 0:
                        fw.op("act", lambda e: e.activation(out=yt[0:nn, 0:512], in_=pst[0:nn, :], func=AF.Identity), reads=[psb], writes=[ytb])
                    else:
                        fw.op("dve", lambda e: e.tensor_copy(out=yt[0:nn, 512:1024], in_=pst[0:nn, :]), reads=[psb], writes=[ytb])
                dst = self.d_yp[t0 + j * 128:t0 + j * 128 + nn, :] if tt < 4 else self.d_ys[:, :]
                fw.dma("sp", dst, yt[0:nn, :], reads=[ytb], is_output=True)

    def out_proj_ln(self, layer, wnames, final):
        fw = self.fw
        self.ps_rot = list(range(6))
        for tt in range(5):
            t0, n = TT[tt]
            acc = ((self.psum[6], self.psb[6]), (self.psum[7], self.psb[7]))
            for m in range(8):
                w, wb = self.wtile(wnames[m // 4], group=wnames[0])
                yps, ypsb = self.ps()
                for k in range(8):
                    fw.op("pe", lambda e: e.matmul(yps[:, 0:n], lhsT=w[:, k, (m % 4) * 128:(m % 4) * 128 + 128], rhs=self.G[:, k, t0:t0 + n],
                                                   start=(k == 0), stop=(k == 7)),
                          reads=[wb, self.gb[(k, tt)]], writes=[ypsb])
                self.resid_add(layer, tt, m, yps, ypsb, acc)
            self.ln_apply(layer, tt, acc, final)
        self.ps_rot = list(range(8))

    def layer_A(self, layer, j, final=False):
        fw = self.fw
        tag = f"a{j}"
        with contextlib.ExitStack() as es:
            self.resid_ln_begin(es)
            if final:
                self.yt = [(self.sb(f"yt{i}", [128, D], F32, es), Buf()) for i in range(2)]
                self.yti = 0
            ext = self.sb("a_ext", [128, 8, 514], F32, es)
            extb = [Buf(f"ext{c}") for c in range(8)]
            exs = self.sb("a_exs", [128, 8, 4, 6], F32, es)
            exsb = [Buf(f"exs{c}") for c in range(8)]
            tmp = [(self.sb(f"a_t{i}", [128, 512], F32, es), Buf()) for i in range(8)]
            ti = [0]

            def T_():
                r = tmp[ti[0]]
                ti[0] = (ti[0] + 1) % len(tmp)
                return r
            fw.op("pool", lambda e: e.memset(ext[:, :, 0:2], 0.0), writes=extb)
            st_in = self.sb("a_stin", [128, 8, 8], F32, es)
            st_out = self.sb("a_stout", [128, 8, 8], F32, es)
            stib, stob = Buf(), [Buf() for _ in range(8)]
            for c in range(8):
                fw.dma("sp", st_in[:, c, :], self.d_sconv[j, :, :, c * 128:(c + 1) * 128].rearrange("s r p -> p (s r)"),
                       writes=[stib], allow_slow_non_contiguous=True)
            for c in range(8):
                fw.op("act", lambda e: e.activation(out=exs[:, c, :, 0:2], in_=st_in[:, c, :].rearrange("p (s r) -> p s r", s=4), func=AF.Identity),
                      reads=[stib], writes=[exsb[c]])
            cw = VEC["a_conv"][0] + j * 24
            for c in range(8):
                w, wb = self.wtile(f"{tag}_in{c}")
                wcol = [self.vecs[:, cw + r * 8 + c:cw + r * 8 + c + 1] for r in range(3)]
                for tt in range(5):
                    t0, n = TT[tt]
                    pss = []
                    for q in range(4):
                        p_, pb_ = self.ps()
                        for k in range(8):
                            fw.op("pe", lambda e: e.matmul(p_[:, 0:n], lhsT=w[:, k, q * 128:(q + 1) * 128], rhs=self.XH[:, k, t0:t0 + n],
                                                           start=(k == 0), stop=(k == 7)),
                                  reads=[wb, self.xb[(k, tt)]], writes=[pb_])
                        pss.append((p_, pb_))
                    (ph, phb), (pb, pbb), (pc, pcb), (pz, pzb) = pss
                    (sz, szb), (hh, hhb), (cv, cvb), (gbt, gbb) = T_(), T_(), T_(), T_()
                    fw.op("act", lambda e: e.activation(out=hh[:, 0:n], in_=ph[:, 0:n], func=AF.Identity), reads=[phb], writes=[hhb])
                    fw.op("act", lambda e: e.activation(out=sz[:, 0:n], in_=pz[:, 0:n], func=AF.Silu), reads=[pzb], writes=[szb])
                    if tt < 4:
                        eb = extb[c]
                        u_new = ext[:, c, 2:2 + n]
                        sh = [ext[:, c, r:r + n] for r in range(3)]
                        o_cv, o_gb, o_hh, o_pc = cv[:, 0:n], gbt[:, 0:n], hh[:, 0:n], pc[:, 0:n]
                        o_sz, o_pb = sz[:, 0:n], pb[:, 0:n]
                        o_g = self.G[:, c, t0:t0 + n]
                    else:
                        eb = exsb[c]
                        v4 = lambda ap: ap[:, 0:n].rearrange("p (s t) -> p s t", s=4)
                        u_new = exs[:, c, :, 2:6]
                        sh = [exs[:, c, :, r:r + 4] for r in range(3)]
                        o_cv, o_gb, o_hh, o_pc = v4(cv), v4(gbt), v4(hh), v4(pc)
                        o_sz, o_pb = v4(sz), v4(pb)
                        o_g = self.G[:, c, t0:t0 + n].rearrange("p (s t) -> p s t", s=4)
                    fw.op("dve", lambda e: e.tensor_tensor(out=u_new, in0=o_pc, in1=o_hh, op=ALU.mult), reads=[pcb, hhb], writes=[eb])
                    fw.op("dve", lambda e: e.tensor_scalar(out=o_cv, in0=sh[0], scalar1=wcol[0], scalar2=None, op0=ALU.mult),
                          reads=[eb, self.b_vecs], writes=[cvb])
                    fw.op("dve", lambda e: e.scalar_tensor_tensor(out=o_cv, in0=sh[1], scalar=wcol[1], in1=o_cv, op0=ALU.mult, op1=ALU.add),
                          reads=[eb, cvb], writes=[cvb])
                    fw.op("dve", lambda e: e.scalar_tensor_tensor(out=o_cv, in0=sh[2], scalar=wcol[2], in1=o_cv, op0=ALU.mult, op1=ALU.add),
                          reads=[eb, cvb], writes=[cvb])
                    fw.op("dve", lambda e: e.tensor_tensor(out=o_gb, in0=o_pb, in1=o_sz, op=ALU.mult), reads=[pbb, szb], writes=[gbb])
                    fw.op("dve", lambda e: e.tensor_tensor(out=o_g, in0=o_gb, in1=o_cv, op=ALU.mult), reads=[gbb, cvb], writes=[self.gb[(c, tt)]])
                    if tt < 3:
                        fw.op("act", lambda e: e.activation(out=ext[:, c, 0:2], in_=ext[:, c, 512:514], func=AF.Identity), reads=[eb], writes=[eb])
                fw.dma("sp", self.d_cap[j, :, c * 128:(c + 1) * 128].rearrange("r p -> p r"), ext[:, c, 512:514],
                       reads=[extb[c]], is_output=True, allow_slow_non_contiguous=True)
                fw.op("act", lambda e: e.activation(out=st_out[:, c, :].rearrange("p (s r) -> p s r", s=4), in_=exs[:, c, :, 4:6], func=AF.Identity),
                      reads=[exsb[c]], writes=[stob[c]])
                fw.dma("sp", self.d_cas[j, :, :, c * 128:(c + 1) * 128].rearrange("s r p -> p (s r)"), st_out[:, c, :],
                       reads=[stob[c]], is_output=True, allow_slow_non_contiguous=True)
            self.out_proj_ln(layer, [f"{tag}_out0", f"{tag}_out1"], final)
            fw.barrier()


    def layer_C(self, layer):
        fw = self.fw
        with contextlib.ExitStack() as es:
            self.resid_ln_begin(es)
            ext = self.sb("c_ext", [128, 2, 515], F32, es)
            extb = [Buf(), Buf()]
            exs = self.sb("c_exs", [128, 8, 4, 7], F32, es)
            exsb = [Buf() for _ in range(8)]
            st_in = self.sb("c_stin", [128, 8, 12], F32, es)
            st_out = self.sb("c_stout", [128, 8, 12], F32, es)
            stib, stob = Buf(), [Buf() for _ in range(8)]
            h0 = self.sb("c_h0", [128, 8, 4], F32, es)
            h0b = Buf()
            hl = self.sb("c_hl", [128, 8], F32, es)
            hlb = [Buf() for _ in range(8)]
            hs = self.sb("c_hs", [128, 8, 4], F32, es)
            hsb = [Buf() for _ in range(8)]
            cl = self.sb("c_cl", [128, 16], F32, es)
            clb = Buf()
            tmp = [(self.sb(f"c_t{i}", [128, 512], F32, es), Buf()) for i in range(10)]
            ucb16 = [(self.sb(f"c_ub{i}", [128, 512], BF16, es), Buf()) for i in range(4)]
            ti = [0]

            def T_():
                r = tmp[ti[0]]
                ti[0] = (ti[0] + 1) % len(tmp)
                return r
            for c in range(8):
                fw.dma("sp", st_in[:, c, :], self.d_lruconv[:, :, c * 128:(c + 1) * 128].rearrange("s r p -> p (s r)"),
                       writes=[stib], allow_slow_non_contiguous=True)
                fw.dma("sp", h0[:, c, :], self.d_lruh[:, c * 128:(c + 1) * 128].rearrange("s p -> p s"),
                       writes=[h0b], allow_slow_non_contiguous=True)
            for c in range(8):
                fw.op("act", lambda e: e.activation(out=exs[:, c, :, 0:3], in_=st_in[:, c, :].rearrange("p (s r) -> p s r", s=4), func=AF.Identity),
                      reads=[stib], writes=[exsb[c]])
            lam = self.vec("c_lam")
            fw.op("act", lambda e: e.activation(out=cl[:, 0:8], in_=lam, func=AF.Exp, scale=-1.0), reads=[self.b_vecs], writes=[clb])
            fw.op("dve", lambda e: e.tensor_scalar(out=cl[:, 0:8], in0=cl[:, 0:8], scalar1=1.0, scalar2=None, op0=ALU.add), reads=[clb], writes=[clb])
            fw.op("act", lambda e: e.activation(out=cl[:, 0:8], in_=cl[:, 0:8], func=AF.Ln), reads=[clb], writes=[clb])
            fw.op("dve", lambda e: e.tensor_scalar(out=cl[:, 8:16], in0=cl[:, 0:8], scalar1=-16.0, scalar2=None, op0=ALU.mult), reads=[clb], writes=[clb])
            fw.op("dve", lambda e: e.tensor_scalar(out=cl[:, 0:8], in0=cl[:, 0:8], scalar1=-8.0, scalar2=None, op0=ALU.mult), reads=[clb], writes=[clb])
            cw = VEC["c_conv"][0]
            V1 = lambda name, m: self.vecs[:, VEC[name][0] + m:VEC[name][0] + m + 1]
            ucf = [(self.sb(f"c_uc{i}", [128, 512], F32, es), Buf()) for i in range(4)]
            st = {}

            def front(nb, tt):
                wu, wub = self.wtile("c_u0" if nb < 2 else "c_u1", group="c_u0")
                t0, n = TT[tt]
                ucs = []
                for kk in range(2):
                    m = 2 * nb + kk
                    q = m % 4
                    pu, pub = self.ps()
                    for k in range(8):
                        fw.op("pe", lambda e: e.matmul(pu[:, 0:n], lhsT=wu[:, k, q * 128:(q + 1) * 128], rhs=self.XH[:, k, t0:t0 + n],
                                                       start=(k == 0), stop=(k == 7)),
                              reads=[wub, self.xb[(k, tt)]], writes=[pub])
                    wcol = [self.vecs[:, cw + r * 8 + m:cw + r * 8 + m + 1] for r in range(4)]
                    uc, ucbuf = ucf[(2 * tt + kk) % 4]
                    ub, ubb = ucb16[(2 * tt + kk) % 4]
                    if tt < 4:
                        eb = extb[kk]
                        u_new = ext[:, kk, 3:3 + n]
                        sh = [ext[:, kk, r:r + n] for r in range(4)]
                        o_uc, o_pu = uc[:, 0:n], pu[:, 0:n]
                    else:
                        eb = exsb[m]
                        v4 = lambda ap: ap[:, 0:n].rearrange("p (s t) -> p s t", s=4)
                        u_new = exs[:, m, :, 3:7]
                        sh = [exs[:, m, :, r:r + 4] for r in range(4)]
                        o_uc, o_pu = v4(uc), v4(pu)
                    fw.op("act", lambda e: e.activation(out=u_new, in_=o_pu, func=AF.Identity), reads=[pub], writes=[eb])
                    fw.op("dve", lambda e: e.tensor_scalar(out=o_uc, in0=sh[0], scalar1=wcol[0], scalar2=V1("c_conv_b", m), op0=ALU.mult, op1=ALU.add),
                          reads=[eb, self.b_vecs], writes=[ucbuf])
                    for r in range(1, 4):
                        fw.op("dve", lambda e: e.scalar_tensor_tensor(out=o_uc, in0=sh[r], scalar=wcol[r], in1=o_uc, op0=ALU.mult, op1=ALU.add),
                              reads=[eb, ucbuf], writes=[ucbuf])
                    fw.op("act", lambda e: e.activation(out=ub[:, 0:n], in_=uc[:, 0:n], func=AF.Identity), reads=[ucbuf], writes=[ubb])
                    if tt < 3:
                        fw.op("act", lambda e: e.activation(out=ext[:, kk, 0:3], in_=ext[:, kk, 512:515], func=AF.Identity), reads=[eb], writes=[eb])
                    if tt == 3:
                        fw.dma("sp", self.d_lcp[:, m * 128:(m + 1) * 128].rearrange("r p -> p r"), ext[:, kk, 512:515],
                               reads=[eb], is_output=True, allow_slow_non_contiguous=True)
                    if tt == 4:
                        fw.op("act", lambda e: e.activation(out=st_out[:, m, :].rearrange("p (s r) -> p s r", s=4), in_=exs[:, m, :, 4:7], func=AF.Identity),
                              reads=[eb], writes=[stob[m]])
                        fw.dma("sp", self.d_lcs[:, :, m * 128:(m + 1) * 128].rearrange("s r p -> p (s r)"), st_out[:, m, :],
                               reads=[stob[m]], is_output=True, allow_slow_non_contiguous=True)
                    ucs.append((uc, ucbuf, ub, ubb))
                st[(nb, tt)] = ucs

            def back(nb, tt):
                wax, waxb = self.wtile("c_wax", group="c_u0")
                t0, n = TT[tt]
                ucs = st.pop((nb, tt))
                P = []
                for kk in range(2):
                    pr, prb = self.ps()
                    pi, pib = self.ps()
                    for k in range(2):
                        fw.op("pe", lambda e: e.matmul(pr[:, 0:n], lhsT=wax[:, nb, k * 256 + kk * 128:k * 256 + kk * 128 + 128], rhs=ucs[k][2][:, 0:n],
                                                       start=(k == 0), stop=(k == 1)),
                              reads=[waxb, ucs[k][3]], writes=[prb])
                    for k in range(2):
                        fw.op("pe", lambda e: e.matmul(pi[:, 0:n], lhsT=wax[:, 4 + nb, k * 256 + kk * 128:k * 256 + kk * 128 + 128], rhs=ucs[k][2][:, 0:n],
                                                       start=(k == 0), stop=(k == 1)),
                              reads=[waxb, ucs[k][3]], writes=[pib])
                    P.append((pr, prb, pi, pib, T_(), T_(), T_(), T_(), T_()))
                for kk in range(2):
                    m = 2 * nb + kk
                    pr, prb, pi, pib, (rr, rrb), (ii, iib), (aa, aab), (bb, bbb), (hh, hhb) = P[kk]
                    fw.op("act", lambda e: e.activation(out=rr[:, 0:n], in_=pr[:, 0:n], func=AF.Sigmoid, bias=V1("c_b_a", m)), reads=[prb, self.b_vecs], writes=[rrb])
                    fw.op("act", lambda e: e.activation(out=ii[:, 0:n], in_=pi[:, 0:n], func=AF.Sigmoid, bias=V1("c_b_x", m)), reads=[pib, self.b_vecs], writes=[iib])
                for kk in range(2):
                    m = 2 * nb + kk
                    pr, prb, pi, pib, (rr, rrb), (ii, iib), (aa, aab), (bb, bbb), (hh, hhb) = P[kk]
                    fw.op("act", lambda e: e.activation(out=aa[:, 0:n], in_=rr[:, 0:n], func=AF.Exp, scale=cl[:, m:m + 1]), reads=[rrb, clb], writes=[aab])
                    fw.op("act", lambda e: e.activation(out=bb[:, 0:n], in_=rr[:, 0:n], func=AF.Exp, scale=cl[:, 8 + m:9 + m]), reads=[rrb, clb], writes=[bbb])
                for kk in range(2):
                    pr, prb, pi, pib, (rr, rrb), (ii, iib), (aa, aab), (bb, bbb), (hh, hhb) = P[kk]
                    uc, ucbuf = ucs[kk][0], ucs[kk][1]
                    fw.op("dve", lambda e: e.tensor_scalar(out=bb[:, 0:n], in0=bb[:, 0:n], scalar1=-1.0, scalar2=1.0, op0=ALU.mult, op1=ALU.add), reads=[bbb], writes=[bbb])
                    fw.op("dve", lambda e: e.tensor_scalar(out=bb[:, 0:n], in0=bb[:, 0:n], scalar1=0.0, scalar2=None, op0=ALU.max), reads=[bbb], writes=[bbb])
                    fw.op("pool", lambda e: e.tensor_tensor(out=ii[:, 0:n], in0=ii[:, 0:n], in1=uc[:, 0:n], op=ALU.mult), reads=[iib, ucbuf], writes=[iib])
                for kk in range(2):
                    pr, prb, pi, pib, (rr, rrb), (ii, iib), (aa, aab), (bb, bbb), (hh, hhb) = P[kk]
                    fw.op("act", lambda e: e.activation(out=bb[:, 0:n], in_=bb[:, 0:n], func=AF.Sqrt), reads=[bbb], writes=[bbb])
                for kk in range(2):
                    m = 2 * nb + kk
                    pr, prb, pi, pib, (rr, rrb), (ii, iib), (aa, aab), (bb, bbb), (hh, hhb) = P[kk]
                    fw.op("dve", lambda e: e.tensor_tensor(out=bb[:, 0:n], in0=bb[:, 0:n], in1=ii[:, 0:n], op=ALU.mult), reads=[bbb, iib], writes=[bbb])
                    if tt < 4:
                        init = 0.0 if tt == 0 else hl[:, m:m + 1]
                        fw.op("dve", lambda e: e.tensor_tensor_scan(out=hh[:, 0:n], data0=aa[:, 0:n], data1=bb[:, 0:n], initial=init, op0=ALU.mult, op1=ALU.add),
                              reads=[aab, bbb, hlb[m]], writes=[hhb])
                    else:
                        for s_ in range(4):
                            fw.op("dve", lambda e: e.tensor_tensor_scan(out=hh[:, 4 * s_:4 * s_ + 4], data0=aa[:, 4 * s_:4 * s_ + 4], data1=bb[:, 4 * s_:4 * s_ + 4],
                                                                        initial=h0[:, m, s_:s_ + 1], op0=ALU.mult, op1=ALU.add),
                                  reads=[aab, bbb, h0b], writes=[hhb])
                for kk in range(2):
                    m = 2 * nb + kk
                    pr, prb, pi, pib, (rr, rrb), (ii, iib), (aa, aab), (bb, bbb), (hh, hhb) = P[kk]
                    if tt < 4:
                        fw.op("act", lambda e: e.activation(out=hl[:, m:m + 1], in_=hh[:, n - 1:n], func=AF.Identity), reads=[hhb], writes=[hlb[m]])
                    else:
                        fw.op("act", lambda e: e.activation(out=hs[:, m, :], in_=hh[:, 0:n].rearrange("p (s t) -> p s t", s=4)[:, :, 3], func=AF.Identity),
                              reads=[hhb], writes=[hsb[m]])
                    fw.op("act", lambda e: e.activation(out=self.G[:, m, t0:t0 + n], in_=hh[:, 0:n], func=AF.Identity), reads=[hhb], writes=[self.gb[(m, tt)]])

            for nb in range(4):
                fw.op("pool", lambda e: e.memset(ext[:, :, 0:3], 0.0), writes=extb)
                front(nb, 0)
                for tt in range(5):
                    if tt + 1 < 5:
                        front(nb, tt + 1)
                    back(nb, tt)
            for m in range(8):
                fw.dma("sp", self.d_lhs[:, m * 128:(m + 1) * 128].rearrange("s p -> p s"), hs[:, m, :], reads=[hsb[m]], is_output=True, allow_slow_non_contiguous=True)
                fw.dma("sp", self.d_lhp[m * 128:(m + 1) * 128].rearrange("(p o) -> p o", o=1), hl[:, m:m + 1], reads=[hlb[m]], is_output=True, allow_slow_non_contiguous=True)
            for m in range(8):
                wz, wzb = self.wtile(f"c_z{m // 4}")
                for tt in range(5):
                    t0, n = TT[tt]
                    pz, pzb = self.ps()
                    for k in range(8):
                        fw.op("pe", lambda e: e.matmul(pz[:, 0:n], lhsT=wz[:, k, (m % 4) * 128:(m % 4) * 128 + 128], rhs=self.XH[:, k, t0:t0 + n],
                                                       start=(k == 0), stop=(k == 7)),
                              reads=[wzb, self.xb[(k, tt)]], writes=[pzb])
                    sz, szb = T_()
                    fw.op("act", lambda e: e.activation(out=sz[:, 0:n], in_=pz[:, 0:n], func=AF.Silu), reads=[pzb], writes=[szb])
                    gsl = self.G[:, m, t0:t0 + n]
                    fw.op("dve", lambda e: e.tensor_tensor(out=gsl, in0=gsl, in1=sz[:, 0:n], op=ALU.mult), reads=[szb, self.gb[(m, tt)]], writes=[self.gb[(m, tt)]])
            self.out_proj_ln(layer, ["c_out0", "c_out1"], False)
            fw.barrier()


    def b_tables(self, es, which):
        fw = self.fw
        t = {}
        def ld(name, dram, shape, dtype, q="pool"):
            tl = self.sb("tb_" + name, shape, dtype, es)
            b = Buf(name)
            fw.dma(q, tl[:], dram, writes=[b])
            t[name] = (tl, b)
        if which == "attn":
            ld("dw", self.d_tdw[:, :], [128, 2432], I16)
            ld("ww", self.d_tww[:, :], [128, 1408], I16)
            ld("dc", self.d_tdc[:, :], [64, 2048], I16)
            ld("e2", self.d_te2[:, :], [128, 2048], BF16)
            ld("selg", self.d_tselg[:, :], [48, 3072], BF16)
        else:
            ld("m12", self.d_tm12[:, :], [128, 2048], F32, q="sp")
        return t

    def layer_B(self, layer):
        fw = self.fw
        with contextlib.ExitStack() as es:
            es_p = contextlib.ExitStack()
            GTh = self.sb("GTh", [48, NTOK], BF16, es)
            KCf = self.sb("KCf", [128, 2, 64], F32, es)
            KC = self.sb("KC", [128, 2, 64], BF16, es)
            VCt = self.sb("VCt", [128, 64], F32, es)
            VC = self.sb("VC", [64, 128], BF16, es)
            kvs = [self.sb(f"kvs{i}", [64, 560], F32, es) for i in range(2)]
            KS = self.sb("KS", [128, 2, NTOK], BF16, es_p)
            KW = self.sb("KW", [128, 2, NTOK], BF16, es_p)
            ksb = {(g, tt): Buf() for g in range(2) for tt in range(5)}
            kwb = {(g, tt): Buf() for g in range(2) for tt in range(5)}
            gtb = [Buf() for _ in range(5)]
            VS = self.sb("VS", [128, 16, 128], BF16, es_p)
            VW = self.sb("VW", [128, 16, 128], BF16, es_p)
            vsb = [Buf() for _ in range(16)]
            NM = self.sb("NM", [128, 2, T], BF16, es_p)
            nmb = [Buf() for _ in range(16)]
            kcfb, kcb, vctb, vcb = Buf(), Buf(), Buf(), Buf()
            kvsb = [Buf() for _ in range(4)]
            self.B = dict(KS=KS, KW=KW, ksb=ksb, kwb=kwb, GTh=GTh, gtb=gtb, VS=VS, VW=VW, vsb=vsb, NM=NM, nmb=nmb,
                          KC=KC, kcb=kcb, VC=VC, vcb=vcb, kvs=kvs, kvsb=kvsb)
            with contextlib.ExitStack() as es1:
                tmp = [(self.sb(f"b_t{i}", [128, 512], F32, es1), Buf()) for i in range(4)]
                kvr = [(self.sb(f"b_kvr{i}", [128, 768], F32, es1), Buf()) for i in range(2)]
                ti = [0]

                def T_():
                    r = tmp[ti[0]]
                    ti[0] = (ti[0] + 1) % len(tmp)
                    return r
                for c in range(8):
                    w, wb = self.wtile(f"b_q{c // 4}")
                    for tt in range(5):
                        t0, n = TT[tt]
                        p_, pb_ = self.ps()
                        for k in range(8):
                            fw.op("pe", lambda e: e.matmul(p_[:, 0:n], lhsT=w[:, k, (c % 4) * 128:(c % 4) * 128 + 128], rhs=self.XH[:, k, t0:t0 + n],
                                                           start=(k == 0), stop=(k == 7)), reads=[wb, self.xb[(k, tt)]], writes=[pb_])
                        fw.op("act", lambda e: e.activation(out=self.G[:, c, t0:t0 + n], in_=p_[:, 0:n], func=AF.Identity, scale=0.125),
                              reads=[pb_], writes=[self.gb[(c, tt)]])
                wk = [self.vec("b_wk0"), self.vec("b_wk1"), self.vec("b_wv")]
                for wi, wname in enumerate(("b_kvA", "b_kvB")):
                    w, wb = self.wtile(wname)
                    for q in range(4):
                        for tt in range(5):
                            t0, n = TT[tt]
                            kind = wi * 4 + q
                            if kind <= 2 and tt == 4:
                                continue
                            p_, pb_ = self.ps()
                            for k in range(8):
                                fw.op("pe", lambda e: e.matmul(p_[:, 0:n], lhsT=w[:, k, q * 128:(q + 1) * 128], rhs=self.XH[:, k, t0:t0 + n],
                                                               start=(k == 0), stop=(k == 7)), reads=[wb, self.xb[(k, tt)]], writes=[pb_])
                            if kind <= 2:
                                tm, tmb = T_()
                                wv_ = wk[kind].rearrange("p (o j) -> p o j", o=1).to_broadcast([128, 16, 32])
                                fw.op("dve", lambda e: e.tensor_tensor(out=tm[:, :].rearrange("p (b j) -> p b j", j=32), in0=p_[:, :].rearrange("p (b j) -> p b j", j=32),
                                                                       in1=wv_, op=ALU.mult), reads=[pb_, self.b_vecs], writes=[tmb])
                                dst = KCf[:, kind, 16 * tt:16 * tt + 16] if kind < 2 else VCt[:, 16 * tt:16 * tt + 16]
                                fw.op("dve", lambda e: e.reduce_sum(out=dst, in_=tm[:, :].rearrange("p (b j) -> p b j", j=32), axis=AX.X),
                                      reads=[tmb], writes=[kcfb if kind < 2 else vctb])
                            elif kind <= 6:
                                g = (kind - 3) % 2
                                dstT, dstb = (KS, ksb) if kind <= 4 else (KW, kwb)
                                fw.op("act", lambda e: e.activation(out=dstT[:, g, t0:t0 + n], in_=p_[:, 0:n], func=AF.Identity), reads=[pb_], writes=[dstb[(g, tt)]])
                            else:
                                tm, tmb = T_()
                                fw.op("act", lambda e: e.activation(out=tm[0:48, 0:n], in_=p_[0:48, 0:n], func=AF.Sigmoid), reads=[pb_], writes=[tmb])
                                fw.op("act", lambda e: e.activation(out=GTh[:, t0:t0 + n], in_=tm[0:48, 0:n], func=AF.Identity), reads=[tmb], writes=[gtb[tt]])
                w1, w1b = self.wtile("b_kvR1", group="b_kvR1")
                w2, w2b = self.wtile("b_kvR2", group="b_kvR1")
                for i in range(16):
                    tt = i // 4
                    kv, kvb = kvr[i % 2]
                    p1, p1b = self.ps()
                    p2, p2b = self.ps()
                    for k in range(8):
                        fw.op("pe", lambda e: e.matmul(p1[:, :], lhsT=self.XH[:, k, i * 128:(i + 1) * 128], rhs=w1[:, k, :], start=(k == 0), stop=(k == 7)),
                              reads=[w1b, self.xb[(k, tt)]], writes=[p1b])
                    for k in range(8):
                        fw.op("pe", lambda e: e.matmul(p2[:, 0:256], lhsT=self.XH[:, k, i * 128:(i + 1) * 128], rhs=w2[:, k, 0:256], start=(k == 0), stop=(k == 7)),
                              reads=[w2b, self.xb[(k, tt)]], writes=[p2b])
                    fw.op("act", lambda e: e.activation(out=kv[:, 0:512], in_=p1[:, :], func=AF.Identity), reads=[p1b], writes=[kvb])
                    fw.op("dve", lambda e: e.tensor_copy(out=kv[:, 512:768], in_=p2[:, 0:256]), reads=[p2b], writes=[kvb])
                    fw.dma("sp", self.d_rowsp[i * 128:(i + 1) * 128, :], kv[:, 0:512], reads=[kvb], is_output=True)
                    if i >= 12:
                        fw.dma("sp", self.d_winp[(i - 12) * 128:(i - 11) * 128, :], kv[:, 512:768], reads=[kvb], is_output=True)
                    fw.op("pool", lambda e: e.tensor_copy(out=VS[:, i, :], in_=kv[:, 384:512]), reads=[kvb], writes=[vsb[i]])
                    fw.op("pool", lambda e: e.tensor_copy(out=VW[:, i, :], in_=kv[:, 640:768]), reads=[kvb], writes=[vsb[i]])
                for s_ in range(4):
                    p1, p1b = self.ps()
                    p2, p2b = self.ps()
                    c0 = T + 4 * s_
                    po = 32 * (s_ % 2)
                    kvt = kvs[s_ // 2]
                    kv, kvb = kvr[s_ % 2]
                    for k in range(8):
                        fw.op("pe", lambda e: e.matmul(p1[po:po + 4, :], lhsT=self.XH[:, k, c0:c0 + 4], rhs=w1[:, k, :], start=(k == 0), stop=(k == 7)),
                              reads=[w1b, self.xb[(k, 4)]], writes=[p1b])
                    for k in range(8):
                        fw.op("pe", lambda e: e.matmul(p2[po:po + 4, 0:304], lhsT=self.XH[:, k, c0:c0 + 4], rhs=w2[:, k, 0:304], start=(k == 0), stop=(k == 7)),
                              reads=[w2b, self.xb[(k, 4)]], writes=[p2b])
                    fw.op("act", lambda e: e.activation(out=kv[po:po + 4, 0:512], in_=p1[po:po + 4, :], func=AF.Identity), reads=[p1b], writes=[kvb])
                    fw.op("act", lambda e: e.activation(out=kvt[po:po + 4, 0:256], in_=p1[po:po + 4, 256:512], func=AF.Identity), reads=[p1b], writes=[kvsb[s_]])
                    fw.op("dve", lambda e: e.tensor_copy(out=kvt[po:po + 4, 256:560], in_=p2[po:po + 4, 0:304]), reads=[p2b], writes=[kvsb[s_]])
                    fw.dma("sp", self.d_rowss[s_, :, :], kv[po:po + 4, 0:512], reads=[kvb], is_output=True)
                    fw.dma("sp", self.d_wins[s_, 508:512, :], kvt[po:po + 4, 256:512], reads=[kvsb[s_]], is_output=True)
                    fw.dma("sp", self.d_wins[s_, 0:508, :], self.d_winbuf[s_, 4:512, :], is_output=True)
                fw.op("act", lambda e: e.activation(out=KC[:, :, :], in_=KCf[:, :, :], func=AF.Identity), reads=[kcfb], writes=[kcb])
                pv, pvb = self.ps()
                fw.op("pe", lambda e: e.transpose(pv[0:64, 0:128], VCt[:, 0:64], self.ident[:, :]), reads=[vctb, self.b_ident], writes=[pvb])
                fw.op("act", lambda e: e.activation(out=VC[:, :], in_=pv[0:64, 0:128], func=AF.Identity), reads=[pvb], writes=[vcb])
                fw.barrier()
            if "Bimp" in self.stages:
                self.b_importance(es)
            if "Battn" in self.stages:
                self.b_attention(es)
            fw.barrier()
            es_p.close()
            if "Bsamp" in self.stages:
                self.b_sample(es)
            with contextlib.ExitStack() as es3:
                self.resid_ln_begin(es3)
                tmp = [(self.sb(f"b_z{i}", [128, 512], F32, es3), Buf()) for i in range(3)]
                for m in range(8):
                    wz, wzb = self.wtile(f"b_z{m // 4}")
                    for tt in range(5):
                        t0, n = TT[tt]
                        pz, pzb = self.ps()
                        for k in range(8):
                            fw.op("pe", lambda e: e.matmul(pz[:, 0:n], lhsT=wz[:, k, (m % 4) * 128:(m % 4) * 128 + 128], rhs=self.XH[:, k, t0:t0 + n],
                                                           start=(k == 0), stop=(k == 7)), reads=[wzb, self.xb[(k, tt)]], writes=[pzb])
                        sz, szb = tmp[(m * 5 + tt) % 3]
                        fw.op("act", lambda e: e.activation(out=sz[:, 0:n], in_=pz[:, 0:n], func=AF.Silu), reads=[pzb], writes=[szb])
                        gsl = self.G[:, m, t0:t0 + n]
                        fw.op("dve", lambda e: e.tensor_tensor(out=gsl, in0=gsl, in1=sz[:, 0:n], op=ALU.mult), reads=[szb, self.gb[(m, tt)]], writes=[self.gb[(m, tt)]])
                fin = "final_after_B" in self.stages
                if fin:
                    self.yt = [(self.sb(f"yt{i}", [128, D], F32, es3), Buf()) for i in range(2)]
                    self.yti = 0
                self.out_proj_ln(layer, ["b_out0", "b_out1"], fin)
                fw.barrier()

    def b_importance(self, es):
        fw = self.fw
        B = self.B
        KC, kcb, NM, nmb = B["KC"], B["kcb"], B["NM"], B["nmb"]
        with contextlib.ExitStack() as es2:
            m12, m12b = self.b_tables(es2, "imp")["m12"]
            KCbd = self.sb("bi_kcbd", [128, 2, 128], BF16, es2)
            kcbdb = Buf()
            fw.op("pool", lambda e: e.memset(KCbd[:, :, :], 0.0), writes=[kcbdb])
            fw.op("act", lambda e: e.activation(out=KCbd[0:64, :, 0:64], in_=KC[0:64, :, :], func=AF.Identity), reads=[kcb], writes=[kcbdb])
            fw.op("act", lambda e: e.activation(out=KCbd[64:128, :, 64:128], in_=KC[64:128, :, :], func=AF.Identity), reads=[kcb], writes=[kcbdb])
            bc = [(self.sb(f"bi_bc{i}", [128, 1024], F32, es2), Buf()) for i in range(2)]
            E = [(self.sb(f"bi_E{i}", [128, 1024], F32, es2), Buf()) for i in range(2)]
            t1 = self.sb("bi_t1", [128, 512], F32, es2); t1b = Buf()
            sm = self.sb("bi_sm", [128, 16], F32, es2); smb = Buf()
            t2 = self.sb("bi_t2", [128, 128], F32, es2); t2b = Buf()
            imp = self.sb("bi_imp", [128, 64], F32, es2); impb = Buf()
            wk = self.sb("bi_wk", [128, 64], F32, es2); wkb = Buf()
            m8 = self.sb("bi_m8", [128, 16], F32, es2); m8b = Buf()
            nq = [(self.sb(f"bi_nq{i}", [128, 2, 128], F32, es2), Buf()) for i in range(2)]
            for nqt, nqb in nq:
                fw.op("pool", lambda e: e.memset(nqt[:, :, :], 0.0), writes=[nqb])
            for qt in range(16):
                bct, bcb = bc[qt % 2]
                Et, Eb = E[qt % 2]
                if KDBG <= -1:
                    continue
                fw.dma("sp", bct[:], self.d_biasc[qt], writes=[bcb])
                if KDBG <= 0:
                    continue
                pA, pAb = self.ps()
                pB, pBb = self.ps()
                for c in range(8):
                    pp, ppb = (pA, pAb) if c < 4 else (pB, pBb)
                    fw.op("pe", lambda e: e.matmul(pp[:, (c % 4) * 128:(c % 4) * 128 + 128], lhsT=self.G[:, c, qt * 128:(qt + 1) * 128],
                                                   rhs=KCbd[:, c // 4, :], start=True, stop=True),
                          reads=[self.gb[(c, qt // 4)], kcbdb], writes=[ppb])
                if not (KDBG == 1 and os.environ.get("KSUB") == "add2"):
                    fw.op("dve", lambda e: e.tensor_tensor(out=Et[:, 0:512], in0=pA[:, :], in1=bct[:, 0:512], op=ALU.add), reads=[pAb, bcb], writes=[Eb])
                if KDBG == 1 and os.environ.get("KSUB") == "add1":
                    continue
                fw.op("dve", lambda e: e.tensor_tensor(out=Et[:, 512:1024], in0=pB[:, :], in1=bct[:, 512:1024], op=ALU.add), reads=[pBb, bcb, Eb], writes=[Eb])
                if KDBG <= 1:
                    continue
                fw.op("act", lambda e: e.activation(out=Et[:, :], in_=Et[:, :], func=AF.Exp), reads=[Eb], writes=[Eb])
                E3 = Et[:, :].rearrange("p (h n) -> p h n", h=16)
                fw.op("dve", lambda e: e.reduce_sum(out=sm[:, :], in_=E3, axis=AX.X), reads=[Eb], writes=[smb])
                fw.op("dve", lambda e: e.tensor_scalar(out=sm[:, :], in0=sm[:, :], scalar1=1e-30, scalar2=None, op0=ALU.max), reads=[smb], writes=[smb])
                fw.op("dve", lambda e: e.reciprocal(out=sm[:, :], in_=sm[:, :]), reads=[smb], writes=[smb])
                fw.op("dve", lambda e: e.tensor_tensor(out=E3, in0=E3, in1=sm[:, :].rearrange("p (h o) -> p h o", o=1).to_broadcast([128, 16, 64]), op=ALU.mult),
                      reads=[Eb, smb], writes=[Eb])
                if KDBG <= 2:
                    continue
                fw.op("dve", lambda e: e.reduce_sum(out=t1[:, :], in_=Et[:, :].rearrange("p (x t) -> p x t", t=2), axis=AX.X), reads=[Eb], writes=[t1b])
                fw.op("dve", lambda e: e.reduce_sum(out=imp[:, :].rearrange("p (g b) -> p g b", g=2), in_=t1[:, :].rearrange("p (g h b) -> p g b h", g=2, h=8), axis=AX.X),
                      reads=[t1b], writes=[impb])
                if KDBG <= 3:
                    continue
                M1 = m12[:, qt * 128:qt * 128 + 64]
                M2 = m12[:, qt * 128 + 64:qt * 128 + 128]
                fw.op("dve", lambda e: e.tensor_tensor(out=imp[:, :], in0=imp[:, :], in1=M1, op=ALU.mult), reads=[impb, m12b], writes=[impb])
                fw.op("dve", lambda e: e.tensor_tensor(out=imp[:, :], in0=imp[:, :], in1=M2, op=ALU.add), reads=[impb, m12b], writes=[impb])
                if KDBG <= 4:
                    continue
                for g in range(2):
                    sl = slice(g * 32, g * 32 + 32)
                    ml = slice(g * 8, g * 8 + 8)
                    fw.op("dve", lambda e: e.max(out=m8[:, ml], in_=imp[:, sl]), reads=[impb], writes=[m8b])
                    fw.op("dve", lambda e: e.match_replace(out=wk[:, sl], in_to_replace=m8[:, ml], in_values=imp[:, sl], imm_value=-2.0), reads=[impb, m8b], writes=[wkb])
                    fw.op("dve", lambda e: e.max(out=m8[:, ml], in_=wk[:, sl]), reads=[wkb], writes=[m8b])
                    fw.op("dve", lambda e: e.match_replace(out=wk[:, sl], in_to_replace=m8[:, ml], in_values=wk[:, sl], imm_value=-2.0), reads=[wkb, m8b], writes=[wkb])
                if KDBG <= 5:
                    continue
                nqt, nqb = nq[qt % 2]
                for g in range(2):
                    fw.op("dve", lambda e: e.tensor_scalar(out=nqt[:, g, :].rearrange("p (a x) -> p a x", a=2)[:, :, 0:32],
                                                           in0=wk[:, g * 32:g * 32 + 32].rearrange("p (o x) -> p o x", o=1).to_broadcast([128, 2, 32]),
                                                           scalar1=-2.0, scalar2=-BIG, op0=ALU.not_equal, op1=ALU.mult), reads=[wkb], writes=[nqb])
                for g in range(2):
                    pt_, ptb_ = self.ps()
                    fw.op("pe", lambda e: e.transpose(pt_[:, 0:128], nqt[:, g, :], self.ident[:, :]), reads=[nqb, self.b_ident], writes=[ptb_])
                    fw.op("act", lambda e: e.activation(out=NM[:, g, qt * 128:(qt + 1) * 128], in_=pt_[:, 0:128], func=AF.Identity), reads=[ptb_], writes=[nmb[qt]])
            fw.barrier()

    def b_attention(self, es):
        fw = self.fw
        B = self.B
        KS, KW, ksb, kwb, VS, VW, vsb = B["KS"], B["KW"], B["ksb"], B["kwb"], B["VS"], B["VW"], B["vsb"]
        KC, kcb, VC, vcb, NM, nmb = B["KC"], B["kcb"], B["VC"], B["vcb"], B["NM"], B["nmb"]
        GTh, gtb = B["GTh"], B["gtb"]
        slopes = [2.0 ** (-8.0 * (h + 1) / 16.0) for h in range(16)]
        LOOK = 2
        with contextlib.ExitStack() as es2:
            tb = self.b_tables(es2, "attn")
            (dw, dwb), (ww, wwb), (dc, dcb), (e2, e2b), (selg, selgb) = (tb[k] for k in ("dw", "ww", "dc", "e2", "selg"))
            SB = [(self.sb(f"ba_sb{i}", [128, 512], F32, es2), Buf()) for i in range(2)]
            PT = [(self.sb(f"ba_pt{i}", [128, 512], BF16, es2), Buf()) for i in range(3)]
            R1 = [(self.sb(f"ba_r{i}", [128, 512], F32, es2), Buf()) for i in range(1)]
            TM = [(self.sb(f"ba_tm{i}", [128, 512], F32, es2), Buf()) for i in range(1)]
            ACC = [(self.sb(f"ba_acc{i}", [128, 512], F32, es2), Buf()) for i in range(1)]
            QZ = [[(self.sb(f"ba_qz{i}{hh}", [128, 512], BF16, es2), Buf()) for hh in range(2)] for i in range(2)]
            for i in range(2):
                for hh in range(2):
                    fw.op("pool", lambda e: e.memset(QZ[i][hh][0][:, :], 0.0), writes=[QZ[i][hh][1]])
            self.ps_rot = [0, 1, 2]
            items = []
            pair_i = 0
            br_i = 0
            for QT in range(4):
                for c in range(8):
                    for br in range(3):
                        for hh in range(2):
                            if br == 0:
                                chunks = [None]
                            elif br == 1:
                                chunks = list(range(0, 4 * QT + 4))
                            else:
                                chunks = list(range(max(0, 4 * QT - 4), 4 * QT + 4))
                            for ci, kc in enumerate(chunks):
                                items.append(dict(QT=QT, c=c, br=br, hh=hh, kc=kc, first=(ci == 0), last=(ci == len(chunks) - 1),
                                                  pair=pair_i, bri=br_i, new_pair=(br == 0 and hh == 0 and ci == 0),
                                                  end_br=(hh == 1 and ci == len(chunks) - 1)))
                        br_i += 1
                    pair_i += 1
            for n_, it in enumerate(items):
                it["pt"] = PT[n_ % 3]
                it["sb"] = SB[n_ % 2]

            def front(it):
                QT, c, br, hh, kc = it["QT"], it["c"], it["br"], it["hh"], it["kc"]
                q0 = QT * 512
                h = 2 * c + hh
                g = h // 8
                qz = QZ[it["pair"] % 2]
                if it["new_pair"]:
                    for h2 in range(2):
                        hs_ = slice(h2 * 64, h2 * 64 + 64)
                        fw.op("act", lambda e: e.activation(out=qz[h2][0][hs_, :], in_=self.G[hs_, c, q0:q0 + 512], func=AF.Identity),
                              reads=[self.gb[(c, QT)]], writes=[qz[h2][1]])
                qrhs, qzb = qz[hh][0][:, :], qz[hh][1]
                ps_, psb_ = self.ps()
                sbt, sbb = it["sb"]
                ptt, ptb = it["pt"]
                if br == 0:
                    nk = 64
                    fw.op("pe", lambda e: e.matmul(ps_[0:64, :], lhsT=KC[:, g, :], rhs=qrhs, start=True, stop=True), reads=[kcb, qzb], writes=[psb_])
                    dsl, dbuf = dc[0:64, q0:q0 + 512], dcb
                else:
                    nk = 128
                    Kt, Kb = (KS, ksb) if br == 1 else (KW, kwb)
                    fw.op("pe", lambda e: e.matmul(ps_[:, :], lhsT=Kt[:, g, kc * 128:(kc + 1) * 128], rhs=qrhs, start=True, stop=(br == 2)),
                          reads=[Kb[(g, kc // 4)], qzb], writes=[psb_])
                    if br == 1:
                        fw.op("pe", lambda e: e.matmul(ps_[:, :], lhsT=e2[:, kc * 128:(kc + 1) * 128], rhs=NM[:, g, q0:q0 + 512], start=False, stop=True),
                              reads=[e2b] + [nmb[QT * 4 + j] for j in range(4)], writes=[psb_])
                    off = 512 * QT - 128 * kc + 384
                    tbl, dbuf = (dw, dwb) if br == 1 else (ww, wwb)
                    dsl = tbl[:, off:off + 512]
                it["nk"] = nk
                fw.op("dve", lambda e: e.scalar_tensor_tensor(out=sbt[0:nk, :], in0=dsl, scalar=slopes[h], in1=ps_[0:nk, :], op0=ALU.mult, op1=ALU.add),
                      reads=[dbuf, psb_], writes=[sbb])
                fw.op("act", lambda e: e.activation(out=ptt[0:nk, :], in_=sbt[0:nk, :], func=AF.Exp), reads=[sbb], writes=[ptb])

            def back(it):
                QT, c, br, hh, kc = it["QT"], it["c"], it["br"], it["hh"], it["kc"]
                q0 = QT * 512
                g = (2 * c + hh) // 8
                hs = slice(hh * 64, hh * 64 + 64)
                oi = it["bri"] % 2
                psO, psOb = self.psum[3 + 3 * oi], self.psb[3 + 3 * oi]
                psD, psDb = self.psum[4 + 3 * oi], self.psb[4 + 3 * oi]
                ptt, ptb = it["pt"]
                nk = it["nk"]
                if br == 0:
                    vl, vbuf = VC[0:64, g * 64:g * 64 + 64], vcb
                else:
                    vl, vbuf = (VS if br == 1 else VW)[:, kc, g * 64:g * 64 + 64], vsb[kc]
                fw.op("pe", lambda e: e.matmul(psO[hs, :], lhsT=vl, rhs=ptt[0:nk, :], start=it["first"], stop=it["last"]), reads=[vbuf, ptb], writes=[psOb])
                fw.op("pe", lambda e: e.matmul(psD[hs, :], lhsT=self.ones[0:nk, 0:64], rhs=ptt[0:nk, :], start=it["first"], stop=it["last"]),
                      reads=[self.b_ones, ptb], writes=[psDb])
                if not it["end_br"]:
                    return
                acc, accb = ACC[0]
                psG, psGb = self.psum[5], self.psb[5]
                so = (br * 8 + c) * 128
                fw.op("pe", lambda e: e.matmul(psG[:, :], lhsT=selg[0:48, so:so + 128], rhs=GTh[0:48, q0:q0 + 512], start=True, stop=True),
                      reads=[selgb, gtb[QT]], writes=[psGb])
                r1, r1b = R1[0]
                fw.op("dve", lambda e: e.tensor_scalar(out=r1[:, :], in0=psD[:, :], scalar1=1e-30, scalar2=None, op0=ALU.max), reads=[psDb], writes=[r1b])
                fw.op("dve", lambda e: e.reciprocal(out=r1[:, :], in_=r1[:, :]), reads=[r1b], writes=[r1b])
                fw.op("dve", lambda e: e.tensor_tensor(out=r1[:, :], in0=psG[:, :], in1=r1[:, :], op=ALU.mult), reads=[psGb, r1b], writes=[r1b])
                if br == 0:
                    fw.op("dve", lambda e: e.tensor_tensor(out=acc[:, :], in0=psO[:, :], in1=r1[:, :], op=ALU.mult), reads=[psOb, r1b], writes=[accb])
                else:
                    tm, tmb = TM[0]
                    fw.op("dve", lambda e: e.tensor_tensor(out=tm[:, :], in0=psO[:, :], in1=r1[:, :], op=ALU.mult), reads=[psOb, r1b], writes=[tmb])
                    fw.op("dve", lambda e: e.tensor_tensor(out=acc[:, :], in0=acc[:, :], in1=tm[:, :], op=ALU.add), reads=[accb, tmb], writes=[accb])
                if br == 2:
                    fw.op("act", lambda e: e.activation(out=self.G[:, c, q0:q0 + 512], in_=acc[:, :], func=AF.Identity), reads=[accb], writes=[self.gb[(c, QT)]])

            do_pre = "Bsamp" in self.stages
            if do_pre:
                gidx = self.sb("ba_gidx", [128, 2, 256], I32, es2); gidxb = Buf()
                stg = [(self.sb(f"ba_stg{i}", [128, 256], F32, es2), Buf()) for i in range(2)]
                gio = self.sb("ba_gio", [128, 1], F32, es2); giob = Buf()
                pti_ = stg[0][0][:, :].bitcast(I32)
                fw.dma("sp", pti_, self.d_ptab.rearrange("s p -> (s p)").rearrange("(o n) -> o n", o=1).to_broadcast([128, 256]), writes=[stg[0][1]])
                fw.op("pool", lambda e: e.iota(gio[:], pattern=[[0, 1]], base=0, channel_multiplier=1, allow_small_or_imprecise_dtypes=True), writes=[giob])
                fw.op("dve", lambda e: e.tensor_copy(out=stg[1][0][:, :], in_=pti_), reads=[stg[0][1]], writes=[stg[1][1]])
                fw.op("dve", lambda e: e.tensor_scalar(out=stg[1][0][:, :], in0=stg[1][0][:, :], scalar1=128.0, scalar2=gio[:, 0:1], op0=ALU.mult, op1=ALU.add),
                      reads=[stg[1][1], giob], writes=[stg[1][1]])
                fw.op("dve", lambda e: e.tensor_scalar(out=gidx[:, 1, :], in0=stg[1][0][:, :], scalar1=2.0, scalar2=1.0, op0=ALU.mult, op1=ALU.add),
                      reads=[stg[1][1]], writes=[gidxb])
                fw.op("dve", lambda e: e.tensor_scalar(out=gidx[:, 0, :], in0=stg[1][0][:, :], scalar1=2.0, scalar2=None, op0=ALU.mult),
                      reads=[stg[1][1]], writes=[gidxb])
            gn = [0]

            def pregather():
                n = gn[0]
                if not do_pre or n >= 512:
                    return
                gn[0] += 1
                st_, stb_ = stg[n % 2]
                hf_, pgi = (0, n) if n < 256 else (1, n - 256)
                fw.dma("pool", None, None, reads=[gidxb], writes=[stb_],
                       fn=lambda e: e.indirect_dma_start(out=st_[:, :], out_offset=None, in_=self.d_kvpool[:, :],
                                                         in_offset=bass.IndirectOffsetOnAxis(ap=gidx[:, hf_, pgi:pgi + 1], axis=0)))
                fw.dma("sp", self.d_scr[n], st_[:, :], reads=[stb_], writes=[self.scrb[n]])

            N = len(items)
            for i in range(N + LOOK):
                if i < N:
                    front(items[i])
                if i >= LOOK:
                    back(items[i - LOOK])
                if i % 2 == 1:
                    pregather()
            while do_pre and gn[0] < 512:
                pregather()
            self.ps_rot = list(range(8))
            fw.barrier()

    def b_sample(self, es):
        fw = self.fw
        B = self.B
        kvs, kvsb, GTh, gtb = B["kvs"], B["kvsb"], B["GTh"], B["gtb"]
        with contextlib.ExitStack() as es2:
            def ld(name, dram, shape, dtype, q="pool"):
                tl = self.sb("ts_" + name, shape, dtype, es2)
                b = Buf(name)
                fw.dma(q, tl[:], dram, writes=[b])
                return tl, b
            EE, EEb = ld("ee", self.d_tee[:, :], [128, 8192], BF16)
            T9, T9b = ld("b9", self.d_tb9[:, :], [128, 71 * 64], BF16)
            SH, SHb = ld("shift", self.d_tshift[:, :], [128, 512], BF16)
            A9, A9b = ld("a9", self.d_ta9[:, :], [128, 128], BF16)
            SW, SWb = ld("selw", self.d_tselw[:, :], [128, 256], BF16)
            B2, B2b = ld("b2", self.d_tb2[:, :], [128, 64], F32, q="sp")
            WT, WTb = ld("wt", self.d_twt[:, :], [128, 256], F32, q="sp")
            selg, selgb = ld("selg", self.d_tselg[:, :], [48, 3072], BF16)
            T9v = T9[:, :].rearrange("p (x c) -> p x c", c=64)
            QS = self.sb("QS", [128, 4, 2, 32], BF16, es2); QSb = Buf()
            idx = self.sb("idx", [128, 2, 256], I32, es2); idxb = Buf()
            pti = self.sb("pti", [128, 256], I32, es2)
            ptf = self.sb("ptf", [128, 256], F32, es2)
            io = self.sb("io", [128, 1], F32, es2)
            ptb_, iob = Buf(), Buf()
            PG = [(self.sb(f"PG{i}", [128, 256], F32, es2), Buf()) for i in range(6)]
            PRD = [(self.sb(f"PRD{i}", [128, 256], BF16, es2), Buf()) for i in range(3)]
            KT = [(self.sb(f"KTs{i}", [128, 128], BF16, es2), Buf()) for i in range(4)]
            VB = [(self.sb(f"VBs{i}", [128, 128], BF16, es2), Buf()) for i in range(4)]
            PTs = [(self.sb(f"PTs{i}", [128, 32], BF16, es2), Buf()) for i in range(10)]
            PTN = [(self.sb(f"PTn{i}", [128, 32], BF16, es2), Buf()) for i in range(4)]
            VNs = [(self.sb(f"VN{i}", [128, 128], BF16, es2), Buf()) for i in range(2)]
            NEWf = self.sb("NEWf", [4, 512], F32, es2); NEWfb = Buf()
            KN = self.sb("KN", [128, 8], BF16, es2); KNb = Buf()
            kcT = self.sb("kcT", [128, 256], BF16, es2); kcTb = Buf()
            vc = self.sb("vcs", [128, 2, 128], BF16, es2); vcb_ = Buf()
            ef = self.sb("ef", [128, 2, 32], F32, es2); efb = Buf()
            pns = self.sb("pns", [128, 2, 4], F32, es2); pnsb = Buf()
            rd = self.sb("rd", [128, 32], F32, es2); rdb = Buf()
            imps = self.sb("imps", [4, 128], F32, es2); impsb = Buf()
            IMP = self.sb("IMP", [32, 136], F32, es2); IMPb = Buf()
            WK = self.sb("WKs", [32, 136], F32, es2); WKb = Buf()
            M8 = self.sb("M8s", [32, 8], F32, es2); M8b = Buf()
            NQ = self.sb("NQs", [32, 128], F32, es2); NQb = Buf()
            NMs = self.sb("NMs", [128, 32], BF16, es2); NMsb = Buf()
            NMr = self.sb("NMr", [128, 8, 8, 4], BF16, es2); NMrb = Buf()
            cr = [(self.sb(f"cr{i}", [128, 16], F32, es2), Buf()) for i in range(2)]
            cacc = self.sb("cacc", [128, 16], F32, es2); caccb = Buf()
            fw.dma("sp", pti[:], self.d_ptab.rearrange("s p -> (s p)").rearrange("(o n) -> o n", o=1).to_broadcast([128, 256]), writes=[ptb_])
            fw.op("pool", lambda e: e.iota(io[:], pattern=[[0, 1]], base=0, channel_multiplier=1, allow_small_or_imprecise_dtypes=True), writes=[iob])
            fw.op("dve", lambda e: e.tensor_copy(out=ptf[:], in_=pti[:]), reads=[ptb_], writes=[ptb_])
            fw.op("dve", lambda e: e.tensor_scalar(out=ptf[:], in0=ptf[:], scalar1=128.0, scalar2=io[:, 0:1], op0=ALU.mult, op1=ALU.add), reads=[ptb_, iob], writes=[ptb_])
            fw.op("dve", lambda e: e.tensor_scalar(out=idx[:, 0, :], in0=ptf[:], scalar1=2.0, scalar2=None, op0=ALU.mult), reads=[ptb_], writes=[idxb])
            fw.op("dve", lambda e: e.tensor_scalar(out=idx[:, 1, :], in0=ptf[:], scalar1=2.0, scalar2=1.0, op0=ALU.mult, op1=ALU.add), reads=[ptb_], writes=[idxb])
            for t_, b_ in PTN + VNs:
                fw.op("pool", lambda e: e.memset(t_[:, :], 0.0), writes=[b_])
            fw.op("pool", lambda e: e.memset(IMP[:, :], 0.0), writes=[IMPb])
            pq, pqb = self.ps()
            for h in range(16):
                c, hh, g = h // 2, h % 2, h // 8
                fw.op("pe", lambda e: e.matmul(pq[:, h * 16:(h + 1) * 16], lhsT=SH[:, (hh * 2 + g) * 128:(hh * 2 + g) * 128 + 128], rhs=self.G[:, c, T:T + 16],
                                               start=True, stop=True), reads=[SHb, self.gb[(c, 4)]], writes=[pqb])
            for g in range(2):
                fw.op("act", lambda e: e.activation(out=QS[:, :, g, :].rearrange("p s (h i) -> p s h i", h=8),
                                                    in_=pq[:, g * 128:(g + 1) * 128].rearrange("p (h s i) -> p s h i", h=8, s=4), func=AF.Identity),
                      reads=[pqb], writes=[QSb])
            cnt = [0, 0, 0]

            def gather(s_, p, half):
                pg, pgb = PG[cnt[0] % len(PG)]
                cnt[0] += 1
                col = s_ * 64 + p
                fw.dma("pool", None, None, reads=[idxb], writes=[pgb],
                       fn=lambda e: e.indirect_dma_start(out=pg[:, :], out_offset=None, in_=self.d_kvpool[:, :],
                                                         in_offset=bass.IndirectOffsetOnAxis(ap=idx[:, half, col:col + 1], axis=0)))
                return pg, pgb

            def scores(lhsT, lb, nk, s_, g, x, pst, pstb):
                fw.op("pe", lambda e: e.matmul(pst[0:nk, 0:32], lhsT=lhsT, rhs=QS[:, s_, g, :], start=True, stop=False), reads=[lb, QSb], writes=[pstb])
                fw.op("pe", lambda e: e.matmul(pst[0:nk, 0:32], lhsT=A9[:, 0:nk], rhs=T9v[:, x, g * 32:(g + 1) * 32], start=False, stop=True),
                      reads=[A9b, T9b], writes=[pstb])

            def pv(ptile, ptileb, nk, vl, vlb, g, col0, first, last):
                pO, pOb, pD, pDb = self.psum[4 + g], self.psb[4 + g], self.psum[6 + g], self.psb[6 + g]
                for par in range(2):
                    rhs = ptile[0:nk, 0:32].rearrange("p (j r i) -> p j r i", j=4, r=2)[:, :, par, :]
                    hs = slice(par * 64, par * 64 + 64)
                    fw.op("pe", lambda e: e.matmul(pO[hs, col0:col0 + 16], lhsT=vl, rhs=rhs, start=first, stop=last), reads=[vlb, ptileb], writes=[pOb])
                    fw.op("pe", lambda e: e.matmul(pD[hs, col0:col0 + 16], lhsT=self.ones[0:nk, 0:64], rhs=rhs, start=first, stop=last), reads=[self.b_ones, ptileb], writes=[pDb])
            self.ps_rot = [0, 1, 2, 3]
            for s_ in range(4):
                pkc, pkcb = self.ps()
                pvc, pvcb = self.ps()
                for p in range(64):
                    pg, pgb = PG[cnt[0] % len(PG)]
                    cnt[0] += 1
                    fw.dma("sp", pg[:, :], self.d_scr[s_ * 64 + p], reads=[self.scrb[s_ * 64 + p]], writes=[pgb])
                    prd, prdb = PRD[p % 3]
                    fw.op("dve", lambda e: e.tensor_tensor(out=prd[:, :], in0=pg[:, :], in1=WT[:, :], op=ALU.mult), reads=[pgb, WTb], writes=[prdb])
                    fw.op("pe", lambda e: e.matmul(pkc[:, 4 * p:4 * p + 4], lhsT=prd[:, 0:128], rhs=SW[:, 124:128], start=True, stop=True), reads=[prdb, SWb], writes=[pkcb])
                    hf, pp = p // 32, p % 32
                    fw.op("pe", lambda e: e.matmul(pvc[:, hf * 128:(hf + 1) * 128], lhsT=SW[:, 124 - 4 * pp:124 - 4 * pp + 128], rhs=prd[:, 128:256],
                                                   start=(pp == 0), stop=(pp == 31)), reads=[prdb, SWb], writes=[pvcb])
                fw.op("act", lambda e: e.activation(out=kcT[:, :], in_=pkc[:, 0:256], func=AF.Identity), reads=[pkcb], writes=[kcTb])
                fw.op("act", lambda e: e.activation(out=vc[:, :, :], in_=pvc[:, 0:256].rearrange("p (a b) -> p a b", a=2), func=AF.Identity), reads=[pvcb], writes=[vcb_])
                for g in range(2):
                    col0 = ((s_ * 3 + 0) * 16)
                    pden, pdenb = self.ps()
                    for nc_ in range(2):
                        pst, pstb = self.ps()
                        scores(kcT[:, nc_ * 128:(nc_ + 1) * 128], kcTb, 128, s_, g, 68 + nc_, pst, pstb)
                        ptile, ptileb = PTs[cnt[1] % len(PTs)]
                        cnt[1] += 1
                        fw.op("act", lambda e: e.activation(out=ef[:, nc_, :], in_=pst[:, 0:32], func=AF.Exp), reads=[pstb], writes=[efb])
                        fw.op("act", lambda e: e.activation(out=ptile[:, :], in_=ef[:, nc_, :], func=AF.Identity), reads=[efb], writes=[ptileb])
                        pv(ptile, ptileb, 128, vc[:, nc_, g * 64:(g + 1) * 64], vcb_, g, col0, nc_ == 0, nc_ == 1)
                        fw.op("pe", lambda e: e.matmul(pden[:, 0:32], lhsT=self.ones[:, :], rhs=ptile[:, :], start=(nc_ == 0), stop=(nc_ == 1)),
                              reads=[self.b_ones, ptileb], writes=[pdenb])
                    fw.op("dve", lambda e: e.reciprocal(out=rd[:, :], in_=pden[:, 0:32]), reads=[pdenb], writes=[rdb])
                    fw.op("dve", lambda e: e.tensor_tensor(out=ef[:, :, :], in0=ef[:, :, :], in1=rd[:, :].rearrange("p (o c) -> p o c", o=1).to_broadcast([128, 2, 32]), op=ALU.mult),
                          reads=[efb, rdb], writes=[efb])
                    fw.op("dve", lambda e: e.reduce_sum(out=pns[:, :, :], in_=ef[:, :, :].rearrange("p a (h i) -> p a i h", h=8), axis=AX.X), reads=[efb], writes=[pnsb])
                    pim, pimb = self.ps()
                    for nc_ in range(2):
                        fw.op("pe", lambda e: e.matmul(pim[0:4, nc_ * 64:(nc_ + 1) * 64], lhsT=pns[:, nc_, :], rhs=B2[:, :], start=True, stop=True),
                              reads=[pnsb, B2b], writes=[pimb])
                    fw.op("act", lambda e: e.activation(out=imps[:, :], in_=pim[0:4, 0:128], func=AF.Identity), reads=[pimb], writes=[impsb])
                    j4 = 4 * (2 * s_ + g)
                    fw.dma("sp", IMP[j4:j4 + 4, 0:128], imps[:, :], reads=[impsb], writes=[IMPb])
            for col in (0, 127, 128):
                fw.op("dve", lambda e: e.memset(IMP[:, col:col + 1], 1e9), writes=[IMPb])
            fw.op("dve", lambda e: e.memset(IMP[:, 129:136], -1.0), writes=[IMPb])
            fw.op("dve", lambda e: e.max(out=M8[:, :], in_=IMP[:, :]), reads=[IMPb], writes=[M8b])
            fw.op("dve", lambda e: e.match_replace(out=WK[:, :], in_to_replace=M8[:, :], in_values=IMP[:, :], imm_value=-2.0), reads=[IMPb, M8b], writes=[WKb])
            fw.op("dve", lambda e: e.max(out=M8[:, :], in_=WK[:, :]), reads=[WKb], writes=[M8b])
            fw.op("dve", lambda e: e.match_replace(out=WK[:, :], in_to_replace=M8[:, :], in_values=WK[:, :], imm_value=-2.0), reads=[WKb, M8b], writes=[WKb])
            fw.op("dve", lambda e: e.tensor_scalar(out=NQ[:, :], in0=WK[:, 0:128], scalar1=-2.0, scalar2=-BIG, op0=ALU.not_equal, op1=ALU.mult), reads=[WKb], writes=[NQb])
            pt_, ptb2 = self.ps()
            fw.op("pe", lambda e: e.transpose(pt_[:, 0:32], NQ[:, :], self.ident[0:32, 0:32]), reads=[NQb, self.b_ident], writes=[ptb2])
            fw.op("act", lambda e: e.activation(out=NMs[:, :], in_=pt_[:, 0:32], func=AF.Identity), reads=[ptb2], writes=[NMsb])
            fw.op("dve", lambda e: e.tensor_copy(out=NMr[:, :, :, :], in_=NMs[:, :].rearrange("p (a o i) -> p a o i", a=8, o=1).to_broadcast([128, 8, 8, 4])),
                  reads=[NMsb], writes=[NMrb])
            items = []
            for s_ in range(4):
                for br in (1, 2):
                    if br == 1:
                        chunks = [("page", p) for p in range(64)] + [("new", 0)]
                    else:
                        chunks = [("win", c) for c in range(4)] + [("new", 1)]
                    for ci, (kind, j) in enumerate(chunks):
                        items.append(dict(s=s_, br=br, kind=kind, j=j, first=(ci == 0), last=(ci == len(chunks) - 1), seq_start=(br == 1 and ci == 0)))
            nbuf = 4
            for n_, it in enumerate(items):
                it["kt"], it["vb"] = KT[n_ % nbuf], VB[n_ % nbuf]
                it["pts"] = [PTs[(2 * n_ + g) % len(PTs)] for g in range(2)]
                it["ptn"] = [PTN[(2 * n_ + g) % len(PTN)] for g in range(2)]
                it["vn"] = VNs[n_ % 2]

            def front2(it):
                s_, br, kind, j = it["s"], it["br"], it["kind"], it["j"]
                po = 32 * (s_ % 2)
                kvt = kvs[s_ // 2]
                if it["seq_start"]:
                    pk, pkb = self.ps()
                    fw.dma("sp", NEWf[:, :], kvt[po:po + 4, 0:512], reads=[kvsb[s_]], writes=[NEWfb])
                    fw.op("pe", lambda e: e.transpose(pk[:, 0:4], NEWf[:, 0:128], self.ident[0:4, 0:4]), reads=[NEWfb, self.b_ident], writes=[pkb])
                    fw.op("pe", lambda e: e.transpose(pk[:, 4:8], NEWf[:, 256:384], self.ident[0:4, 0:4]), reads=[NEWfb, self.b_ident], writes=[pkb])
                    fw.op("act", lambda e: e.activation(out=KN[:, :], in_=pk[:, 0:8], func=AF.Identity), reads=[pkb], writes=[KNb])
                if kind == "new":
                    voff = 128 if br == 1 else 384
                    vn, vnb = it["vn"]
                    fw.op("act", lambda e: e.activation(out=vn[0:4, :], in_=NEWf[:, voff:voff + 128], func=AF.Identity), reads=[NEWfb], writes=[vnb])
                    for g in range(2):
                        pst, pstb = self.ps()
                        scores(KN[:, 4 * j:4 * j + 4], KNb, 4, s_, g, 70, pst, pstb)
                        ptn, ptnb = it["ptn"][g]
                        fw.op("act", lambda e: e.activation(out=ptn[0:4, :], in_=pst[0:4, 0:32], func=AF.Exp), reads=[pstb], writes=[ptnb])
                    return
                if kind == "page":
                    pg, pgb = PG[cnt[0] % len(PG)]
                    cnt[0] += 1
                    fw.dma("sp", pg[:, :], self.d_scr[256 + s_ * 64 + j], reads=[self.scrb[256 + s_ * 64 + j]], writes=[pgb])
                    x = j
                else:
                    pg, pgb = PG[cnt[0] % len(PG)]
                    cnt[0] += 1
                    fw.dma("sp", pg[:, :], self.d_winbuf[s_, j * 128:(j + 1) * 128, :], writes=[pgb])
                    x = 64 + j
                kt, ktb = it["kt"]
                vb, vbb = it["vb"]
                ptp, ptpb = self.ps()
                fw.op("pe", lambda e: e.transpose(ptp[:, 0:128], pg[:, 0:128], self.ident[:, :]), reads=[pgb, self.b_ident], writes=[ptpb])
                fw.op("act", lambda e: e.activation(out=kt[:, :], in_=ptp[:, 0:128], func=AF.Identity), reads=[ptpb], writes=[ktb])
                fw.op("dve", lambda e: e.tensor_copy(out=vb[:, :], in_=pg[:, 128:256]), reads=[pgb], writes=[vbb])
                for g in range(2):
                    pst, pstb = self.ps()
                    fw.op("pe", lambda e: e.matmul(pst[:, 0:32], lhsT=kt[:, :], rhs=QS[:, s_, g, :], start=True, stop=False), reads=[ktb, QSb], writes=[pstb])
                    if kind == "page":
                        fw.op("pe", lambda e: e.matmul(pst[:, 0:32], lhsT=EE[:, j * 128:(j + 1) * 128], rhs=NMr[:, 2 * s_ + g, :, :], start=False, stop=False),
                              reads=[EEb, NMrb], writes=[pstb])
                    fw.op("pe", lambda e: e.matmul(pst[:, 0:32], lhsT=A9[:, :], rhs=T9v[:, x, g * 32:(g + 1) * 32], start=False, stop=True),
                          reads=[A9b, T9b], writes=[pstb])
                    ptile, ptileb = it["pts"][g]
                    fw.op("act", lambda e: e.activation(out=ptile[:, :], in_=pst[:, 0:32], func=AF.Exp), reads=[pstb], writes=[ptileb])

            def back2(it):
                s_, br, kind = it["s"], it["br"], it["kind"]
                col0 = ((s_ * 3 + br) * 16)
                for g in range(2):
                    if kind == "new":
                        ptn, ptnb = it["ptn"][g]
                        vn, vnb = it["vn"]
                        pv(ptn, ptnb, 128, vn[:, g * 64:(g + 1) * 64], vnb, g, col0, it["first"], it["last"])
                    else:
                        ptile, ptileb = it["pts"][g]
                        vb, vbb = it["vb"]
                        pv(ptile, ptileb, 128, vb[:, g * 64:(g + 1) * 64], vbb, g, col0, it["first"], it["last"])

            LOOK2 = 3
            N2 = len(items)
            for i in range(N2 + LOOK2):
                if i < N2:
                    front2(items[i])
                if i >= LOOK2:
                    back2(items[i - LOOK2])
            for s_ in range(4):
                for g in range(2):
                    pO, pOb, pD, pDb = self.psum[4 + g], self.psb[4 + g], self.psum[6 + g], self.psb[6 + g]
                    for br in range(3):
                        col0 = ((s_ * 3 + br) * 16)
                        pgt, pgtb = self.ps()
                        for j in range(4):
                            so = (br * 8 + 4 * g + j) * 128
                            fw.op("pe", lambda e: e.matmul(pgt[:, j * 4:(j + 1) * 4], lhsT=selg[0:48, so:so + 128], rhs=GTh[0:48, T + 4 * s_:T + 4 * s_ + 4], start=True, stop=True),
                                  reads=[selgb, gtb[4]], writes=[pgtb])
                        r1, r1b = cr[(br) % 2]
                        fw.op("dve", lambda e: e.tensor_scalar(out=r1[:, :], in0=pD[:, col0:col0 + 16], scalar1=1e-30, scalar2=None, op0=ALU.max), reads=[pDb], writes=[r1b])
                        fw.op("dve", lambda e: e.reciprocal(out=r1[:, :], in_=r1[:, :]), reads=[r1b], writes=[r1b])
                        fw.op("dve", lambda e: e.tensor_tensor(out=r1[:, :], in0=pgt[:, 0:16], in1=r1[:, :], op=ALU.mult), reads=[pgtb, r1b], writes=[r1b])
                        fw.op("dve", lambda e: e.tensor_tensor(out=r1[:, :], in0=pO[:, col0:col0 + 16], in1=r1[:, :], op=ALU.mult), reads=[pOb, r1b], writes=[r1b])
                        if br == 0:
                            fw.op("dve", lambda e: e.tensor_copy(out=cacc[:, :], in_=r1[:, :]), reads=[r1b], writes=[caccb])
                        else:
                            fw.op("dve", lambda e: e.tensor_tensor(out=cacc[:, :], in0=cacc[:, :], in1=r1[:, :], op=ALU.add), reads=[r1b, caccb], writes=[caccb])
                    fw.op("act", lambda e: e.activation(out=self.G[:, 4 * g:4 * g + 4, T + 4 * s_:T + 4 * s_ + 4], in_=cacc[:, :].rearrange("p (j i) -> p j i", j=4), func=AF.Identity),
                          reads=[caccb], writes=[self.gb[(4 * g + j, 4)] for j in range(4)])
            self.ps_rot = list(range(8))
            fw.barrier()


def build_program(widx, ntiles, nvec, stages):
    nc = bass.Bass("TRN2", target_bir_lowering=False)
    with contextlib.ExitStack() as es:
        K = Kern(nc, es, widx, ntiles, nvec, stages)
        K.setup()
        with contextlib.ExitStack() as es2:
            K.load_x(es2)
            K.fw.barrier()
        if "A0" in stages:
            K.layer_A(0, 0, final=("final_after_A0" in stages))
        if "B" in stages:
            K.layer_B(1)
        if "C" in stages:
            K.layer_C(2)
        if "A1" in stages:
            K.layer_A(3, 1, final=True)
        K.fw.finish()
        print("instructions:", K.fw.ninst, {k: v for k, v in K.fw.cnt.items()}, K.fw.dcnt)
    return nc


def _b_consts():
    NEG = -32000.0
    kl = np.arange(128)[:, None]
    j = np.arange(2432)[None, :]
    d = (kl - (j - 384)).astype(np.float32)
    tdw = np.where(d <= 0, d, NEG).astype(np.float32)
    j = np.arange(1408)[None, :]
    d = (kl - (j - 384)).astype(np.float32)
    tww = np.where((d <= 0) & (d > -512), d, NEG).astype(np.float32)
    n = np.arange(64)[:, None]
    t = np.arange(2048)[None, :]
    d = (32 * n + 31 - t).astype(np.float32)
    tdc = np.where(d <= 0, d, NEG).astype(np.float32)
    blk = np.arange(32)[:, None]
    pos = np.arange(2048)[None, :]
    e = (blk == pos // 64).astype(np.float32)
    z = np.zeros_like(e)
    te2 = np.concatenate([e, z, e, z], axis=0)
    tselg = np.zeros((48, 3, 8, 128), np.float32)
    for b in range(3):
        for c in range(8):
            for m in range(128):
                tselg[b * 16 + 2 * c + m // 64, b, c, m] = 1.0
    tselg = tselg.reshape(48, 3072)
    tq = np.arange(2048)
    cur = (tq // 64)[:, None]
    b32 = np.arange(32)[None, :]
    forced = (b32 == 0) | ((b32 <= cur) & (b32 > cur - 2))
    M1 = ((b32 <= cur) & ~forced).astype(np.float32)
    M2 = np.where(forced, 1e9, np.where(b32 <= cur, 0.0, -1.0)).astype(np.float32)
    m12 = np.zeros((128, 16, 2, 2, 32), np.float32)
    for qt in range(16):
        for g in range(2):
            m12[:, qt, 0, g, :] = M1[qt * 128:(qt + 1) * 128]
            m12[:, qt, 1, g, :] = M2[qt * 128:(qt + 1) * 128]
    tm12 = m12.reshape(128, 2048)
    slopes = 2.0 ** (-8.0 * np.arange(1, 17) / 16.0)
    cend = 32 * np.arange(64) + 31
    dd = (cend[None, :] - tq[:, None]).astype(np.float64)
    hord = np.array([0, 2, 4, 6, 8, 10, 12, 14, 1, 3, 5, 7, 9, 11, 13, 15])
    bias = np.where(dd[:, None, :] <= 0, slopes[None, :, None] * dd[:, None, :], -30000.0)
    biasc = bias.reshape(16, 128, 1024).astype(np.float32)
    return dict(tdw=tdw, tww=tww, tdc=tdc, te2=te2, tselg=tselg, tm12=tm12, biasc=np.ascontiguousarray(biasc))


def _bf16_pieces(x):
    import ml_dtypes
    x = np.asarray(x, np.float64)
    p1 = x.astype(ml_dtypes.bfloat16).astype(np.float64)
    p2 = (x - p1).astype(ml_dtypes.bfloat16).astype(np.float64)
    p3 = (x - p1 - p2).astype(ml_dtypes.bfloat16).astype(np.float64)
    return np.stack([p1, p2, p3]).astype(np.float32)


def _s_consts():
    slopes = 2.0 ** (-8.0 * np.arange(1, 17) / 16.0)
    blk = np.arange(128)[:, None]
    pos = np.arange(8192)[None, :]
    tee = (blk == pos // 64).astype(np.float32)
    g_, h8_, i_ = np.meshgrid(np.arange(2), np.arange(8), np.arange(4), indexing="ij")
    sl = slopes[(8 * g_ + h8_)].reshape(64)
    ii = i_.reshape(64).astype(np.float64)
    tb9 = np.zeros((128, 71, 64), np.float32)
    sp = _bf16_pieces(sl)
    si = _bf16_pieces(-sl * ii)
    for p in range(64):
        tb9[0:3, p] = _bf16_pieces(sl * (128.0 * p - 8192.0))
        tb9[3:6, p] = sp
        tb9[6:9, p] = si
    for ch in range(4):
        x = 64 + ch
        tb9[0:3, x] = _bf16_pieces(sl * (128.0 * ch - 512.0))
        tb9[3:6, x] = sp
        tb9[6:9, x] = si
        if ch == 0:
            for r in range(4):
                tb9[9 + r, x] = np.where(r <= ii, -BIG, 0.0)
    for nc_ in range(2):
        x = 68 + nc_
        tb9[0:3, x] = _bf16_pieces(sl * (4096.0 * nc_ + 31.0 - 8192.0 - ii))
        tb9[3:6, x] = _bf16_pieces(32.0 * sl)
    x = 70
    tb9[3:6, x] = sp
    tb9[6:9, x] = si
    for r in range(4):
        tb9[9 + r, x] = np.where(r > ii, -BIG, 0.0)
    ta9 = np.zeros((128, 128), np.float32)
    ta9[0:3] = 1.0
    ta9[3:6] = np.arange(128)[None, :]
    ta9[6:9] = 1.0
    for r in range(4):
        ta9[9 + r, r] = 1.0
    tshift = np.zeros((128, 4, 128), np.float32)
    for hh in range(2):
        for g in range(2):
            for d in range(64):
                tshift[hh * 64 + d, hh * 2 + g, g * 64 + d] = 1.0
    row = np.arange(128)[:, None]
    cc = np.arange(256)[None, :]
    tselw = ((cc - 124) == row // 32).astype(np.float32)
    tb2 = (np.arange(128)[:, None] // 2 == np.arange(64)[None, :]).astype(np.float32)
    return dict(tee=tee, tb9=np.ascontiguousarray(tb9.reshape(128, 71 * 64)), tshift=np.ascontiguousarray(tshift.reshape(128, 512)),
                ta9=ta9, tselw=tselw, tb2=tb2)


def prepare_inputs(inp, cores):
    wt, widx = _weights_layout(inp)
    vecs = _vec_layout(inp)
    ident = np.eye(128, dtype=np.float32)
    bc = _b_consts()
    bc.update(_s_consts())
    wc = inp["b_w_cmp"][0]
    r32 = np.arange(128) % 32
    bc["twt"] = np.ascontiguousarray(np.concatenate([wc[0][r32].reshape(128, 128), wc[1][r32].reshape(128, 128)], axis=1))
    kvpool = np.ascontiguousarray(inp["cache_nsa_kv"][0].reshape(2560 * 128 * 2, 256))
    maps = []
    for c in cores:
        s0 = 4 * c
        maps.append({
            "xp": np.ascontiguousarray(inp["x_prompt"][c]),
            "xs": np.ascontiguousarray(inp["x_sample"][s0:s0 + 4].reshape(NS, D)),
            "wt": wt,
            "vecs": vecs,
            "ident": ident,
            "sconv": np.ascontiguousarray(inp["state_conv_a"][:, s0:s0 + 4]),
            "lruh": np.ascontiguousarray(inp["state_lru_h"][0, s0:s0 + 4]),
            "lruconv": np.ascontiguousarray(inp["state_lru_conv"][0, s0:s0 + 4]),
            "winbuf": np.ascontiguousarray(inp["cache_nsa_win"][0, s0:s0 + 4].reshape(4, 512, 256)),
            "kvpool": kvpool,
            "ptab": np.ascontiguousarray(inp["page_table"][s0:s0 + 4].astype(np.int32)),
            **bc,
        })
    return maps, widx, wt.shape[0], vecs.shape[1]


ALL_STAGES = ("A0", "B", "Bimp", "Battn", "Bsamp", "C", "A1")


def run_cores(inp, cores, stages=ALL_STAGES, trace=False):
    maps, widx, ntiles, nvec = prepare_inputs(inp, cores)
    nc = build_program(widx, ntiles, nvec, stages)
    res = run_bass_kernel_spmd(nc, maps, core_ids=list(range(len(cores))), trace=trace)
    return res


def kernel(**inputs):
    inp = {k: np.asarray(v) for k, v in inputs.items()}
    res = run_cores(inp, list(range(NCORES)))
    r = res.results
    n = NCORES
    f = np.float32
    yp = np.stack([r[c]["yp"] for c in range(n)]).astype(f)
    ys = np.concatenate([r[c]["ys"].reshape(4, 4, D) for c in range(n)]).astype(f)
    cap = np.stack([r[c]["cap"] for c in range(n)], axis=1).astype(f)
    cas = np.concatenate([r[c]["cas"] for c in range(n)], axis=1).astype(f)
    rowsp = np.stack([r[c]["rowsp"].reshape(T, 4, 2, 64) for c in range(n)])[None].astype(f)
    rowss = np.concatenate([r[c]["rowss"].reshape(4, 4, 4, 2, 64) for c in range(n)])[None].astype(f)
    winp = np.stack([r[c]["winp"].reshape(512, 2, 2, 64) for c in range(n)])[None].astype(f)
    wins = np.concatenate([r[c]["wins"].reshape(4, 512, 2, 2, 64) for c in range(n)])[None].astype(f)
    lhp = np.stack([r[c]["lhp"] for c in range(n)])[None].astype(f)
    lhs = np.concatenate([r[c]["lhs"] for c in range(n)])[None].astype(f)
    lcp = np.stack([r[c]["lcp"] for c in range(n)])[None].astype(f)
    lcs = np.concatenate([r[c]["lcs"] for c in range(n)])[None].astype(f)
    return (yp, ys, cap, cas, rowsp, rowss, winp, wins, lhp, lhs, lcp, lcs)
```
